# Optimizing a Trainium2 kernel written in Bass

```python
import math
import jax, jax.numpy as jnp
from jax import lax
import numpy as np

D_MODEL = 2048
BATCH = 4
SEQ = 2048
DEPTH = 4

N_MIXERS = 3
N_A = (DEPTH + 2) // 3
N_B = (DEPTH + 1) // 3
N_C = DEPTH // 3
N_A_VRES = max(N_A - 1, 0)
N_META = 16
RMS_EPS = 1e-6

RWKV_HEAD = 64
RWKV_HEADS = D_MODEL // RWKV_HEAD
DECAY_LORA = max(32, int(round(1.8 * D_MODEL ** 0.5 / 32)) * 32)
AAA_LORA = max(32, int(round(1.8 * D_MODEL ** 0.5 / 32)) * 32)
MV_LORA = max(32, int(round(1.3 * D_MODEL ** 0.5 / 32)) * 32)
GATE_LORA = max(32, int(round(0.6 * D_MODEL ** 0.8 / 32)) * 32)
LNX_EPS = 1e-5 * RWKV_HEAD

MLA_HEADS = D_MODEL // 128
Q_LORA = D_MODEL // 4
KV_LORA = D_MODEL // 4
NOPE_D = 128
ROPE_D = 64
V_D = 128
ROPE_THETA = 10000.0
Q_BLOCK = 128

POOL_WINDOWS = (2, 4, 8, 16)
POOL_GROUP = D_MODEL // len(POOL_WINDOWS)

FFN_HIDDEN = -(-8 * D_MODEL // (3 * 256)) * 256

kernel_name = 'hybrid_rwkv7_mla_pool_trunk'


def _rmsnorm(x, g):
    x32 = x.astype(jnp.float32)
    y = x32 * lax.rsqrt(jnp.mean(x32 * x32, axis=-1, keepdims=True) + RMS_EPS)
    return (y * g.astype(jnp.float32)).astype(x.dtype)


def _swiglu(x, w_gate, w_up, w_down):
    return (jax.nn.silu(x @ w_gate) * (x @ w_up)) @ w_down


def _token_shift(x):
    return jnp.pad(x, ((0, 0), (1, 0), (0, 0)))[:, :-1]


def _wkv7_scan(r, w, k, v, a, b):
    B, L, H, N = r.shape
    tm = lambda t: jnp.moveaxis(t, 1, 0)

    def step(S, inp):
        r_t, w_t, k_t, v_t, a_t, b_t = inp
        sa = jnp.einsum('bhij,bhj->bhi', S, a_t)
        S = (S * w_t[:, :, None, :] + sa[..., None] * b_t[:, :, None, :]
             + v_t[..., None] * k_t[:, :, None, :])
        return S, jnp.einsum('bhij,bhj->bhi', S, r_t)

    S0 = jnp.zeros((B, H, N, N), jnp.float32)
    _, ys = lax.scan(step, S0, (tm(r), tm(w), tm(k), tm(v), tm(a), tm(b)))
    return jnp.moveaxis(ys, 0, 1)


def _rwkv7_time_mix(x, v_first, mix, w0, w1, w2, a0, a1, a2, vres, g1, g2,
                    k_k, k_a, r_k, w_rkv, w_o, lnx_g, lnx_b):
    B, L, D = x.shape
    H, N = RWKV_HEADS, RWKV_HEAD
    xx = _token_shift(x) - x
    xs = x[None] + xx[None] * mix[:, None, None, :]
    xr, xw, xk, xv, xa, xg = xs[0], xs[1], xs[2], xs[3], xs[4], xs[5]
    rkv = jnp.einsum('sbld,sde->sble', jnp.stack([xr, xk, xv]), w_rkv)
    r, k, v = rkv[0], rkv[1], rkv[2]
    w = -jax.nn.softplus(-(w0 + jnp.tanh(xw @ w1) @ w2)) - 0.5
    a = jax.nn.sigmoid(a0 + (xa @ a1) @ a2)
    g = jax.nn.sigmoid(xg @ g1) @ g2
    if vres is None:
        v_first = v
    else:
        v0, v1, v2 = vres
        v = v + (v_first - v) * jax.nn.sigmoid(v0 + (xv @ v1) @ v2)
    hd = lambda t: t.reshape(B, L, H, N).astype(jnp.float32)
    r_h, w_h, k_h, v_h, a_h = hd(r), hd(w), hd(k), hd(v), hd(a)
    kk = k_h * k_k.astype(jnp.float32).reshape(H, N)
    kk = kk / jnp.maximum(jnp.sqrt(jnp.sum(kk * kk, axis=-1, keepdims=True)), 1e-12)
    k_h = k_h * (1.0 + (a_h - 1.0) * k_a.astype(jnp.float32).reshape(H, N))
    decay = jnp.exp(-jnp.exp(w_h))
    y = _wkv7_scan(r_h, decay, k_h, v_h, -kk, kk * a_h)
    mu = jnp.mean(y, axis=-1, keepdims=True)
    var = jnp.mean((y - mu) ** 2, axis=-1, keepdims=True)
    y = ((y - mu) * lax.rsqrt(var + LNX_EPS)).reshape(B, L, D)
    y = y * lnx_g.astype(jnp.float32) + lnx_b.astype(jnp.float32)
    bonus = jnp.sum(r_h * k_h * r_k.astype(jnp.float32), axis=-1, keepdims=True) * v_h
    y = y + bonus.reshape(B, L, D)
    out = (y.astype(x.dtype) * g) @ w_o
    return out, v_first


def _rope_tables(pos, d):
    inv_freq = 1.0 / (ROPE_THETA ** (jnp.arange(0, d, 2, dtype=jnp.float32) / d))
    ang = pos.astype(jnp.float32)[:, None] * inv_freq[None, :]
    return jnp.cos(ang), jnp.sin(ang)


def _rope(x, cos, sin):
    half = x.shape[-1] // 2
    x1, x2 = x[..., :half], x[..., half:]
    cos = cos.astype(x.dtype)
    sin = sin.astype(x.dtype)
    return jnp.concatenate([x1 * cos - x2 * sin, x2 * cos + x1 * sin], axis=-1)


def _mla(x, pos, w_in, q_norm, w_uq, kv_norm, w_ukv, w_o):
    B, L, D = x.shape
    H = MLA_HEADS
    lat = x @ w_in
    c_q = _rmsnorm(lat[..., :Q_LORA], q_norm)
    c_kv = _rmsnorm(lat[..., Q_LORA:Q_LORA + KV_LORA], kv_norm)
    k_pe = lat[..., Q_LORA + KV_LORA:]
    q = (c_q @ w_uq).reshape(B, L, H, NOPE_D + ROPE_D)
    q_nope, q_pe = q[..., :NOPE_D], q[..., NOPE_D:]
    kv = (c_kv @ w_ukv).reshape(B, L, H, NOPE_D + V_D)
    k_nope, v = kv[..., :NOPE_D], kv[..., NOPE_D:]
    cos, sin = _rope_tables(pos, ROPE_D)
    q_pe = _rope(q_pe, cos[:, None, :], sin[:, None, :])
    k_pe = _rope(k_pe, cos, sin)
    n_blk = -(-L // Q_BLOCK)
    Lp = n_blk * Q_BLOCK
    pad = lambda t: jnp.pad(t, [(0, 0), (0, Lp - L)] + [(0, 0)] * (t.ndim - 2))
    q_nope, q_pe, k_nope, k_pe, v = pad(q_nope), pad(q_pe), pad(k_nope), pad(k_pe), pad(v)
    kpos = jnp.arange(Lp)
    scale = (NOPE_D + ROPE_D) ** -0.5
    qb = lambda t: jnp.moveaxis(t.reshape(B, n_blk, Q_BLOCK, *t.shape[2:]), 1, 0)

    def block(args):
        qn, qp, i = args
        s = (jnp.einsum('bqhd,bkhd->bhqk', qn, k_nope)
             + jnp.einsum('bqhd,bkd->bhqk', qp, k_pe)).astype(jnp.float32) * scale
        qpos = i * Q_BLOCK + jnp.arange(Q_BLOCK)
        s = jnp.where(kpos[None, :] <= qpos[:, None], s, -jnp.inf)
        p = jax.nn.softmax(s, axis=-1).astype(v.dtype)
        return jnp.einsum('bhqk,bkhd->bqhd', p, v)

    o = lax.map(block, (qb(q_nope), qb(q_pe), jnp.arange(n_blk)))
    o = jnp.moveaxis(o, 0, 1).reshape(B, Lp, H * V_D)[:, :L]
    return o @ w_o


def _pool_mix(x, w_grp, scale):
    B, L, D = x.shape
    xf = x.astype(jnp.float32)
    c0 = jnp.pad(jnp.cumsum(xf, axis=1), ((0, 0), (1, 0), (0, 0)))
    t = np.arange(L)
    outs = []
    for gi, win in enumerate(POOL_WINDOWS):
        lo = np.maximum(t + 1 - win, 0)
        cnt = (t + 1 - lo).astype(np.float32)
        cg = c0[..., gi * POOL_GROUP:(gi + 1) * POOL_GROUP]
        outs.append((cg[:, 1:] - cg[:, lo]) / cnt[None, :, None])
    pooled = jnp.stack(outs, axis=2) - xf.reshape(B, L, len(POOL_WINDOWS), POOL_GROUP)
    y = jnp.einsum('blgc,gce->blge', pooled.astype(x.dtype), w_grp).reshape(B, L, D)
    return y * scale


def setup_inputs(seed: int = 0) -> dict:
    key = jax.random.key(seed)
    ks = iter(jax.random.split(key, 64))
    nrm = lambda shape, s: s * jax.random.normal(next(ks), shape, jnp.float32)
    gain = lambda shape: 1.0 + nrm(shape, 0.05)
    D, F, H = D_MODEL, FFN_HIDDEN, MLA_HEADS
    inp = {}
    inp['x'] = nrm((BATCH, SEQ, D), 1.0)
    inp['meta_tokens'] = nrm((N_META, D), 1.0)
    inp['norm_mix_pre'] = gain((DEPTH, D))
    inp['norm_mix_post'] = gain((DEPTH, D))
    inp['norm_ffn_pre'] = gain((DEPTH, D))
    inp['norm_ffn_post'] = gain((DEPTH, D))
    inp['ffn_w_gate'] = nrm((DEPTH, D, F), D ** -0.5)
    inp['ffn_w_up'] = nrm((DEPTH, D, F), D ** -0.5)
    inp['ffn_w_down'] = nrm((DEPTH, F, D), F ** -0.5)
    inp['rwkv_mix'] = jax.random.uniform(next(ks), (N_A, 6, D), jnp.float32)
    inp['rwkv_w0'] = jax.random.uniform(next(ks), (N_A, D), jnp.float32, -6.0, -1.0)
    inp['rwkv_w1'] = nrm((N_A, D, DECAY_LORA), D ** -0.5)
    inp['rwkv_w2'] = nrm((N_A, DECAY_LORA, D), 0.3 * DECAY_LORA ** -0.5)
    inp['rwkv_a0'] = nrm((N_A, D), 0.1)
    inp['rwkv_a1'] = nrm((N_A, D, AAA_LORA), D ** -0.5)
    inp['rwkv_a2'] = nrm((N_A, AAA_LORA, D), AAA_LORA ** -0.5)
    inp['rwkv_v0'] = nrm((N_A_VRES, D), 0.1)
    inp['rwkv_v1'] = nrm((N_A_VRES, D, MV_LORA), D ** -0.5)
    inp['rwkv_v2'] = nrm((N_A_VRES, MV_LORA, D), MV_LORA ** -0.5)
    inp['rwkv_g1'] = nrm((N_A, D, GATE_LORA), D ** -0.5)
    inp['rwkv_g2'] = nrm((N_A, GATE_LORA, D), GATE_LORA ** -0.5)
    inp['rwkv_k_k'] = 0.85 + nrm((N_A, D), 0.05)
    inp['rwkv_k_a'] = 1.0 + nrm((N_A, D), 0.05)
    inp['rwkv_r_k'] = nrm((N_A, RWKV_HEADS, RWKV_HEAD), 0.1)
    inp['rwkv_w_rkv'] = nrm((N_A, 3, D, D), D ** -0.5)
    inp['rwkv_w_o'] = nrm((N_A, D, D), D ** -0.5)
    inp['rwkv_lnx_g'] = gain((N_A, D))
    inp['rwkv_lnx_b'] = nrm((N_A, D), 0.02)
    inp['mla_w_in'] = nrm((N_B, D, Q_LORA + KV_LORA + ROPE_D), D ** -0.5)
    inp['mla_q_norm'] = gain((N_B, Q_LORA))
    inp['mla_w_uq'] = nrm((N_B, Q_LORA, H * (NOPE_D + ROPE_D)), Q_LORA ** -0.5)
    inp['mla_kv_norm'] = gain((N_B, KV_LORA))
    inp['mla_w_ukv'] = nrm((N_B, KV_LORA, H * (NOPE_D + V_D)), KV_LORA ** -0.5)
    inp['mla_w_o'] = nrm((N_B, H * V_D, D), (H * V_D) ** -0.5)
    inp['pool_w'] = nrm((N_C, len(POOL_WINDOWS), POOL_GROUP, POOL_GROUP), POOL_GROUP ** -0.5)
    inp['pool_scale'] = 1.0 + nrm((N_C, D), 0.1)
    return inp


def reference(x, meta_tokens, norm_mix_pre, norm_mix_post, norm_ffn_pre, norm_ffn_post,
              ffn_w_gate, ffn_w_up, ffn_w_down,
              rwkv_mix, rwkv_w0, rwkv_w1, rwkv_w2, rwkv_a0, rwkv_a1, rwkv_a2,
              rwkv_v0, rwkv_v1, rwkv_v2, rwkv_g1, rwkv_g2, rwkv_k_k, rwkv_k_a, rwkv_r_k,
              rwkv_w_rkv, rwkv_w_o, rwkv_lnx_g, rwkv_lnx_b,
              mla_w_in, mla_q_norm, mla_w_uq, mla_kv_norm, mla_w_ukv, mla_w_o,
              pool_w, pool_scale):
    B = x.shape[0]
    meta = jnp.broadcast_to(meta_tokens.astype(x.dtype)[None], (B, N_META, D_MODEL))
    h = jnp.concatenate([meta, x], axis=1)
    L = h.shape[1]
    pos = jnp.arange(L, dtype=jnp.int32)
    v_first = None
    for i in range(DEPTH):
        hn = _rmsnorm(h, norm_mix_pre[i])
        kind, j = i % N_MIXERS, i // N_MIXERS
        if kind == 0:
            vres = None if j == 0 else (rwkv_v0[j - 1], rwkv_v1[j - 1], rwkv_v2[j - 1])
            mix_out, v_first = _rwkv7_time_mix(
                hn, v_first, rwkv_mix[j], rwkv_w0[j], rwkv_w1[j], rwkv_w2[j],
                rwkv_a0[j], rwkv_a1[j], rwkv_a2[j], vres, rwkv_g1[j], rwkv_g2[j],
                rwkv_k_k[j], rwkv_k_a[j], rwkv_r_k[j], rwkv_w_rkv[j], rwkv_w_o[j],
                rwkv_lnx_g[j], rwkv_lnx_b[j])
        elif kind == 1:
            mix_out = _mla(hn, pos, mla_w_in[j], mla_q_norm[j], mla_w_uq[j],
                           mla_kv_norm[j], mla_w_ukv[j], mla_w_o[j])
        else:
            mix_out = _pool_mix(hn, pool_w[j], pool_scale[j])
        h = h + _rmsnorm(mix_out, norm_mix_post[i])
        f = _swiglu(_rmsnorm(h, norm_ffn_pre[i]), ffn_w_gate[i], ffn_w_up[i], ffn_w_down[i])
        h = h + _rmsnorm(f, norm_ffn_post[i])
    return h[:, N_META:]
```

```python
import numpy as np
from contextlib import ExitStack
import concourse.bass as bass
import concourse.mybir as mybir
from concourse.bass_utils import run_bass_kernel_spmd

F32 = mybir.dt.float32
BF16 = mybir.dt.bfloat16
AF = mybir.ActivationFunctionType
ALU = mybir.AluOpType
AX = mybir.AxisListType

NDMA_SEMS = 6


class Buf:
    __slots__ = ("name", "w", "rs")

    def __init__(self, name):
        self.name = name
        self.w = None
        self.rs = []


class Tok:
    __slots__ = ("sem", "val", "clock")

    def __init__(self, sem, val, clock):
        self.sem = sem
        self.val = val
        self.clock = clock


class Eng:
    def __init__(self, name, sem):
        self.name = name
        self.sem = sem
        self.n = 0
        self.clock = {}
        self.ops = []
        self.dsems = []
        self.dcnt = []
        self.dlast = []
        self.di = 0


class FW:
    def __init__(self, nc, stack, same_engine_sync=True):
        self.nc = nc
        self.stack = stack
        self.same = same_engine_sync
        self.E = {}
        for nm in ("pe", "act", "dve", "pool", "sp"):
            sem = stack.enter_context(nc.semaphore("s_" + nm))
            e = Eng(nm, sem)
            self.E[nm] = e
        for nm in ("sp", "act", "pool"):
            e = self.E[nm]
            for i in range(NDMA_SEMS):
                e.dsems.append(stack.enter_context(nc.semaphore(f"d_{nm}{i}")))
                e.dcnt.append(0)
                e.dlast.append(None)
        self.semid = {}
        self.nbuf = 0
        self.uid = 0
        self.ccsem = stack.enter_context(nc.semaphore("cc_sem"))
        self.ccn = 0

    def buf(self, name=None):
        self.nbuf += 1
        return Buf(name or f"b{self.nbuf}")

    def sb(self, name, shape, dt):
        self.uid += 1
        return self.stack.enter_context(self.nc.sbuf_tensor(f"sb{self.uid}_{name}", list(shape), dt))

    def ps(self, name, shape, dt=F32):
        self.uid += 1
        return self.stack.enter_context(self.nc.psum_tensor(f"pp{self.uid}_{name}", list(shape), dt))

    def _need(self, e, tok, waits):
        if tok is None:
            return
        k = id(tok.sem)
        if e.clock.get(k, 0) >= tok.val:
            return
        cur = waits.get(k)
        if cur is None or cur[1] < tok.val:
            waits[k] = (tok.sem, tok.val)
        for kk, vv in tok.clock.items():
            if e.clock.get(kk, 0) < vv:
                e.clock[kk] = vv
        e.clock[k] = max(e.clock.get(k, 0), tok.val)

    def op(self, eng, fn, reads=(), writes=(), dma=False, pe_accum=False, cc=False):
        e = self.E[eng]
        waits = {}
        own = id(e.sem)
        for b in reads:
            self._need(e, b.w, waits)
        for b in writes:
            self._need(e, b.w, waits)
            for t in b.rs:
                self._need(e, t, waits)
        if cc:
            self.ccn += 1
            inc = (self.ccsem, 1)
            tok = Tok(self.ccsem, self.ccn, dict(e.clock))
        elif dma:
            i = e.di
            e.di = (e.di + 1) % len(e.dsems)
            self._need(e, e.dlast[i], waits)
            e.dcnt[i] += 16
            sem = e.dsems[i]
            val = e.dcnt[i]
            inc = (sem, 16)
            tok = Tok(sem, val, dict(e.clock))
            e.dlast[i] = tok
        else:
            e.n += 1
            inc = (e.sem, 1)
            tok = Tok(e.sem, e.n, dict(e.clock))
        wl = []
        for k, (sem, val) in waits.items():
            if k == own and not dma and not cc:
                if not self.same or (eng == "pe"):
                    continue
            wl.append((sem, val))
        e.ops.append((wl, fn, inc))
        for b in reads:
            b.rs.append(tok)
        for b in writes:
            b.w = tok
            b.rs = []
        return tok

    def barrier(self):
        wl = []
        for e in self.E.values():
            if e.n > 0:
                wl.append((e.sem, e.n))
            for i, t in enumerate(e.dlast):
                if t is not None:
                    wl.append((t.sem, t.val))
        for e in self.E.values():
            e.ops.append((list(wl), None, None))
            for sem, val in wl:
                k = id(sem)
                if e.clock.get(k, 0) < val:
                    e.clock[k] = val

    def final_wait(self, eng, toks):
        e = self.E[eng]
        wl = [(t.sem, t.val) for t in toks]
        e.ops.append((wl, None, None))

    def emit(self):
        nc = self.nc
        E = self.E
        with nc.Block() as block:
            def run(e, eng):
                for wl, fn, inc in e.ops:
                    for sem, val in wl:
                        eng.wait_ge(sem, val)
                    if fn is not None:
                        ins = fn(eng)
                        ins.then_inc(inc[0], inc[1])

            @block.tensor
            def _(eng):
                run(E["pe"], eng)

            @block.scalar
            def _(eng):
                run(E["act"], eng)

            @block.vector
            def _(eng):
                run(E["dve"], eng)

            @block.gpsimd
            def _(eng):
                run(E["pool"], eng)

            @block.sync
            def _(eng):
                run(E["sp"], eng)


D = 2048
DC = 16
FH = 5632
FC = 44
RMS_EPS = 1e-6


class Tile:
    def __init__(self, fw, name, shape, dt, nb=1):
        self.t = fw.sb(name, shape, dt)
        self.b = [fw.buf(f"{name}{i}") for i in range(nb)]


class PsPool:
    def __init__(self, fw, names):
        self.tiles = []
        for nm in names:
            t = fw.ps(nm, [128, 512], F32)
            self.tiles.append((t, fw.buf(nm)))
        self.i = 0

    def get(self):
        r = self.tiles[self.i]
        self.i = (self.i + 1) % len(self.tiles)
        return r


class WSlots:
    def __init__(self, fw, name, kc, n, width=128):
        self.fw = fw
        self.kc = kc
        self.slots = [Tile(fw, f"{name}{i}", [128, kc, width], BF16) for i in range(n)]
        self.i = 0

    def load(self, src_ap, kc=None, eng="pool"):
        s = self.slots[self.i]
        self.i = (self.i + 1) % len(self.slots)
        kc = kc or self.kc
        self.fw.op(eng, lambda e: e.dma_start(out=s.t[:, 0:kc, :], in_=src_ap), writes=[s.b[0]], dma=True)
        return s


def slices_of(nt, mx=512):
    n = -(-nt // mx)
    base = -(-nt // n)
    if base % 2:
        base += 1
    out = []
    o = 0
    while o < nt:
        w = min(base, nt - o)
        out.append((o, w))
        o += w
    return out


class StageB:
    def __init__(self, fw, NT, bl_dt=BF16):
        self.fw = fw
        self.NT = NT
        self.sl = slices_of(NT)
        self.h = Tile(fw, "h", [128, DC, NT], F32, DC)
        self.of = Tile(fw, "of", [128, DC, NT], F32, DC)
        self.mn = Tile(fw, "mn", [128, DC, NT], BF16, DC)
        self.act = Tile(fw, "act", [128, FC, NT], BF16, FC)
        self.w16 = WSlots(fw, "w16_", 16, 6)
        self.w44 = WSlots(fw, "w44_", 44, 3)
        self.psum = PsPool(fw, [f"psB{i}" for i in range(6)])
        self.ss = [(fw.ps(f"ssB{i}", [128, 512], F32), fw.buf(f"ssB{i}")) for i in range(2)]
        self.ones = Tile(fw, "onesB", [128, 128], BF16)
        fw.op("pool", lambda e: e.memset(self.ones.t[:], 1.0), writes=[self.ones.b[0]])
        self.sq = [Tile(fw, f"sqB{i}", [128, 512], BF16) for i in range(3)]
        self.sqi = 0
        self.tmp = [Tile(fw, f"tmpB{i}", [128, 512], F32) for i in range(3)]
        self.tmpi = 0
        self.rstd = Tile(fw, "rstdB", [128, NT], F32, len(self.sl))
        self.gains = Tile(fw, "gainsB", [128, 4, DC], F32)
        self.blA = [Tile(fw, f"blA{i}", [128, NT], bl_dt) for i in range(2)]
        self.blB = [Tile(fw, f"blB{i}", [128, NT], bl_dt) for i in range(2)]

    def _sq(self):
        s = self.sq[self.sqi]
        self.sqi = (self.sqi + 1) % len(self.sq)
        return s

    def _tmp(self):
        s = self.tmp[self.tmpi]
        self.tmpi = (self.tmpi + 1) % len(self.tmp)
        return s

    def ss_accum(self, src_ap, src_bufs, si, c, nchunks, eng="act"):
        fw = self.fw
        o, w = self.sl[si]
        s = self._sq()
        fw.op(eng, lambda e: e.activation(out=s.t[:, 0:w], in_=src_ap, func=AF.Square),
              reads=src_bufs, writes=[s.b[0]])
        sst, ssb = self.ss[si]
        fw.op("pe", lambda e: e.matmul(sst[:, 0:w], lhsT=self.ones.t[:], rhs=s.t[:, 0:w],
                                       start=(c == 0), stop=(c == nchunks - 1)),
              reads=[s.b[0], self.ones.b[0]], writes=[ssb])

    def make_rstd(self, si, dim):
        fw = self.fw
        o, w = self.sl[si]
        sst, ssb = self.ss[si]
        r = self.rstd
        fw.op("dve", lambda e: e.tensor_scalar(out=r.t[:, o:o + w], in0=sst[:, 0:w], scalar1=1.0 / dim,
                                               scalar2=RMS_EPS, op0=ALU.mult, op1=ALU.add),
              reads=[ssb], writes=[r.b[si]])
        fw.op("act", lambda e: e.activation(out=r.t[:, o:o + w], in_=r.t[:, o:o + w], func=AF.Sqrt),
              reads=[r.b[si]], writes=[r.b[si]])
        fw.op("dve", lambda e: e.reciprocal(out=r.t[:, o:o + w], in_=r.t[:, o:o + w]),
              reads=[r.b[si]], writes=[r.b[si]])

    def resid_add(self, gi):
        fw = self.fw
        for si, (o, w) in enumerate(self.sl):
            for c in range(DC):
                t = self._tmp()
                fw.op("dve", lambda e, t=t, c=c, o=o, w=w: e.tensor_tensor(
                    out=t.t[:, 0:w], in0=self.of.t[:, c, o:o + w], in1=self.rstd.t[:, o:o + w], op=ALU.mult),
                    reads=[self.of.b[c], self.rstd.b[si]], writes=[t.b[0]])
                fw.op("dve", lambda e, t=t, c=c, o=o, w=w: e.scalar_tensor_tensor(
                    out=self.h.t[:, c, o:o + w], in0=t.t[:, 0:w], scalar=self.gains.t[:, gi, c:c + 1],
                    in1=self.h.t[:, c, o:o + w], op0=ALU.mult, op1=ALU.add),
                    reads=[t.b[0], self.h.b[c]], writes=[self.h.b[c]])

    def run(self, mT, hT_in, hT_out, wo, kmap_n, gains, wg, wu, wd, out_tok=None):
        fw = self.fw
        NT = self.NT
        fw.op("sp", lambda e: e.dma_start(out=self.gains.t[:].rearrange("p a b -> p (a b)"), in_=gains),
              writes=[self.gains.b[0]], dma=True)
        if isinstance(mT, tuple):
            mA, mB, sel = mT
            for c in range(DC):
                ta = self.blA[c % 2]
                tb = self.blB[c % 2]
                fw.op("sp", lambda e, c=c, ta=ta: e.dma_start(out=ta.t[:], in_=mA(c)),
                      writes=ta.b, dma=True)
                fw.op("sp", lambda e, c=c, tb=tb: e.dma_start(out=tb.t[:], in_=mB(c)),
                      writes=tb.b, dma=True)
                fw.op("dve", lambda e, c=c, ta=ta: e.tensor_scalar(out=self.mn.t[:, c, :], in0=ta.t[:], scalar1=sel.t[:, 0:1],
                                                             scalar2=None, op0=ALU.mult),
                      reads=ta.b + sel.b, writes=[self.mn.b[c]])
                fw.op("dve", lambda e, c=c, tb=tb: e.scalar_tensor_tensor(out=self.mn.t[:, c, :], in0=tb.t[:], scalar=sel.t[:, 1:2],
                                                                    in1=self.mn.t[:, c, :], op0=ALU.mult, op1=ALU.add),
                      reads=tb.b + sel.b + [self.mn.b[c]], writes=[self.mn.b[c]])
        else:
            for c in range(DC):
                fw.op("sp", lambda e, c=c: e.dma_start(out=self.mn.t[:, c, :], in_=mT[c * 128:(c + 1) * 128, :]),
                      writes=[self.mn.b[c]], dma=True)
        for c in range(DC):
            fw.op("sp", lambda e, c=c: e.dma_start(out=self.h.t[:, c, :], in_=hT_in[c * 128:(c + 1) * 128, :]),
                  writes=[self.h.b[c]], dma=True)
        for ec in range(DC):
            ws = self.w16.load(wo[ec], kc=kmap_n)
            if kmap_n == DC:
                kcs = list(range(DC))
            else:
                g = ec // kmap_n
                kcs = [g * kmap_n + i for i in range(kmap_n)]
            for si, (o, w) in enumerate(self.sl):
                pt, pb = self.psum.get()
                for i, kc in enumerate(kcs):
                    fw.op("pe", lambda e, pt=pt, ws=ws, i=i, kc=kc, o=o, w=w: e.matmul(
                        pt[:, 0:w], lhsT=ws.t[:, i, :], rhs=self.mn.t[:, kc, o:o + w],
                        start=(i == 0), stop=(i == len(kcs) - 1)),
                        reads=[ws.b[0], self.mn.b[kc]], writes=[pb])
                fw.op("dve", lambda e, pt=pt, ec=ec, o=o, w=w: e.tensor_scalar(
                    out=self.of.t[:, ec, o:o + w], in0=pt[:, 0:w], scalar1=self.gains.t[:, 3, ec:ec + 1],
                    scalar2=None, op0=ALU.mult),
                    reads=[pb, self.gains.b[0]], writes=[self.of.b[ec]])
                self.ss_accum(self.of.t[:, ec, o:o + w], [self.of.b[ec]], si, ec, DC)
        for si in range(len(self.sl)):
            self.make_rstd(si, D)
        self.resid_add(0)
        for c in range(DC):
            for si, (o, w) in enumerate(self.sl):
                self.ss_accum(self.h.t[:, c, o:o + w], [self.h.b[c]], si, c, DC)
        for si in range(len(self.sl)):
            self.make_rstd(si, D)
        for si, (o, w) in enumerate(self.sl):
            for c in range(DC):
                t = self._tmp()
                fw.op("dve", lambda e, t=t, c=c, o=o, w=w: e.tensor_tensor(
                    out=t.t[:, 0:w], in0=self.h.t[:, c, o:o + w], in1=self.rstd.t[:, o:o + w], op=ALU.mult),
                    reads=[self.h.b[c], self.rstd.b[si]], writes=[t.b[0]])
                fw.op("act", lambda e, t=t, c=c, o=o, w=w: e.activation(
                    out=self.mn.t[:, c, o:o + w], in_=t.t[:, 0:w], func=AF.Copy,
                    scale=self.gains.t[:, 1, c:c + 1]),
                    reads=[t.b[0], self.gains.b[0]], writes=[self.mn.b[c]])
        for fc in range(FC):
            wsg = self.w16.load(wg[fc])
            wsu = self.w16.load(wu[fc])
            for si, (o, w) in enumerate(self.sl):
                pg, pgb = self.psum.get()
                pu, pub = self.psum.get()
                for kc in range(DC):
                    fw.op("pe", lambda e, pg=pg, wsg=wsg, kc=kc, o=o, w=w: e.matmul(
                        pg[:, 0:w], lhsT=wsg.t[:, kc, :], rhs=self.mn.t[:, kc, o:o + w],
                        start=(kc == 0), stop=(kc == DC - 1)),
                        reads=[wsg.b[0], self.mn.b[kc]], writes=[pgb])
                for kc in range(DC):
                    fw.op("pe", lambda e, pu=pu, wsu=wsu, kc=kc, o=o, w=w: e.matmul(
                        pu[:, 0:w], lhsT=wsu.t[:, kc, :], rhs=self.mn.t[:, kc, o:o + w],
                        start=(kc == 0), stop=(kc == DC - 1)),
                        reads=[wsu.b[0], self.mn.b[kc]], writes=[pub])
                t = self._tmp()
                fw.op("act", lambda e, t=t, pg=pg, w=w: e.activation(out=t.t[:, 0:w], in_=pg[:, 0:w], func=AF.Silu),
                      reads=[pgb], writes=[t.b[0]])
                fw.op("dve", lambda e, t=t, pu=pu, fc=fc, o=o, w=w: e.tensor_tensor(
                    out=self.act.t[:, fc, o:o + w], in0=pu[:, 0:w], in1=t.t[:, 0:w], op=ALU.mult),
                    reads=[pub, t.b[0]], writes=[self.act.b[fc]])
        for ec in range(DC):
            ws = self.w44.load(wd[ec])
            for si, (o, w) in enumerate(self.sl):
                pt, pb = self.psum.get()
                for kc in range(FC):
                    fw.op("pe", lambda e, pt=pt, ws=ws, kc=kc, o=o, w=w: e.matmul(
                        pt[:, 0:w], lhsT=ws.t[:, kc, :], rhs=self.act.t[:, kc, o:o + w],
                        start=(kc == 0), stop=(kc == FC - 1)),
                        reads=[ws.b[0], self.act.b[kc]], writes=[pb])
                fw.op("dve", lambda e, pt=pt, ec=ec, o=o, w=w: e.tensor_copy(
                    out=self.of.t[:, ec, o:o + w], in_=pt[:, 0:w]),
                    reads=[pb], writes=[self.of.b[ec]])
                self.ss_accum(self.of.t[:, ec, o:o + w], [self.of.b[ec]], si, ec, DC)
        for si in range(len(self.sl)):
            self.make_rstd(si, D)
        self.resid_add(2)
        toks = []
        for c in range(DC):
            toks.append(fw.op("sp", lambda e, c=c: e.dma_start(out=hT_out[c * 128:(c + 1) * 128, :],
                                                               in_=self.h.t[:, c, :]),
                              reads=[self.h.b[c]], dma=True))
        return toks


def tile_w(W, kc_list_per_out=None):
    K, M = W.shape
    KC, MC = K // 128, M // 128
    return np.ascontiguousarray(W.reshape(KC, 128, MC, 128).transpose(2, 1, 0, 3)).reshape(MC, 128, KC * 128)


def gains_pack(g1, g2, g3, osc):
    arr = np.stack([g.reshape(16, 128).T for g in (g1, g2, g3, osc)], axis=1)
    return np.ascontiguousarray(arr.reshape(128, 64)).astype(np.float32)


import math

L = 2064
NTILE = 17
CO = 1024
CC = 8
C0 = -math.exp(-0.5)
LNX_EPS = 64e-5


def tiles_of(Ltot=L):
    out = []
    o = 0
    while o < Ltot:
        out.append((o, min(128, Ltot - o)))
        o += 128
    return out


class H:
    def __init__(self, fw):
        self.fw = fw

    def tt(self, eng, out, in0, in1, op, r, w):
        return self.fw.op(eng, lambda e: e.tensor_tensor(out=out, in0=in0, in1=in1, op=op), reads=r, writes=w)

    def ts(self, eng, out, in0, s1, s2, op0, op1, r, w):
        if op1 is None:
            return self.fw.op(eng, lambda e: e.tensor_scalar(out=out, in0=in0, scalar1=s1, scalar2=None, op0=op0),
                              reads=r, writes=w)
        return self.fw.op(eng, lambda e: e.tensor_scalar(out=out, in0=in0, scalar1=s1, scalar2=s2, op0=op0, op1=op1),
                          reads=r, writes=w)

    def stt(self, out, in0, sc, in1, op0, op1, r, w):
        return self.fw.op("dve", lambda e: e.scalar_tensor_tensor(out=out, in0=in0, scalar=sc, in1=in1, op0=op0,
                                                                  op1=op1), reads=r, writes=w)

    def act(self, out, in_, func, r, w, scale=None, bias=None, eng="act"):
        kw = {}
        if scale is not None:
            kw["scale"] = scale
        if bias is not None:
            kw["bias"] = bias
        return self.fw.op(eng, lambda e: e.activation(out=out, in_=in_, func=func, **kw), reads=r, writes=w)

    def cp(self, eng, out, in_, r, w):
        if eng == "act":
            return self.fw.op("act", lambda e: e.activation(out=out, in_=in_, func=AF.Copy), reads=r, writes=w)
        return self.fw.op(eng, lambda e: e.tensor_copy(out=out, in_=in_), reads=r, writes=w)

    def mm(self, out, lhsT, rhs, start, stop, r, w):
        return self.fw.op("pe", lambda e: e.matmul(out, lhsT=lhsT, rhs=rhs, start=start, stop=stop), reads=r, writes=w)

    def dma(self, eng, out, in_, r, w):
        return self.fw.op(eng, lambda e: e.dma_start(out=out, in_=in_), reads=r, writes=w, dma=True)


PFX = [""]


def declare_rwkv(nc, vres, fused=None):
    d = {}
    P = PFX[0]
    I = lambda n, s, dt=F32: nc.dram_tensor(P + n, s, dt, kind="ExternalInput").ap()
    if fused is None:
        hT = I("hT", [D, L])
        d["hT_fn"] = lambda o, w: hT[:, o:o + w]
    else:
        d["hT_fn"] = fused["hT_fn"]
        d["chunked"] = True
    d["gpre"] = I("gpre", [128, 16])
    d["mixp"] = I("mixp", [128, 6 * 16])
    for n in ("w_r", "w_k", "w_v"):
        d[n] = I(n, [CC, 128, 2048])
    d["w1t"] = I("w1t", [128, 16 * 96])
    d["a1t"] = I("a1t", [128, 16 * 96])
    d["g1t"] = I("g1t", [2, 128, 2048])
    d["w2o"] = I("w2o", [96, CO])
    d["a2o"] = I("a2o", [96, CO])
    d["g2o"] = I("g2o", [128, 2 * CO])
    d["chanp"] = I("chanp", [128, 8 * CC])
    if vres:
        d["v1t"] = I("v1t", [128, 16 * 64])
        d["v2o"] = I("v2o", [64, CO])
        d["vfirst"] = I("vfirst", [CO, L]) if fused is None else fused["vfirst"]
        d["vscr"] = nc.dram_tensor(P + "vscr", [CO, L], F32, kind="Internal").ap()
    else:
        if fused is None:
            d["vscr"] = nc.dram_tensor(P + "vfirst_out", [CO, L], F32, kind="ExternalOutput").ap()
        else:
            d["vscr"] = nc.dram_tensor(P + "vfirst_scr", [CO, L], F32, kind="Internal").ap()
    d["scr"] = [nc.dram_tensor(P + f"scr{i}", [CO, L], F32, kind="Internal").ap() for i in range(5)]
    if fused is None:
        d["mT_out"] = nc.dram_tensor(P + "mT_out", [CO, L], BF16, kind="ExternalOutput").ap()
    else:
        d["mT_out_t"] = nc.dram_tensor(P + "mT_src", [CO, L], F32)
        d["m_dt"] = F32
        d["mT_out"] = d["mT_out_t"].ap()
    return d


def rwkv_phase1(fw, d, vres):
    h = H(fw)
    TH = 688
    SL = 344
    ones = Tile(fw, "ones1", [128, 128], BF16)
    fw.op("pool", lambda e: e.memset(ones.t[:], 1.0), writes=[ones.b[0]])
    hnb = Tile(fw, "hnb", [128, 16, L + 2], BF16, 16 * 3)
    xs = [Tile(fw, f"xs{i}", [128, 16, TH], BF16, 16) for i in range(2)]
    hsl = [Tile(fw, f"hsl{i}", [128, 16, SL], F32) for i in range(1)]
    gpre = Tile(fw, "gpre", [128, 16], F32)
    mixp = Tile(fw, "mixp", [128, 6, 16], F32)
    omm = Tile(fw, "omm", [128, 6, 16], F32)
    chanp = Tile(fw, "chanp1", [128, 8, CC], F32)
    w2 = Tile(fw, "w2", [96, CO], BF16)
    a2 = Tile(fw, "a2", [96, CO], BF16)
    g2 = Tile(fw, "g2", [128, 2, CO], BF16)
    slots = WSlots(fw, "ws1_", 16, 4)
    psum = PsPool(fw, [f"ps1_{i}" for i in range(7)])
    sst, ssb = fw.ps("ss1", [128, 512], F32), fw.buf("ss1")
    sq = [Tile(fw, f"sq1_{i}", [128, SL], BF16) for i in range(2)]
    rstd = Tile(fw, "rstd1", [128, SL], F32)
    zt = Tile(fw, "zt", [128, 2, TH], BF16)
    ev = [Tile(fw, f"ev{i}", [128, SL], F32) for i in range(4)]
    evi = [0]

    def getev():
        t = ev[evi[0]]
        evi[0] = (evi[0] + 1) % len(ev)
        return t

    h.dma("sp", gpre.t[:], d["gpre"], [], gpre.b)
    h.dma("sp", mixp.t[:].rearrange("p a b -> p (a b)"), d["mixp"], [], mixp.b)
    h.dma("sp", chanp.t[:].rearrange("p a b -> p (a b)"), d["chanp"], [], chanp.b)
    h.dma("pool", w2.t[:], d["w2o"], [], w2.b)
    h.dma("pool", a2.t[:], d["a2o"], [], a2.b)
    h.dma("pool", g2.t[:].rearrange("p a b -> p (a b)"), d["g2o"], [], g2.b)
    if vres:
        v2 = Tile(fw, "v2", [64, CO], BF16)
        h.dma("pool", v2.t[:], d["v2o"], [], v2.b)
    h.ts("dve", omm.t[:], mixp.t[:], -1.0, 1.0, ALU.mult, ALU.add, mixp.b, omm.b)
    allhb = hnb.b
    fw.op("pool", lambda e: e.memset(hnb.t[:, :, 0:1], 0.0), writes=[hnb.b[c * 3] for c in range(16)])

    for s in range(6):
        o = s * SL
        third = o // TH
        hs = hsl[0]
        if d.get("chunked"):
            for c in range(16):
                h.dma("sp", hs.t[:, c, :], d["hT_fn"](c, o, SL), [], hs.b)
        else:
            h.dma("sp", hs.t[:], d["hT_fn"](o, SL).rearrange("(c p) t -> p c t", p=128), [], hs.b)
        for c in range(16):
            q = sq[c % 2]
            h.act(q.t[:], hs.t[:, c, :], AF.Square, hs.b, q.b)
            h.mm(sst[:, 0:SL], ones.t[:], q.t[:], c == 0, c == 15, q.b + ones.b, [ssb])
        h.ts("dve", rstd.t[:], sst[:, 0:SL], 1.0 / D, RMS_EPS, ALU.mult, ALU.add, [ssb], rstd.b)
        h.act(rstd.t[:], rstd.t[:], AF.Sqrt, rstd.b, rstd.b)
        fw.op("dve", lambda e: e.reciprocal(out=rstd.t[:], in_=rstd.t[:]), reads=rstd.b, writes=rstd.b)
        for c in range(16):
            h.stt(hnb.t[:, c, 1 + o:1 + o + SL], hs.t[:, c, :], gpre.t[:, c:c + 1], rstd.t[:], ALU.mult, ALU.mult,
                  hs.b + gpre.b + rstd.b, [hnb.b[c * 3 + third]])

    scr = d["scr"]
    plan = [("r", 0), ("w", 1), ("k", 2), ("v", 3), ("a", 4), ("g", 5)]
    xi = 0
    xx = Tile(fw, "xx1", [128, 16, TH], BF16, 16)
    for third in range(3):
        o = third * TH
        for c in range(16):
            rb = [hnb.b[c * 3 + third]] + ([hnb.b[c * 3 + third - 1]] if third > 0 else [])
            h.tt("dve", xx.t[:, c, :], hnb.t[:, c, o:o + TH], hnb.t[:, c, 1 + o:1 + o + TH], ALU.subtract,
                 rb, [xx.b[c]])
        for pname, mi in plan:
            x = xs[xi % 2]
            xi += 1
            for c in range(16):
                h.stt(x.t[:, c, :], xx.t[:, c, :], mixp.t[:, mi, c:c + 1], hnb.t[:, c, 1 + o:1 + o + TH], ALU.mult, ALU.add,
                      [xx.b[c], hnb.b[c * 3 + third]] + mixp.b, [x.b[c]])

            def proj(ws, width, sl_o, pt, pb):
                for kc in range(16):
                    h.mm(pt[0:width, 0:SL], ws.t[:, kc, 0:width], x.t[:, kc, sl_o:sl_o + SL], kc == 0, kc == 15,
                         ws.b + [x.b[kc]], [pb])

            if pname in ("r", "k"):
                dst = scr[0] if pname == "r" else scr[1]
                for cc in range(CC):
                    ws = slots.load(d["w_" + pname][cc])
                    for s2 in range(2):
                        pt, pb = psum.get()
                        proj(ws, 128, s2 * SL, pt, pb)
                        e_ = getev()
                        h.cp("act", e_.t[:], pt[:, 0:SL], [pb], e_.b)
                        t0 = o + s2 * SL
                        h.dma("sp", dst[cc * 128:(cc + 1) * 128, t0:t0 + SL], e_.t[:], e_.b, [])
            elif pname == "v":
                if vres:
                    s_ = slots.slots[slots.i]
                    slots.i = (slots.i + 1) % len(slots.slots)
                    fw.op("pool", lambda e, s_=s_: e.dma_start(out=s_.t[:, :, 0:64],
                                                               in_=d["v1t"].rearrange("p (k m) -> p k m", m=64)),
                          writes=s_.b, dma=True)
                    for s2 in range(2):
                        pt, pb = psum.get()
                        proj(s_, 64, s2 * SL, pt, pb)
                        h.cp("act", zt.t[0:64, 0, s2 * SL:(s2 + 1) * SL], pt[0:64, 0:SL], [pb], zt.b)
                for cc in range(CC):
                    ws = slots.load(d["w_v"][cc])
                    for s2 in range(2):
                        t0 = o + s2 * SL
                        pt, pb = psum.get()
                        proj(ws, 128, s2 * SL, pt, pb)
                        e_ = getev()
                        if not vres:
                            h.cp("act", e_.t[:], pt[:, 0:SL], [pb], e_.b)
                        else:
                            p2, p2b = psum.get()
                            h.mm(p2[:, 0:SL], v2.t[:, cc * 128:(cc + 1) * 128], zt.t[0:64, 0, s2 * SL:(s2 + 1) * SL],
                                 True, True, v2.b + zt.b, [p2b])
                            vm = getev()
                            h.act(vm.t[:], p2[:, 0:SL], AF.Sigmoid, [p2b] + chanp.b, vm.b, bias=chanp.t[:, 2, cc:cc + 1])
                            vf = getev()
                            h.dma("sp", vf.t[:], d["vfirst"][cc * 128:(cc + 1) * 128, t0:t0 + SL], [], vf.b)
                            h.tt("dve", vf.t[:], vf.t[:], pt[:, 0:SL], ALU.subtract, vf.b + [pb], vf.b)
                            h.tt("dve", vf.t[:], vf.t[:], vm.t[:], ALU.mult, vf.b + vm.b, vf.b)
                            h.tt("dve", e_.t[:], vf.t[:], pt[:, 0:SL], ALU.add, vf.b + [pb], e_.b)
                        h.dma("sp", d["vscr"][cc * 128:(cc + 1) * 128, t0:t0 + SL], e_.t[:], e_.b, [])
            elif pname in ("w", "a"):
                s_ = slots.slots[slots.i]
                slots.i = (slots.i + 1) % len(slots.slots)
                src = d["w1t"] if pname == "w" else d["a1t"]
                fw.op("pool", lambda e, s_=s_, src=src: e.dma_start(out=s_.t[:, :, 0:96],
                                                                    in_=src.rearrange("p (k m) -> p k m", m=96)),
                      writes=s_.b, dma=True)
                for s2 in range(2):
                    pt, pb = psum.get()
                    proj(s_, 96, s2 * SL, pt, pb)
                    if pname == "w":
                        h.act(zt.t[0:96, 0, s2 * SL:(s2 + 1) * SL], pt[0:96, 0:SL], AF.Tanh, [pb], zt.b)
                    else:
                        h.cp("act", zt.t[0:96, 0, s2 * SL:(s2 + 1) * SL], pt[0:96, 0:SL], [pb], zt.b)
                w2t = w2 if pname == "w" else a2
                pi = 0 if pname == "w" else 1
                dst = scr[2] if pname == "w" else scr[3]
                for cc in range(CC):
                    for s2 in range(2):
                        t0 = o + s2 * SL
                        pt, pb = psum.get()
                        h.mm(pt[:, 0:SL], w2t.t[:, cc * 128:(cc + 1) * 128], zt.t[0:96, 0, s2 * SL:(s2 + 1) * SL],
                             True, True, w2t.b + zt.b, [pb])
                        e_ = getev()
                        h.act(e_.t[:], pt[:, 0:SL], AF.Sigmoid, [pb] + chanp.b, e_.b, bias=chanp.t[:, pi, cc:cc + 1])
                        h.dma("sp", dst[cc * 128:(cc + 1) * 128, t0:t0 + SL], e_.t[:], e_.b, [])
            elif pname == "g":
                for gc in range(2):
                    ws = slots.load(d["g1t"][gc])
                    for s2 in range(2):
                        pt, pb = psum.get()
                        proj(ws, 128, s2 * SL, pt, pb)
                        h.act(zt.t[:, gc, s2 * SL:(s2 + 1) * SL], pt[:, 0:SL], AF.Sigmoid, [pb], zt.b)
                for cc in range(CC):
                    for s2 in range(2):
                        t0 = o + s2 * SL
                        pt, pb = psum.get()
                        for gc in range(2):
                            h.mm(pt[:, 0:SL], g2.t[:, gc, cc * 128:(cc + 1) * 128], zt.t[:, gc, s2 * SL:(s2 + 1) * SL],
                                 gc == 0, gc == 1, g2.b + zt.b, [pb])
                        e_ = getev()
                        h.cp("act", e_.t[:], pt[:, 0:SL], [pb], e_.b)
                        h.dma("sp", scr[4][cc * 128:(cc + 1) * 128, t0:t0 + SL], e_.t[:], e_.b, [])
    e = fw.E["sp"]
    fw.final_wait("sp", [t for t in e.dlast if t is not None])


DBG_ON = False
DBG_SPECS = {}


def rwkv_phase2(fw, d, consts):
    h = H(fw)
    nc = fw.nc

    def dbg(name, ap, shape, dt, bufs):
        if not DBG_ON:
            return
        t = nc.dram_tensor("dbg_" + name, list(shape), dt, kind="ExternalOutput").ap()
        DBG_SPECS[name] = shape
        h.dma("sp", t, ap, bufs, [])

    TL = tiles_of()
    NG = 2
    GC_ = 4
    ident = Tile(fw, "ident", [128, 128], BF16)
    mask4 = Tile(fw, "mask4", [128, 4, 128], BF16)
    lsm = Tile(fw, "lsm", [128, 128], BF16)
    bdones = Tile(fw, "bdones", [128, 128], BF16)
    onesf = Tile(fw, "onesf", [128, 128], F32)
    h.dma("pool", ident.t[:], consts["ident"], [], ident.b)
    h.dma("pool", mask4.t[:].rearrange("p a b -> p (a b)"), consts["mask4"], [], mask4.b)
    h.dma("pool", lsm.t[:], consts["lsm"], [], lsm.b)
    h.dma("pool", bdones.t[:], consts["bdones"], [], bdones.b)
    fw.op("pool", lambda e: e.memset(onesf.t[:], 1.0), writes=onesf.b)
    chanp = Tile(fw, "chanp2", [128, 8, CC], F32)
    h.dma("sp", chanp.t[:].rearrange("p a b -> p (a b)"), d["chanp"], [], chanp.b)
    PW = {}
    for nm, qi in (("kk", 3), ("ka", 4), ("rk", 5), ("lg", 6), ("lb", 7)):
        t = Tile(fw, "pw_" + nm, [128, CC, 128], F32)
        for cc in range(CC):
            h.ts("pool", t.t[:, cc, :], onesf.t[:], chanp.t[:, qi, cc:cc + 1], None, ALU.mult, None,
                 onesf.b + chanp.b, t.b)
        PW[nm] = t

    def dbl(name, shape, dt):
        return [Tile(fw, f"{name}{i}", shape, dt) for i in range(2)]

    W3 = [128, GC_, 128]
    IN = {nm: dbl("in_" + nm, W3, F32) for nm in ("r", "k", "v", "sw", "as", "g")}
    AR = dbl("AR", [128, GC_, 2, 128], BF16)
    BK = dbl("BK", [128, GC_, 2, 128], BF16)
    FB = dbl("FB", [128, GC_, 3, 128], BF16)
    TM = dbl("TM", [128, GC_, 4, 128], BF16)
    BV = dbl("BV", W3, F32)
    GCt = dbl("GCt", [128, GC_], F32)
    tnames = ["kk0", "kk", "kp", "bq", "cs", "G1", "G2", "G3", "G4", "t1", "rn", "d1"]
    TP = {nm: Tile(fw, "tp_" + nm, W3, F32) for nm in tnames}
    csA = Tile(fw, "csA", [128, GC_, 256], F32)
    csB = Tile(fw, "csB", [128, GC_, 256], F32)
    fw.op("pool", lambda e: e.memset(csA.t[:], 0.0), writes=csA.b)
    fw.op("pool", lambda e: e.memset(csB.t[:], 0.0), writes=csB.b)
    sq16 = Tile(fw, "sq16", W3, BF16)
    rk16 = Tile(fw, "rk16", W3, BF16)
    cl = Tile(fw, "cl", [128, GC_], F32)
    y32G = [Tile(fw, f"y32_{q}", W3, F32) for q in range(2)]
    y16 = Tile(fw, "y16", W3, BF16)
    yc = Tile(fw, "yc", W3, F32)
    rs = Tile(fw, "rs", W3, F32)
    yn = Tile(fw, "yn", W3, F32)
    m16 = dbl("m16", W3, d.get("m_dt", BF16))
    NH = 8
    MallG = [[Tile(fw, f"Mall{q}_{i}", [128, 4, 128], BF16) for i in range(NH)] for q in range(NG)]
    PT0G = [[Tile(fw, f"PT0_{q}_{i}", [128, 128], BF16) for i in range(NH)] for q in range(NG)]
    TtG = [[[Tile(fw, f"T{q}_{i}_{j}", [128, 128], BF16) for j in range(2)] for i in range(NH)] for q in range(NG)]
    PPG = [[[Tile(fw, f"PP{q}_{i}_{j}", [128, 2, 128], BF16) for j in range(2)] for i in range(NH)] for q in range(NG)]
    XsG = [Tile(fw, f"Xs{q}", [128, NH, 64], BF16) for q in range(NG)]
    AhG = [Tile(fw, f"Ah{q}", [128, GC_, 128], BF16) for q in range(NG)]
    vnG = [Tile(fw, f"vn{q}", [128, NH, 64], BF16) for q in range(NG)]
    ST32 = [Tile(fw, f"ST32_{g}", [128, GC_, 64], F32) for g in range(NG)]
    STb = [[Tile(fw, f"STb{g}_{i}", [128, GC_, 64], BF16) for i in range(2)] for g in range(NG)]
    for g in range(NG):
        fw.op("pool", lambda e, g=g: e.memset(ST32[g].t[:], 0.0), writes=ST32[g].b)
        fw.op("pool", lambda e, g=g: e.memset(STb[g][0].t[:], 0.0), writes=STb[g][0].b)
    psum = PsPool(fw, [f"ps2_{i}" for i in range(8)])

    def v3(pt, b):
        return pt[:, :].rearrange("p (a b) -> p a b", b=b)

    scr = d["scr"]
    srcs = {"r": scr[0], "k": scr[1], "sw": scr[2], "as": scr[3], "g": scr[4], "v": d["vscr"]}
    out_toks = []
    def iter_body(ti, t0, tn, g, it):
        if True:
            par = it % 2
            Mall, PT0, Tt, PP, Xs, Ah, vn = MallG[g], PT0G[g], TtG[g], PPG[g], XsG[g], AhG[g], vnG[g]
            y32 = y32G[g]
            cs_ = slice(g * GC_, (g + 1) * GC_)
            X = {}
            for nm in IN:
                tl = IN[nm][par]
                h.dma("sp", tl.t[:, :, 0:tn],
                      srcs[nm][g * 512:(g + 1) * 512, t0:t0 + tn].rearrange("(c p) t -> p c t", p=128), [], tl.b)
                X[nm] = tl
            R, K, V, SW, AS, G = (X[n] for n in ("r", "k", "v", "sw", "as", "g"))
            W = lambda t: t.t[:, :, 0:tn]
            ar, bk, fb, tm, bv, gct = AR[par], BK[par], FB[par], TM[par], BV[par], GCt[par]
            T_ = TP
            h.tt("pool", W(T_["kk0"]), W(K), PW["kk"].t[:, cs_, 0:tn], ALU.mult, K.b + PW["kk"].b, T_["kk0"].b)
            h.act(W(sq16), W(T_["kk0"]), AF.Square, T_["kk0"].b, sq16.b)
            pt, pb = psum.get()
            p3 = v3(pt, 128)
            h.mm(p3[:, :, 0:tn], bdones.t[:], W(sq16), True, True, bdones.b + sq16.b, [pb])
            h.act(W(T_["rn"]), p3[:, :, 0:tn], AF.Sqrt, [pb], T_["rn"].b)
            h.ts("dve", W(T_["rn"]), W(T_["rn"]), 1e-12, None, ALU.max, None, T_["rn"].b, T_["rn"].b)
            fw.op("dve", lambda e, a=W(T_["rn"]): e.reciprocal(out=a, in_=a), reads=T_["rn"].b, writes=T_["rn"].b)
            h.tt("dve", W(T_["kk"]), W(T_["kk0"]), W(T_["rn"]), ALU.mult, T_["kk0"].b + T_["rn"].b, T_["kk"].b)
            h.stt(W(T_["t1"]), W(AS), -1.0, PW["ka"].t[:, cs_, 0:tn], ALU.add, ALU.mult, AS.b + PW["ka"].b, T_["t1"].b)
            h.stt(W(T_["kp"]), W(T_["t1"]), 1.0, W(K), ALU.add, ALU.mult, T_["t1"].b + K.b, T_["kp"].b)
            h.tt("pool", W(T_["bq"]), W(T_["kk"]), W(AS), ALU.mult, T_["kk"].b + AS.b, T_["bq"].b)
            h.cp("pool", csA.t[:, :, 128:128 + tn], W(SW), SW.b, csA.b)
            src_, dst_ = csA, csB
            s_ = 1
            lvl = 0
            while s_ < tn:
                eng_ = "dve" if lvl % 2 == 0 else "pool"
                h.tt(eng_, dst_.t[:, :, 128:128 + tn], src_.t[:, :, 128:128 + tn], src_.t[:, :, 128 - s_:128 - s_ + tn],
                     ALU.add, src_.b, dst_.b)
                src_, dst_ = dst_, src_
                s_ *= 2
                lvl += 1
            h.cp("pool", W(T_["cs"]), src_.t[:, :, 128:128 + tn], src_.b, T_["cs"].b)
            h.ts("dve", cl.t[:], T_["cs"].t[:, :, tn - 1], C0, None, ALU.mult, None, T_["cs"].b, cl.b)
            h.act(gct.t[:], cl.t[:], AF.Exp, cl.b, gct.b)
            h.act(W(T_["G1"]), W(T_["cs"]), AF.Exp, T_["cs"].b, T_["G1"].b, scale=C0)
            h.act(W(T_["G2"]), W(T_["cs"]), AF.Exp, T_["cs"].b, T_["G2"].b, scale=-C0)
            h.tt("pool", W(T_["d1"]), W(T_["cs"]), W(SW), ALU.subtract, T_["cs"].b + SW.b, T_["d1"].b)
            h.act(W(T_["G3"]), W(T_["d1"]), AF.Exp, T_["d1"].b, T_["G3"].b, scale=C0)
            for c in range(GC_):
                h.act(T_["G4"].t[:, c, 0:tn], T_["cs"].t[:, c, 0:tn], AF.Exp, T_["cs"].b + cl.b, T_["G4"].b,
                      scale=-C0, bias=cl.t[:, c:c + 1])
            h.stt(ar.t[:, :, 0, 0:tn], W(T_["kk"]), -1.0, W(T_["G3"]), ALU.mult, ALU.mult, T_["kk"].b + T_["G3"].b, ar.b)
            h.tt("dve", ar.t[:, :, 1, 0:tn], W(R), W(T_["G1"]), ALU.mult, R.b + T_["G1"].b, ar.b)
            h.tt("dve", bk.t[:, :, 0, 0:tn], W(T_["bq"]), W(T_["G2"]), ALU.mult, T_["bq"].b + T_["G2"].b, bk.b)
            h.tt("dve", bk.t[:, :, 1, 0:tn], W(T_["kp"]), W(T_["G2"]), ALU.mult, T_["kp"].b + T_["G2"].b, bk.b)
            h.tt("pool", fb.t[:, :, 0, 0:tn], W(T_["bq"]), W(T_["G4"]), ALU.mult, T_["bq"].b + T_["G4"].b, fb.b)
            h.tt("pool", fb.t[:, :, 1, 0:tn], W(T_["kp"]), W(T_["G4"]), ALU.mult, T_["kp"].b + T_["G4"].b, fb.b)
            h.cp("pool", fb.t[:, :, 2, 0:tn], W(V), V.b, fb.b)
            h.tt("pool", W(T_["t1"]), W(R), PW["rk"].t[:, cs_, 0:tn], ALU.mult, R.b + PW["rk"].b, T_["t1"].b)
            h.tt("dve", W(rk16), W(T_["t1"]), W(T_["kp"]), ALU.mult, T_["t1"].b + T_["kp"].b, rk16.b)
            pt, pb = psum.get()
            p3 = v3(pt, 128)
            h.mm(p3[:, :, 0:tn], bdones.t[:], W(rk16), True, True, bdones.b + rk16.b, [pb])
            h.tt("dve", W(bv), p3[:, :, 0:tn], W(V), ALU.mult, [pb] + V.b, bv.b)
            for c in range(GC_):
                pt, pb = psum.get()
                p3 = v3(pt, 128)
                srcs_t = [ar.t[:, c, 0, 0:tn], fb.t[:, c, 0, 0:tn], fb.t[:, c, 1, 0:tn], fb.t[:, c, 2, 0:tn]]
                srcb = [ar.b, fb.b, fb.b, fb.b]
                for q in range(4):
                    h.mm(p3[0:tn, q, :], srcs_t[q], ident.t[:], True, True, srcb[q] + ident.b, [pb])
                h.cp("act", tm.t[0:tn, c, :, :], p3[0:tn, :, :], [pb], tm.b)
            yield "prep"
            heads = [(c, h2) for c in range(GC_) for h2 in range(2)]
            K_lv = max(1, math.ceil(math.log2(tn)))
            Tcur = {}
            for hi, (c, h2) in enumerate(heads):
                rows = slice(64 * h2, 64 * h2 + 64)
                pt, pb = psum.get()
                p3 = v3(pt, 128)
                h.mm(p3[0:tn, 0:2, 0:tn], bk.t[rows, c, 0, 0:tn], ar.t[rows, c, :, 0:tn], True, True, bk.b + ar.b, [pb])
                h.mm(p3[0:tn, 2:4, 0:tn], bk.t[rows, c, 1, 0:tn], ar.t[rows, c, :, 0:tn], True, True, bk.b + ar.b, [pb])
                ma = Mall[hi]
                h.tt("dve", ma.t[0:tn, :, 0:tn], p3[0:tn, :, 0:tn], mask4.t[0:tn, :, 0:tn], ALU.mult, [pb] + mask4.b, ma.b)
                pt, pb = psum.get()
                h.mm(pt[0:tn, 0:tn], ar.t[rows, c, 0, 0:tn], bk.t[rows, c, 0, 0:tn], True, True, ar.b + bk.b, [pb])
                h.tt("dve", PT0[hi].t[0:tn, 0:tn], pt[0:tn, 0:tn], lsm.t[0:tn, 0:tn], ALU.mult, [pb] + lsm.b, PT0[hi].b)
                h.tt("pool", Tt[hi][0].t[0:tn, 0:tn], ma.t[0:tn, 0, 0:tn], ident.t[0:tn, 0:tn], ALU.add,
                     ma.b + ident.b, Tt[hi][0].b)
                Tcur[hi] = 0
            yield "c"
            for k in range(1, K_lv):
                last = (k == K_lv - 1)
                for hi in range(len(heads)):
                    if k == 1:
                        P_ap, P_b = Mall[hi].t[0:tn, 0, 0:tn], Mall[hi].b
                        PT_ap, PT_b = PT0[hi].t[0:tn, 0:tn], PT0[hi].b
                    else:
                        pp = PP[hi][(k - 1) % 2]
                        P_ap, P_b = pp.t[0:tn, 0, 0:tn], pp.b
                        PT_ap, PT_b = pp.t[0:tn, 1, 0:tn], pp.b
                    ppn = PP[hi][k % 2]
                    pt, pb = psum.get()
                    p3 = v3(pt, 128)
                    h.mm(p3[0:tn, 1, 0:tn], P_ap, PT_ap, True, True, P_b + PT_b, [pb])
                    if not last:
                        h.mm(p3[0:tn, 0, 0:tn], PT_ap, P_ap, True, True, P_b + PT_b, [pb])
                        h.cp("act", ppn.t[0:tn, :, 0:tn], p3[0:tn, 0:2, 0:tn], [pb], ppn.b)
                    else:
                        h.cp("act", ppn.t[0:tn, 1, 0:tn], p3[0:tn, 1, 0:tn], [pb], ppn.b)
                yield "c"
                for hi in range(len(heads)):
                    ppn = PP[hi][k % 2]
                    tc_ = Tt[hi][Tcur[hi]]
                    tn_ = Tt[hi][1 - Tcur[hi]]
                    pt, pb = psum.get()
                    h.mm(pt[0:tn, 0:tn], ppn.t[0:tn, 1, 0:tn], tc_.t[0:tn, 0:tn], True, True, ppn.b + tc_.b, [pb])
                    h.tt("dve", tn_.t[0:tn, 0:tn], pt[0:tn, 0:tn], tc_.t[0:tn, 0:tn], ALU.add, [pb] + tc_.b, tn_.b)
                    Tcur[hi] = 1 - Tcur[hi]
                yield "c"
            TF = [Tt[hi][Tcur[hi]] for hi in range(len(heads))]
            pt, pb = psum.get()
            p3 = v3(pt, 64)
            for hi, (c, h2) in enumerate(heads):
                cols = slice(64 * h2, 64 * h2 + 64)
                h.mm(p3[0:tn, hi, :], Mall[hi].t[0:tn, 2, 0:tn], tm.t[0:tn, c, 3, cols], True, True, Mall[hi].b + tm.b, [pb])
            h.cp("act", Xs.t[0:tn, :, :], p3[0:tn, 0:NH, :], [pb], Xs.b)
            yield "c"
            pt, pb = psum.get()
            p3 = v3(pt, 128)
            for hi, (c, h2) in enumerate(heads):
                rows = slice(64 * h2, 64 * h2 + 64)
                h.mm(p3[rows, c, 0:tn], tm.t[0:tn, c, 0, rows], TF[hi].t[0:tn, 0:tn], True, True, tm.b + TF[hi].b, [pb])
            h.cp("act", Ah.t[:, :, 0:tn], p3[:, 0:GC_, 0:tn], [pb], Ah.b)
            yield "c"
            stb_old = STb[g][ti % 2]
            stb_new = STb[g][(ti + 1) % 2]
            pt, pb = psum.get()
            p3 = v3(pt, 64)
            for hi, (c, h2) in enumerate(heads):
                rows = slice(64 * h2, 64 * h2 + 64)
                h.mm(p3[0:tn, hi, :], TF[hi].t[0:tn, 0:tn], Xs.t[0:tn, hi, :], True, False, TF[hi].b + Xs.b, [pb])
                h.mm(p3[0:tn, hi, :], Ah.t[rows, c, 0:tn], stb_old.t[rows, c, :], False, True, Ah.b + stb_old.b, [pb])
            h.cp("dve", vn.t[0:tn, :, :], p3[0:tn, 0:NH, :], [pb], vn.b)
            yield "c"
            pty, pyb = psum.get()
            py3 = v3(pty, 128)
            for hi, (c, h2) in enumerate(heads):
                rows = slice(64 * h2, 64 * h2 + 64)
                h.mm(py3[rows, c, 0:tn], stb_old.t[rows, c, :], ar.t[rows, c, 1, 0:tn], True, False, stb_old.b + ar.b, [pyb])
                h.mm(py3[rows, c, 0:tn], tm.t[0:tn, c, 3, rows], Mall[hi].t[0:tn, 3, 0:tn], False, False, tm.b + Mall[hi].b, [pyb])
                h.mm(py3[rows, c, 0:tn], vn.t[0:tn, hi, :], Mall[hi].t[0:tn, 1, 0:tn], False, True, vn.b + Mall[hi].b, [pyb])
            h.cp("act", W(y32), py3[:, 0:GC_, 0:tn], [pyb], y32.b)
            yield "c"
            pt, pb = psum.get()
            p3 = v3(pt, 64)
            for hi, (c, h2) in enumerate(heads):
                rows = slice(64 * h2, 64 * h2 + 64)
                h.mm(p3[rows, c, :], tm.t[0:tn, c, 2, rows], tm.t[0:tn, c, 3, rows], True, False, tm.b, [pb])
                h.mm(p3[rows, c, :], tm.t[0:tn, c, 1, rows], vn.t[0:tn, hi, :], False, True, tm.b + vn.b, [pb])
            for c in range(GC_):
                h.stt(ST32[g].t[:, c, :], ST32[g].t[:, c, :], gct.t[:, c:c + 1], p3[:, c, :], ALU.mult, ALU.add,
                      ST32[g].b + gct.b + [pb], ST32[g].b)
            h.cp("act", stb_new.t[:], ST32[g].t[:], ST32[g].b, stb_new.b)
            yield "chunk_done"
            h.cp("pool", W(y16), W(y32), y32.b, y16.b)
            pt, pb = psum.get()
            p3 = v3(pt, 128)
            h.mm(p3[:, 0:GC_, 0:tn], bdones.t[:], W(y16), True, True, bdones.b + y16.b, [pb])
            h.stt(W(yc), p3[:, 0:GC_, 0:tn], -1.0 / 64, W(y32), ALU.mult, ALU.add, [pb] + y32.b, yc.b)
            h.act(W(sq16), W(yc), AF.Square, yc.b, sq16.b)
            pt, pb = psum.get()
            p3 = v3(pt, 128)
            h.mm(p3[:, 0:GC_, 0:tn], bdones.t[:], W(sq16), True, True, bdones.b + sq16.b, [pb])
            h.ts("dve", W(rs), p3[:, 0:GC_, 0:tn], 1.0 / 64, LNX_EPS, ALU.mult, ALU.add, [pb], rs.b)
            h.act(W(rs), W(rs), AF.Sqrt, rs.b, rs.b)
            fw.op("dve", lambda e, a=W(rs): e.reciprocal(out=a, in_=a), reads=rs.b, writes=rs.b)
            h.tt("pool", W(yn), W(yc), W(rs), ALU.mult, yc.b + rs.b, yn.b)
            h.tt("pool", W(yn), W(yn), PW["lg"].t[:, cs_, 0:tn], ALU.mult, yn.b + PW["lg"].b, yn.b)
            h.tt("pool", W(yn), W(yn), PW["lb"].t[:, cs_, 0:tn], ALU.add, yn.b + PW["lb"].b, yn.b)
            h.tt("dve", W(yn), W(yn), W(bv), ALU.add, yn.b + bv.b, yn.b)
            mo = m16[par]
            h.tt("dve", W(mo), W(yn), W(G), ALU.mult, yn.b + G.b, mo.b)
            out_toks.append(h.dma("sp", d["mT_out"][g * 512:(g + 1) * 512, t0:t0 + tn].rearrange("(c p) t -> p c t", p=128),
                                  W(mo), mo.b, []))
            if ti == 0 and g == 0:
                for nm in ("r", "k", "v", "sw", "as", "g"):
                    dbg("in_" + nm, X[nm].t[:], [128, GC_, 128], F32, X[nm].b)
                for nm in ("kk", "kp", "bq", "cs", "G1", "G2", "G3", "G4"):
                    dbg(nm, T_[nm].t[:], [128, GC_, 128], F32, T_[nm].b)
                dbg("ar", ar.t[:], [128, GC_, 2, 128], BF16, ar.b)
                dbg("bk", bk.t[:], [128, GC_, 2, 128], BF16, bk.b)
                dbg("tm", tm.t[:], [128, GC_, 4, 128], BF16, tm.b)
                dbg("mall0", Mall[0].t[:], [128, 4, 128], BF16, Mall[0].b)
                dbg("mall1", Mall[1].t[:], [128, 4, 128], BF16, Mall[1].b)
                dbg("pt0", PT0[0].t[:], [128, 128], BF16, PT0[0].b)
                dbg("tf0", TF[0].t[:], [128, 128], BF16, TF[0].b)
                dbg("tf1", TF[1].t[:], [128, 128], BF16, TF[1].b)
                dbg("xs", Xs.t[:], [128, NH, 64], BF16, Xs.b)
                dbg("ah", Ah.t[:], [128, GC_, 128], BF16, Ah.b)
                dbg("vn", vn.t[:], [128, NH, 64], BF16, vn.b)
                dbg("y32", y32.t[:], [128, GC_, 128], F32, y32.b)
                dbg("yn", yn.t[:], [128, GC_, 128], F32, yn.b)
                dbg("bv", bv.t[:], [128, GC_, 128], F32, bv.b)
                dbg("st32", ST32[0].t[:], [128, GC_, 64], F32, ST32[0].b)
    its = []
    it = 0
    for ti, (t0, tn) in enumerate(TL):
        for g in range(NG):
            its.append((ti, t0, tn, g, it))
            it += 1
    gens = [iter_body(*a) for a in its]
    pairs = [tuple(gens[NG * i:NG * (i + 1)]) for i in range(len(TL))]
    for i, grp_ in enumerate(pairs):
        for gq in grp_:
            assert next(gq) == "prep"
        done = [False] * len(grp_)
        while not all(done):
            for qi_, gq in enumerate(grp_):
                if not done[qi_]:
                    done[qi_] = (next(gq) == "chunk_done")
        for gq in grp_:
            for _ in gq:
                pass
    fw.final_wait("sp", [t for t in fw.E["sp"].dlast if t is not None])


def rwkv_consts_np():
    import ml_dtypes
    bf = ml_dtypes.bfloat16
    ident = np.eye(128, dtype=np.float32)
    s = np.arange(128)[:, None]
    t = np.arange(128)[None, :]
    us = (t > s).astype(np.float32)
    ui = (t >= s).astype(np.float32)
    mask4 = np.stack([us, ui, us, ui], axis=1).reshape(128, 512)
    lsm = (t < s).astype(np.float32)
    bd = ((s // 64) == (t // 64)).astype(np.float32)
    return {"c_ident": ident, "c_mask4": mask4, "c_lsm": lsm, "c_bdones": bd}


def declare_consts(nc):
    I = lambda n, s: nc.dram_tensor(n, s, F32, kind="ExternalInput").ap()
    return {"ident": I("c_ident", [128, 128]), "mask4": I("c_mask4", [128, 512]), "lsm": I("c_lsm", [128, 128]),
            "bdones": I("c_bdones", [128, 128])}


def build_rwkv(vres):
    nc = bass.Bass("TRN2", target_bir_lowering=False)
    d = declare_rwkv(nc, vres)
    consts = declare_consts(nc)
    with ExitStack() as st0:
        fw = FW(nc, st0)
        with ExitStack() as st:
            fw.stack = st
            rwkv_phase1(fw, d, vres)
            print("sbuf remaining p1", nc.sbuf_bytes_remaining)
            fw.emit()
        for e in fw.E.values():
            e.ops = []
        fw.barrier()
        with ExitStack() as st:
            fw.stack = st
            rwkv_phase2(fw, d, consts)
            print("sbuf remaining p2", nc.sbuf_bytes_remaining)
            fw.emit()
        print("rwkv ops", {k: v.n for k, v in fw.E.items()})
    return nc


def rwkv_inputs_np(hT_b, hg, j, inp, vres, gpre, vfirst=None):
    own = slice(hg * CO, (hg + 1) * CO)
    f32 = np.float32
    m = {}
    m["hT"] = hT_b
    m["gpre"] = np.ascontiguousarray(gpre.reshape(16, 128).T)
    m["mixp"] = np.ascontiguousarray(inp["rwkv_mix"][j].reshape(6, 16, 128).transpose(2, 0, 1)).reshape(128, 96)
    wr = inp["rwkv_w_rkv"][j]
    m["w_r"] = tile_w(wr[0][:, own])
    m["w_k"] = tile_w(wr[1][:, own])
    m["w_v"] = tile_w(wr[2][:, own])
    lt = lambda w: np.ascontiguousarray(w.reshape(16, 128, -1).transpose(1, 0, 2)).reshape(128, -1)
    m["w1t"] = lt(inp["rwkv_w1"][j])
    m["a1t"] = lt(inp["rwkv_a1"][j])
    m["g1t"] = tile_w(inp["rwkv_g1"][j])
    m["w2o"] = np.ascontiguousarray(inp["rwkv_w2"][j][:, own])
    m["a2o"] = np.ascontiguousarray(inp["rwkv_a2"][j][:, own])
    m["g2o"] = np.ascontiguousarray(inp["rwkv_g2"][j][:, own].reshape(2, 128, CO).transpose(1, 0, 2)).reshape(128, 2 * CO)
    v0 = inp["rwkv_v0"][j - 1] if vres else np.zeros(D, f32)
    plist = [inp["rwkv_w0"][j], inp["rwkv_a0"][j], v0, inp["rwkv_k_k"][j], inp["rwkv_k_a"][j],
             inp["rwkv_r_k"][j].reshape(-1), inp["rwkv_lnx_g"][j], inp["rwkv_lnx_b"][j]]
    cp = np.stack([p[own].reshape(CC, 128).T for p in plist], axis=1)
    m["chanp"] = np.ascontiguousarray(cp.reshape(128, 8 * CC)).astype(f32)
    if vres:
        m["v1t"] = lt(inp["rwkv_v1"][j - 1])
        m["v2o"] = np.ascontiguousarray(inp["rwkv_v2"][j - 1][:, own])
        m["vfirst"] = vfirst
    m.update(rwkv_consts_np())
    return m


import math

NTOK = 1032
HALO = 16
NEXT = NTOK + HALO


def declare_pool(nc, fused=None):
    P = PFX[0]
    I = lambda n, s, dt=F32: nc.dram_tensor(P + n, s, dt, kind="ExternalInput").ap()
    d = {"gpre": I("gpre", [128, 16]), "invcnt": I("invcnt", [128, 4 * NTOK])}
    if fused is None:
        d["hT"] = I("hT", [D, NEXT])
        d["mT_out"] = nc.dram_tensor(P + "mT_out", [D, NTOK], BF16, kind="ExternalOutput").ap()
    else:
        d["h_own"] = fused["h_own"]
        d["h_halo"] = fused["h_halo"]
        d["sel"] = fused["sel"]
        d["mT_out"] = nc.dram_tensor(P + "mT_pool", [D, NTOK], BF16, kind="Internal").ap()
    return d


def pool_stage(fw, d):
    h = H(fw)
    SLS = slices_of(NEXT)
    ones = Tile(fw, "onesP", [128, 128], BF16)
    fw.op("pool", lambda e: e.memset(ones.t[:], 1.0), writes=ones.b)
    hn = Tile(fw, "hnP", [128, 16, NEXT], F32, 16)
    gpre = Tile(fw, "gpreP", [128, 16], F32)
    invc = Tile(fw, "invc", [128, 4, NTOK], F32)
    sq = [Tile(fw, f"sqP{i}", [128, 512], BF16) for i in range(2)]
    rstd = Tile(fw, "rstdP", [128, NEXT], F32, len(SLS))
    A = [Tile(fw, f"plA{i}", [128, NEXT], F32) for i in range(4)]
    mo = [Tile(fw, f"plM{i}", [128, NTOK], BF16) for i in range(2)]
    ss = [(fw.ps(f"ssP{i}", [128, 512], F32), fw.buf()) for i in range(len(SLS))]
    h.dma("sp", gpre.t[:], d["gpre"], [], gpre.b)
    h.dma("sp", invc.t[:].rearrange("p a b -> p (a b)"), d["invcnt"], [], invc.b)
    if "h_own" in d:
        sel = d["sel"]
        for c in range(16):
            h.dma("sp", hn.t[:, c, HALO:NEXT], d["h_own"][c * 128:(c + 1) * 128, :], [], [hn.b[c]])
            h.dma("sp", hn.t[:, c, 0:HALO], d["h_halo"](c), [], [hn.b[c]])
        h.ts("dve", hn.t[:, :, 0:HALO], hn.t[:, :, 0:HALO], sel.t[:, 2:3], None, ALU.mult, None, hn.b + sel.b, hn.b)
    else:
        for c in range(16):
            h.dma("sp", hn.t[:, c, :], d["hT"][c * 128:(c + 1) * 128, :], [], [hn.b[c]])
    for c in range(16):
        for si, (o, w) in enumerate(SLS):
            q = sq[(c * len(SLS) + si) % 2]
            h.act(q.t[:, 0:w], hn.t[:, c, o:o + w], AF.Square, [hn.b[c]], q.b)
            h.mm(ss[si][0][:, 0:w], ones.t[:], q.t[:, 0:w], c == 0, c == 15, q.b + ones.b, [ss[si][1]])
    for si, (o, w) in enumerate(SLS):
        h.ts("dve", rstd.t[:, o:o + w], ss[si][0][:, 0:w], 1.0 / D, RMS_EPS, ALU.mult, ALU.add, [ss[si][1]], [rstd.b[si]])
        h.act(rstd.t[:, o:o + w], rstd.t[:, o:o + w], AF.Sqrt, [rstd.b[si]], [rstd.b[si]])
        fw.op("dve", lambda e, o=o, w=w: e.reciprocal(out=rstd.t[:, o:o + w], in_=rstd.t[:, o:o + w]),
              reads=[rstd.b[si]], writes=[rstd.b[si]])
    toks = []
    ai = 0
    for c in range(16):
        h.stt(hn.t[:, c, :], hn.t[:, c, :], gpre.t[:, c:c + 1], rstd.t[:], ALU.mult, ALU.mult,
              [hn.b[c]] + gpre.b + rstd.b, [hn.b[c]])
        gi = c // 4
        win = 2 ** (gi + 1)
        src_ap, src_b = hn.t[:, c, :], [hn.b[c]]
        lo = 0
        sh = 1
        lvl = 0
        while sh < win:
            dst = A[ai % 4]
            ai += 1
            lo2 = lo + sh
            eng = "dve" if lvl % 2 == 0 else "pool"
            h.tt(eng, dst.t[:, lo2:NEXT], src_ap[:, lo2:NEXT], src_ap[:, lo2 - sh:NEXT - sh], ALU.add, src_b, dst.b)
            src_ap, src_b = dst.t[:, :], dst.b
            lo = lo2
            sh *= 2
            lvl += 1
        t1 = A[ai % 4]
        ai += 1
        h.tt("pool", t1.t[:, 0:NTOK], src_ap[:, HALO:NEXT], invc.t[:, gi, :], ALU.mult, src_b + invc.b, t1.b)
        m = mo[c % 2]
        h.tt("dve", m.t[:], t1.t[:, 0:NTOK], hn.t[:, c, HALO:NEXT], ALU.subtract, t1.b + [hn.b[c]], m.b)
        toks.append(h.dma("sp", d["mT_out"][c * 128:(c + 1) * 128, :], m.t[:], m.b, []))
    fw.final_wait("sp", toks)


def build_pool():
    nc = bass.Bass("TRN2", target_bir_lowering=False)
    d = declare_pool(nc)
    with ExitStack() as st:
        fw = FW(nc, st)
        pool_stage(fw, d)
        fw.emit()
    return nc


def pool_inputs_np(hT_ext, half, gpre):
    t = np.arange(NTOK) + half * NTOK
    rows = []
    for win in (2, 4, 8, 16):
        cnt = np.minimum(t + 1, win).astype(np.float32)
        rows.append(1.0 / cnt)
    ic = np.stack(rows, 0).astype(np.float32)
    return {"hT": hT_ext, "gpre": np.ascontiguousarray(gpre.reshape(16, 128).T),
            "invcnt": np.ascontiguousarray(np.broadcast_to(ic.reshape(1, 4 * NTOK), (128, 4 * NTOK)))}


HO = 8
SLQ = 344
MOFF = 344
MW = 816


def declare_mla(nc, fused=None):
    P = PFX[0]
    I = lambda n, s, dt=F32: nc.dram_tensor(P + n, s, dt, kind="ExternalInput").ap()
    d = {"gpre": I("gpre", [128, 16]),
         "w_in": I("w_in", [8, 128, 2048]), "w_inpe": I("w_inpe", [2, 128, 16 * 64]),
         "qkg": I("qkg", [128, 8]),
         "w_qn": I("w_qn", [HO, 128, 4 * 128]), "w_qp": I("w_qp", [HO, 2, 128, 4 * 64]),
         "w_kn": I("w_kn", [HO, 128, 4 * 128]), "w_v": I("w_v", [HO, 128, 4 * 128]),
         "cosT": I("cosT", [64, L]), "sinT": I("sinT", [64, L]), "mbig": I("mbig", [128, MW])}
    if fused is None:
        hT = I("hT", [D, L])
        d["hT_fn"] = lambda o, w: hT[:, o:o + w]
        d["oT_out"] = nc.dram_tensor(P + "oT_out", [HO * 128, L], BF16, kind="ExternalOutput").ap()
    else:
        d["hT_fn"] = fused["hT_fn"]
        d["chunked"] = True
        d["mT_out_t"] = nc.dram_tensor(P + "oT_src", [HO * 128, L], F32)
        d["m_dt"] = F32
        d["oT_out"] = d["mT_out_t"].ap()
    return d


def mla_persist(fw, d):
    h = H(fw)
    TL = tiles_of()
    QS = [(i * SLQ, SLQ) for i in range(6)]
    scale = (128 + 64) ** -0.5
    ones = Tile(fw, "onesM", [128, 128], BF16)
    fw.op("pool", lambda e: e.memset(ones.t[:], 1.0), writes=ones.b)
    qkg = Tile(fw, "qkg", [128, 8], F32)
    cq = Tile(fw, "cq", [128, 4, L], BF16, 4)
    ckv = Tile(fw, "ckv", [128, 4, L], BF16, 4)
    kp = Tile(fw, "kpM", [64, L], BF16)
    cosT = Tile(fw, "cosT", [64, L], F32)
    sinT = Tile(fw, "sinT", [64, L], F32)
    mbig = Tile(fw, "mbig", [128, MW], BF16)
    s4 = WSlots(fw, "ws4_", 4, 6)
    psum = PsPool(fw, [f"psM{i}" for i in range(4)])
    pO = [(fw.ps(f"pO{i}", [128, 512], F32), fw.buf()) for i in range(2)]
    pL = [(fw.ps(f"pL{i}", [128, 512], F32), fw.buf()) for i in range(2)]


    return locals()


def mla_phase1(fw, d, P):
    h = P['h']; TL = P['TL']; QS = P['QS']; ones = P['ones']; qkg = P['qkg']; cq = P['cq']; ckv = P['ckv']; kp = P['kp']
    cosT = P['cosT']; sinT = P['sinT']; mbig = P['mbig']; psum = P['psum']; pO = P['pO']; pL = P['pL']; s4 = P['s4']; scale = P['scale']
    hnb = Tile(fw, "hnbM", [128, 16, L], BF16, 16)
    hsl = Tile(fw, "hslM", [128, 16, SLQ], F32)
    gpre = Tile(fw, "gpreM", [128, 16], F32)
    raw = Tile(fw, "rawM", [128, 4, L], F32, 4)
    slots = WSlots(fw, "wsM_", 16, 3)
    sq = [Tile(fw, f"sqM{i}", [128, SLQ], BF16) for i in range(2)]
    rstd = Tile(fw, "rstdM", [128, SLQ], F32)
    tmpf = [Tile(fw, f"tmpM{i}", [128, SLQ], F32) for i in range(3)]
    tmi = [0]
    def gtmp():
        t = tmpf[tmi[0]]
        tmi[0] = (tmi[0] + 1) % 3
        return t
    h.dma("sp", gpre.t[:], d["gpre"], [], gpre.b)
    h.dma("sp", qkg.t[:], d["qkg"], [], qkg.b)
    h.dma("sp", cosT.t[:], d["cosT"], [], cosT.b)
    h.dma("sp", sinT.t[:], d["sinT"], [], sinT.b)
    h.dma("pool", mbig.t[:], d["mbig"], [], mbig.b)
    for s in range(6):
        o = s * SLQ
        if d.get("chunked"):
            for c in range(16):
                h.dma("sp", hsl.t[:, c, :], d["hT_fn"](c, o, SLQ), [], hsl.b)
        else:
            h.dma("sp", hsl.t[:], d["hT_fn"](o, SLQ).rearrange("(c p) t -> p c t", p=128), [], hsl.b)
        for c in range(16):
            q = sq[c % 2]
            h.act(q.t[:], hsl.t[:, c, :], AF.Square, hsl.b, q.b)
            h.mm(pL[0][0][:, 0:SLQ], ones.t[:], q.t[:], c == 0, c == 15, q.b + ones.b, [pL[0][1]])
        h.ts("dve", rstd.t[:], pL[0][0][:, 0:SLQ], 1.0 / D, RMS_EPS, ALU.mult, ALU.add, [pL[0][1]], rstd.b)
        h.act(rstd.t[:], rstd.t[:], AF.Sqrt, rstd.b, rstd.b)
        fw.op("dve", lambda e: e.reciprocal(out=rstd.t[:], in_=rstd.t[:]), reads=rstd.b, writes=rstd.b)
        for c in range(16):
            h.stt(hnb.t[:, c, o:o + SLQ], hsl.t[:, c, :], gpre.t[:, c:c + 1], rstd.t[:], ALU.mult, ALU.mult,
                  hsl.b + gpre.b + rstd.b, [hnb.b[c]])

    def proj16(ws, width, o, pt, pb):
        for kc in range(16):
            h.mm(pt[0:width, 0:SLQ], ws.t[:, kc, 0:width], hnb.t[:, kc, o:o + SLQ], kc == 0, kc == 15,
                 ws.b + [hnb.b[kc]], [pb])

    for which, dst, gofs in ((0, cq, 0), (1, ckv, 4)):
        for c in range(4):
            ws = slots.load(d["w_in"][which * 4 + c])
            for s in range(6):
                o = s * SLQ
                pt, pb = psum.get()
                proj16(ws, 128, o, pt, pb)
                h.cp("act", raw.t[:, c, o:o + SLQ], pt[:, 0:SLQ], [pb], [raw.b[c]])
        for s in range(6):
            o = s * SLQ
            for c in range(4):
                q = sq[c % 2]
                h.act(q.t[:], raw.t[:, c, o:o + SLQ], AF.Square, [raw.b[c]], q.b)
                h.mm(pL[0][0][:, 0:SLQ], ones.t[:], q.t[:], c == 0, c == 3, q.b + ones.b, [pL[0][1]])
            h.ts("dve", rstd.t[:], pL[0][0][:, 0:SLQ], 1.0 / 512, RMS_EPS, ALU.mult, ALU.add, [pL[0][1]], rstd.b)
            h.act(rstd.t[:], rstd.t[:], AF.Sqrt, rstd.b, rstd.b)
            fw.op("dve", lambda e: e.reciprocal(out=rstd.t[:], in_=rstd.t[:]), reads=rstd.b, writes=rstd.b)
            for c in range(4):
                h.stt(dst.t[:, c, o:o + SLQ], raw.t[:, c, o:o + SLQ], qkg.t[:, gofs + c:gofs + c + 1], rstd.t[:],
                      ALU.mult, ALU.mult, [raw.b[c]] + qkg.b + rstd.b, [dst.b[c]])
    wsA = slots.slots[slots.i]; slots.i = (slots.i + 1) % len(slots.slots)
    fw.op("pool", lambda e: e.dma_start(out=wsA.t[:, :, 0:64], in_=d["w_inpe"][0].rearrange("p (k m) -> p k m", m=64)),
          writes=wsA.b, dma=True)
    wsB = slots.slots[slots.i]; slots.i = (slots.i + 1) % len(slots.slots)
    fw.op("pool", lambda e: e.dma_start(out=wsB.t[:, :, 0:64], in_=d["w_inpe"][1].rearrange("p (k m) -> p k m", m=64)),
          writes=wsB.b, dma=True)
    for s in range(6):
        o = s * SLQ
        pa, pab = psum.get()
        proj16(wsA, 64, o, pa, pab)
        pb_, pbb = psum.get()
        proj16(wsB, 64, o, pb_, pbb)
        t1 = gtmp()
        t2 = gtmp()
        h.tt("dve", t1.t[0:64, :], pa[0:64, 0:SLQ], cosT.t[:, o:o + SLQ], ALU.mult, [pab] + cosT.b, t1.b)
        h.tt("dve", t2.t[0:64, :], pb_[0:64, 0:SLQ], sinT.t[:, o:o + SLQ], ALU.mult, [pbb] + sinT.b, t2.b)
        h.tt("pool", kp.t[:, o:o + SLQ], t1.t[0:64, :], t2.t[0:64, :], ALU.add, t1.b + t2.b, kp.b)

    return locals()


def mla_phase2(fw, d, P):
    h = P['h']; TL = P['TL']; QS = P['QS']; ones = P['ones']; qkg = P['qkg']; cq = P['cq']; ckv = P['ckv']; kp = P['kp']
    cosT = P['cosT']; sinT = P['sinT']; mbig = P['mbig']; psum = P['psum']; pO = P['pO']; pL = P['pL']; s4 = P['s4']; scale = P['scale']
    tmpf = [Tile(fw, f"tmpM2_{i}", [128, SLQ], F32) for i in range(3)]
    tmi = [0]

    def gtmp():
        t = tmpf[tmi[0]]
        tmi[0] = (tmi[0] + 1) % 3
        return t
    qn = [Tile(fw, f"qn{i}", [128, L], BF16) for i in range(2)]
    qp = [Tile(fw, f"qp{i}", [64, L], BF16) for i in range(2)]
    kn = [Tile(fw, f"kn{i}", [128, L], BF16) for i in range(2)]
    vt = [Tile(fw, f"vt{i}", [128, 20, 128], BF16) for i in range(2)]
    pts = [Tile(fw, f"ptS{i}", [128, SLQ], BF16) for i in range(3)]
    pti = 0
    rl = Tile(fw, "rlM", [128, SLQ], F32)
    ob = [Tile(fw, f"obM{i}", [128, SLQ], d.get("m_dt", BF16)) for i in range(2)]
    out_toks = []

    def proj4(ws, width, src, o, pt, pb):
        for kc in range(4):
            h.mm(pt[0:width, 0:SLQ], ws.t[:, kc, 0:width], src.t[:, kc, o:o + SLQ], kc == 0, kc == 3,
                 ws.b + [src.b[kc]], [pb])

    for hd in range(HO):
        par = hd % 2
        wqn = s4.load(d["w_qn"][hd])
        wkn = s4.load(d["w_kn"][hd])
        wv = s4.load(d["w_v"][hd])
        wqa = s4.slots[s4.i]; s4.i = (s4.i + 1) % len(s4.slots)
        fw.op("pool", lambda e, wqa=wqa, hd=hd: e.dma_start(out=wqa.t[:, :, 0:64],
                                                         in_=d["w_qp"][hd, 0].rearrange("p (k m) -> p k m", m=64)),
              writes=wqa.b, dma=True)
        wqb = s4.slots[s4.i]; s4.i = (s4.i + 1) % len(s4.slots)
        fw.op("pool", lambda e, wqb=wqb, hd=hd: e.dma_start(out=wqb.t[:, :, 0:64],
                                                         in_=d["w_qp"][hd, 1].rearrange("p (k m) -> p k m", m=64)),
              writes=wqb.b, dma=True)
        for s in range(6):
            o = s * SLQ
            pt, pb = psum.get()
            proj4(wqn, 128, cq, o, pt, pb)
            h.cp("act", qn[par].t[:, o:o + SLQ], pt[:, 0:SLQ], [pb], qn[par].b)
            pt, pb = psum.get()
            proj4(wkn, 128, ckv, o, pt, pb)
            h.cp("act", kn[par].t[:, o:o + SLQ], pt[:, 0:SLQ], [pb], kn[par].b)
            pa, pab = psum.get()
            proj4(wqa, 64, cq, o, pa, pab)
            pb_, pbb = psum.get()
            proj4(wqb, 64, cq, o, pb_, pbb)
            t1 = gtmp()
            t2 = gtmp()
            h.tt("dve", t1.t[0:64, :], pa[0:64, 0:SLQ], cosT.t[:, o:o + SLQ], ALU.mult, [pab] + cosT.b, t1.b)
            h.tt("dve", t2.t[0:64, :], pb_[0:64, 0:SLQ], sinT.t[:, o:o + SLQ], ALU.mult, [pbb] + sinT.b, t2.b)
            h.tt("pool", qp[par].t[:, o:o + SLQ], t1.t[0:64, :], t2.t[0:64, :], ALU.add, t1.b + t2.b, qp[par].b)
        for tb in range(0, len(TL), 4):
            pt, pb = psum.get()
            p3 = pt[:, :].rearrange("p (a b) -> p a b", b=128)
            grp = TL[tb:tb + 4]
            for gi_, (t0, tn) in enumerate(grp):
                for kc in range(4):
                    h.mm(p3[0:tn, gi_, :], ckv.t[:, kc, t0:t0 + tn], wv.t[:, kc, :], kc == 0, kc == 3,
                         [ckv.b[kc]] + wv.b, [pb])
            tn_last = grp[-1][1]
            if tn_last == 128:
                h.cp("act", vt[par].t[:, tb:tb + len(grp), :], p3[:, 0:len(grp), :], [pb], vt[par].b)
            else:
                if len(grp) > 1:
                    h.cp("act", vt[par].t[:, tb:tb + len(grp) - 1, :], p3[:, 0:len(grp) - 1, :], [pb], vt[par].b)
                h.cp("act", vt[par].t[0:tn_last, tb + len(grp) - 1, :], p3[0:tn_last, len(grp) - 1, :], [pb], vt[par].b)
        for qi, (q0, qw) in enumerate(QS):
            po, pob = pO[qi % 2]
            pl, plb = pL[qi % 2]
            kts = [(ki, k0, tn) for ki, (k0, tn) in enumerate(TL) if k0 <= q0 + qw - 1]
            for j_, (ki, k0, tn) in enumerate(kts):
                pt, pb = psum.get()
                h.mm(pt[0:tn, 0:qw], kn[par].t[:, k0:k0 + tn], qn[par].t[:, q0:q0 + qw], True, False,
                     kn[par].b + qn[par].b, [pb])
                h.mm(pt[0:tn, 0:qw], kp.t[:, k0:k0 + tn], qp[par].t[:, q0:q0 + qw], False, True,
                     kp.b + qp[par].b, [pb])
                P = pts[pti % 3]
                pti += 1
                h.act(P.t[0:tn, 0:qw], pt[0:tn, 0:qw], AF.Exp, [pb], P.b, scale=scale)
                if k0 + tn - 1 > q0:
                    mo_ = (q0 - k0) + MOFF
                    h.tt("dve", P.t[0:tn, 0:qw], P.t[0:tn, 0:qw], mbig.t[0:tn, mo_:mo_ + qw], ALU.mult,
                         P.b + mbig.b, P.b)
                first = (j_ == 0)
                last = (j_ == len(kts) - 1)
                h.mm(po[:, 0:qw], vt[par].t[0:tn, ki, :], P.t[0:tn, 0:qw], first, last, vt[par].b + P.b, [pob])
                h.mm(pl[:, 0:qw], ones.t[0:tn, :], P.t[0:tn, 0:qw], first, last, ones.b + P.b, [plb])
            fw.op("dve", lambda e, pl=pl, qw=qw: e.reciprocal(out=rl.t[:, 0:qw], in_=pl[:, 0:qw]), reads=[plb], writes=rl.b)
            o_ = ob[qi % 2]
            h.tt("dve", o_.t[:, 0:qw], po[:, 0:qw], rl.t[:, 0:qw], ALU.mult, [pob] + rl.b, o_.b)
            out_toks.append(h.dma("sp", d["oT_out"][hd * 128:(hd + 1) * 128, q0:q0 + qw], o_.t[:, 0:qw], o_.b, []))
    fw.final_wait("sp", [t for t in fw.E["sp"].dlast if t is not None])


def build_mla():
    nc = bass.Bass("TRN2", target_bir_lowering=False)
    d = declare_mla(nc)
    with ExitStack() as st0:
        fw = FW(nc, st0)
        P = mla_persist(fw, d)
        with ExitStack() as st:
            fw.stack = st
            mla_phase1(fw, d, P)
            print("mla sbuf remaining p1", nc.sbuf_bytes_remaining)
            fw.emit()
        for e in fw.E.values():
            e.ops = []
        fw.barrier()
        with ExitStack() as st:
            fw.stack = st
            mla_phase2(fw, d, P)
            print("mla sbuf remaining p2", nc.sbuf_bytes_remaining)
            fw.emit()
        print("mla ops", {k: v.n for k, v in fw.E.items()})
    return nc


def mla_consts_np():
    inv_freq = 1.0 / (10000.0 ** (np.arange(0, 64, 2, dtype=np.float32) / 64))
    pos = np.arange(L, dtype=np.float32)
    ang = pos[:, None] * inv_freq[None, :]
    cos, sin = np.cos(ang).astype(np.float32), np.sin(ang).astype(np.float32)
    cosT = np.concatenate([cos, cos], 1).T
    sinT = np.concatenate([-sin, sin], 1).T
    k = np.arange(128)[:, None]
    xo = np.arange(MW)[None, :]
    mbig = ((xo - MOFF) >= k).astype(np.float32)
    return {"cosT": np.ascontiguousarray(cosT), "sinT": np.ascontiguousarray(sinT), "mbig": mbig}


def mla_inputs_np(hT_b, hg, inp, gpre, consts):
    m = {"hT": hT_b, "gpre": np.ascontiguousarray(gpre.reshape(16, 128).T)}
    w_in = inp["mla_w_in"][0]
    m["w_in"] = tile_w(w_in[:, 0:1024])
    kpe = w_in[:, 1024:1088]
    sw = np.concatenate([kpe[:, 32:64], kpe[:, 0:32]], 1)
    lt = lambda w: np.ascontiguousarray(w.reshape(-1, 128, w.shape[1]).transpose(1, 0, 2)).reshape(128, -1)
    m["w_inpe"] = np.stack([lt(kpe), lt(sw)], 0)
    m["qkg"] = np.ascontiguousarray(np.concatenate([inp["mla_q_norm"][0].reshape(4, 128).T,
                                                    inp["mla_kv_norm"][0].reshape(4, 128).T], 1))
    wuq = inp["mla_w_uq"][0]
    wukv = inp["mla_w_ukv"][0]
    qn, qp, kn, vv = [], [], [], []
    for hl in range(HO):
        hd = hg * HO + hl
        qn.append(lt(wuq[:, hd * 192:hd * 192 + 128]))
        pe = wuq[:, hd * 192 + 128:hd * 192 + 192]
        pes = np.concatenate([pe[:, 32:64], pe[:, 0:32]], 1)
        qp.append(np.stack([lt(pe), lt(pes)], 0))
        kn.append(lt(wukv[:, hd * 256:hd * 256 + 128]))
        vv.append(lt(wukv[:, hd * 256 + 128:hd * 256 + 256]))
    m["w_qn"] = np.stack(qn, 0)
    m["w_qp"] = np.stack(qp, 0)
    m["w_kn"] = np.stack(kn, 0)
    m["w_v"] = np.stack(vv, 0)
    m.update(consts)
    return m


N_META = 16
PAIRS = [[0, 1], [2, 3], [4, 5], [6, 7]]
_FUSED = {}


def _next_block(fw):
    for e in fw.E.values():
        e.ops = []
    fw.barrier()


def _allgather(fw, src_t, dst_list, relay, rows_per=128):
    bcc = fw.buf()
    for k, dst_t in enumerate(dst_list):
        if rows_per == 128:
            in_ap = src_t.ap()[k * 128:(k + 1) * 128, :]
        else:
            a_ = rows_per // 128
            in_ap = src_t.ap()[k * rows_per:(k + 1) * rows_per, :].rearrange("(p a) c -> p (a c)", a=a_)
        fw.op("pool", lambda e, in_ap=in_ap, dst_t=dst_t: e.collective_compute(
            "AllGather", mybir.AluOpType.bypass, replica_groups=PAIRS,
            ins=[in_ap], outs=[dst_t.ap().opt()]), writes=[bcc], cc=True)
    fw.op("pool", lambda e: e.memset(relay.t[:], 0.0), reads=[bcc], writes=relay.b)
    fw.barrier()


def build_fused():
    nc = bass.Bass("TRN2", target_bir_lowering=False)
    NT = NTOK // 2
    I = lambda n, s, dt=F32: nc.dram_tensor(n, s, dt, kind="ExternalInput").ap()
    h0T = I("h0T", [D, L])
    h0own = I("h0own", [D, NTOK])
    sel_in = I("sel", [128, 4])
    hO = nc.dram_tensor("hO", [D, NTOK], F32, kind="ExternalOutput").ap()
    h_own_t = nc.dram_tensor("h_own", [D, NTOK], F32)
    hg_c = [nc.dram_tensor(f"hg_c{c}", [256, 2 * NTOK], F32) for c in range(8)]

    def hg_rows(c, blk):
        base = blk * 128 + (c % 2) * 64
        return hg_c[c // 2].ap()[base:base + 64, :].rearrange("q (a c) -> (q a) c", a=2)
    h_own = h_own_t.ap()
    consts = declare_consts(nc)

    def hT_gathered(c, o, w):
        blk = o // NTOK
        c0 = o % NTOK
        assert c0 + w <= NTOK
        return hg_rows(c, blk)[:, c0:c0 + w]

    with ExitStack() as st0:
        fw = FW(nc, st0)
        relay = Tile(fw, "relay", [128, 8], F32)
        sel = Tile(fw, "sel", [128, 4], F32)
        fw.op("sp", lambda e: e.dma_start(out=sel.t[:], in_=sel_in), writes=sel.b, dma=True)
        vfirst_ap = None
        for i in range(4):
            kind = i % 3
            PFX[0] = f"L{i}_"
            hfn = (lambda c, o, w: h0T[c * 128:(c + 1) * 128, o:o + w]) if i == 0 else hT_gathered
            m_t = None
            if kind == 0:
                vres = i > 0
                d = declare_rwkv(nc, vres, fused={"hT_fn": hfn, "vfirst": vfirst_ap})
                if not vres:
                    vfirst_ap = d["vscr"]
                with ExitStack() as st:
                    fw.stack = st
                    rwkv_phase1(fw, d, vres)
                    fw.emit()
                _next_block(fw)
                with ExitStack() as st:
                    fw.stack = st
                    rwkv_phase2(fw, d, consts)
                    fw.emit()
                _next_block(fw)
                m_t = d["mT_out_t"]
            elif kind == 1:
                d = declare_mla(nc, fused={"hT_fn": hfn})
                with ExitStack() as stp:
                    fw.stack = stp
                    P = mla_persist(fw, d)
                    with ExitStack() as st:
                        fw.stack = st
                        mla_phase1(fw, d, P)
                        fw.emit()
                    _next_block(fw)
                    with ExitStack() as st:
                        fw.stack = st
                        mla_phase2(fw, d, P)
                        fw.emit()
                _next_block(fw)
                m_t = d["mT_out_t"]
            else:
                d = declare_pool(nc, fused={"h_own": h_own, "h_halo": (lambda c: hg_rows(c, 0)[:, NTOK - HALO:NTOK]), "sel": sel})
                with ExitStack() as st:
                    fw.stack = st
                    pool_stage(fw, d)
                    fw.emit()
                _next_block(fw)
            if m_t is not None:
                mg_c = [nc.dram_tensor(f"mg{i}_c{c}", [256, L], F32) for c in range(CC)]
                _allgather(fw, m_t, mg_c, relay)
            PFX[0] = f"B{i}_"
            kmap_n = 4 if kind == 2 else 16
            wo = I(f"B{i}_wo", [16, 128, kmap_n * 128])
            gains = I(f"B{i}_gains", [128, 64])
            wg = I(f"B{i}_wg", [FC, 128, 2048])
            wu = I(f"B{i}_wu", [FC, 128, 2048])
            wd = I(f"B{i}_wd", [16, 128, FC * 128])
            h_in = h0own if i == 0 else h_own
            h_out = hO if i == 3 else h_own
            with ExitStack() as st:
                fw.stack = st
                sb = StageB(fw, NT, bl_dt=F32)
                toks = []
                for p in range(2):
                    cs = slice(p * NT, (p + 1) * NT)
                    if m_t is not None:
                        mT = ((lambda c, p=p, mg_c=mg_c: mg_c[c % CC].ap()[(c // CC) * 128:(c // CC + 1) * 128, p * NT:(p + 1) * NT]),
                              (lambda c, p=p, mg_c=mg_c: mg_c[c % CC].ap()[(c // CC) * 128:(c // CC + 1) * 128,
                                                                           NTOK + p * NT:NTOK + (p + 1) * NT]), sel)
                    else:
                        mT = d["mT_out"][:, cs]
                    toks += sb.run(mT, h_in[:, cs], h_out[:, cs], wo, kmap_n, gains, wg, wu, wd)
                fw.final_wait("sp", toks)
                fw.emit()
            _next_block(fw)
            if i < 3:
                _allgather(fw, h_own_t, hg_c, relay, rows_per=256)
        with ExitStack() as st:
            fw.stack = st
            fw.emit()
        print("fused ops", {k: v.n for k, v in fw.E.items()})
    return nc


def kernel(**inp):
    inp = {k: np.asarray(v) for k, v in inp.items()}
    x = inp["x"].astype(np.float32)
    B = x.shape[0]
    meta = np.broadcast_to(inp["meta_tokens"][None].astype(np.float32), (B, N_META, D))
    h0 = np.concatenate([meta, x], axis=1)
    ones_d = np.ones(D, np.float32)
    if "nc" not in _FUSED:
        _FUSED["nc"] = build_fused()
    nc = _FUSED["nc"]
    mla_c = mla_consts_np()
    shared = {}
    for i in range(4):
        kind, j = i % 3, i // 3
        if kind == 0:
            wo = tile_w(inp["rwkv_w_o"][j]); osc = ones_d
        elif kind == 1:
            wo = tile_w(inp["mla_w_o"][j]); osc = ones_d
        else:
            wo = np.concatenate([tile_w(inp["pool_w"][j][g]) for g in range(4)], axis=0); osc = inp["pool_scale"][j]
        shared[f"B{i}_wo"] = wo
        shared[f"B{i}_gains"] = gains_pack(inp["norm_mix_post"][i], inp["norm_ffn_pre"][i], inp["norm_ffn_post"][i], osc)
        shared[f"B{i}_wg"] = tile_w(inp["ffn_w_gate"][i])
        shared[f"B{i}_wu"] = tile_w(inp["ffn_w_up"][i])
        shared[f"B{i}_wd"] = tile_w(inp["ffn_w_down"][i])
    shared.update(rwkv_consts_np())
    maps = []
    for core in range(8):
        b, hf = core // 2, core % 2
        m = dict(shared)
        m["h0T"] = np.ascontiguousarray(h0[b].T)
        m["h0own"] = np.ascontiguousarray(h0[b, hf * NTOK:(hf + 1) * NTOK].T)
        selv = np.zeros((128, 4), np.float32)
        selv[:, 0] = 1.0 if hf == 0 else 0.0
        selv[:, 1] = 0.0 if hf == 0 else 1.0
        selv[:, 2] = 0.0 if hf == 0 else 1.0
        m["sel"] = selv
        for i in range(4):
            kind, j = i % 3, i // 3
            gpre = inp["norm_mix_pre"][i]
            if kind == 0:
                mm = rwkv_inputs_np(None, hf, j, inp, j > 0, gpre, None)
                for k_ in ("hT", "vfirst", "c_ident", "c_mask4", "c_lsm", "c_bdones"):
                    mm.pop(k_, None)
            elif kind == 1:
                mm = mla_inputs_np(None, hf, inp, gpre, mla_c)
                mm.pop("hT", None)
            else:
                mm = pool_inputs_np(None, hf, gpre)
                mm.pop("hT", None)
            for k_, v_ in mm.items():
                m[f"L{i}_{k_}"] = v_
        maps.append(m)
    res = run_bass_kernel_spmd(nc, maps, core_ids=list(range(8)))
    out = np.empty((B, L, D), np.float32)
    for core in range(8):
        b, hf = core // 2, core % 2
        out[b, hf * NTOK:(hf + 1) * NTOK] = res.results[core]["hO"].T
    return np.ascontiguousarray(out[:, N_META:])
```

```python
import numpy as np
from contextlib import ExitStack
import concourse.bass as bass
import concourse.mybir as mybir
from concourse.bass_utils import run_bass_kernel_spmd

F32 = mybir.dt.float32
BF16 = mybir.dt.bfloat16
AF = mybir.ActivationFunctionType
ALU = mybir.AluOpType
AX = mybir.AxisListType

NDMA_SEMS = 6


class Buf:
    __slots__ = ("name", "w", "rs")

    def __init__(self, name):
        self.name = name
        self.w = None
        self.rs = []


class Tok:
    __slots__ = ("sem", "val", "clock")

    def __init__(self, sem, val, clock):
        self.sem = sem
        self.val = val
        self.clock = clock


class Eng:
    def __init__(self, name, sem):
        self.name = name
        self.sem = sem
        self.n = 0
        self.clock = {}
        self.ops = []
        self.dsems = []
        self.dcnt = []
        self.dlast = []
        self.di = 0


class FW:
    def __init__(self, nc, stack, same_engine_sync=True):
        self.nc = nc
        self.stack = stack
        self.same = same_engine_sync
        self.E = {}
        for nm in ("pe", "act", "dve", "pool", "sp"):
            sem = stack.enter_context(nc.semaphore("s_" + nm))
            e = Eng(nm, sem)
            self.E[nm] = e
        for nm in ("sp", "act", "pool"):
            e = self.E[nm]
            for i in range(NDMA_SEMS):
                e.dsems.append(stack.enter_context(nc.semaphore(f"d_{nm}{i}")))
                e.dcnt.append(0)
                e.dlast.append(None)
        self.semid = {}
        self.nbuf = 0
        self.uid = 0
        self.ccsem = stack.enter_context(nc.semaphore("cc_sem"))
        self.ccn = 0

    def buf(self, name=None):
        self.nbuf += 1
        return Buf(name or f"b{self.nbuf}")

    def sb(self, name, shape, dt):
        self.uid += 1
        return self.stack.enter_context(self.nc.sbuf_tensor(f"sb{self.uid}_{name}", list(shape), dt))

    def ps(self, name, shape, dt=F32):
        self.uid += 1
        return self.stack.enter_context(self.nc.psum_tensor(f"pp{self.uid}_{name}", list(shape), dt))

    def _need(self, e, tok, waits):
        if tok is None:
            return
        k = id(tok.sem)
        if e.clock.get(k, 0) >= tok.val:
            return
        cur = waits.get(k)
        if cur is None or cur[1] < tok.val:
            waits[k] = (tok.sem, tok.val)
        for kk, vv in tok.clock.items():
            if e.clock.get(kk, 0) < vv:
                e.clock[kk] = vv
        e.clock[k] = max(e.clock.get(k, 0), tok.val)

    def op(self, eng, fn, reads=(), writes=(), dma=False, pe_accum=False, cc=False):
        e = self.E[eng]
        waits = {}
        own = id(e.sem)
        for b in reads:
            self._need(e, b.w, waits)
        for b in writes:
            self._need(e, b.w, waits)
            for t in b.rs:
                self._need(e, t, waits)
        if cc:
            self.ccn += 1
            inc = (self.ccsem, 1)
            tok = Tok(self.ccsem, self.ccn, dict(e.clock))
        elif dma:
            i = e.di
            e.di = (e.di + 1) % len(e.dsems)
            self._need(e, e.dlast[i], waits)
            e.dcnt[i] += 16
            sem = e.dsems[i]
            val = e.dcnt[i]
            inc = (sem, 16)
            tok = Tok(sem, val, dict(e.clock))
            e.dlast[i] = tok
        else:
            e.n += 1
            inc = (e.sem, 1)
            tok = Tok(e.sem, e.n, dict(e.clock))
        wl = []
        for k, (sem, val) in waits.items():
            if k == own and not dma and not cc:
                if not self.same or (eng == "pe"):
                    continue
            wl.append((sem, val))
        e.ops.append((wl, fn, inc))
        for b in reads:
            b.rs.append(tok)
        for b in writes:
            b.w = tok
            b.rs = []
        return tok

    def barrier(self):
        wl = []
        for e in self.E.values():
            if e.n > 0:
                wl.append((e.sem, e.n))
            for i, t in enumerate(e.dlast):
                if t is not None:
                    wl.append((t.sem, t.val))
        for e in self.E.values():
            e.ops.append((list(wl), None, None))
            for sem, val in wl:
                k = id(sem)
                if e.clock.get(k, 0) < val:
                    e.clock[k] = val

    def final_wait(self, eng, toks):
        e = self.E[eng]
        wl = [(t.sem, t.val) for t in toks]
        e.ops.append((wl, None, None))

    def emit(self):
        nc = self.nc
        E = self.E
        with nc.Block() as block:
            def run(e, eng):
                for wl, fn, inc in e.ops:
                    for sem, val in wl:
                        eng.wait_ge(sem, val)
                    if fn is not None:
                        ins = fn(eng)
                        ins.then_inc(inc[0], inc[1])

            @block.tensor
            def _(eng):
                run(E["pe"], eng)

            @block.scalar
            def _(eng):
                run(E["act"], eng)

            @block.vector
            def _(eng):
                run(E["dve"], eng)

            @block.gpsimd
            def _(eng):
                run(E["pool"], eng)

            @block.sync
            def _(eng):
                run(E["sp"], eng)


D = 2048
DC = 16
FH = 5632
FC = 44
RMS_EPS = 1e-6


class Tile:
    def __init__(self, fw, name, shape, dt, nb=1):
        self.t = fw.sb(name, shape, dt)
        self.b = [fw.buf(f"{name}{i}") for i in range(nb)]


class PsPool:
    def __init__(self, fw, names):
        self.tiles = []
        for nm in names:
            t = fw.ps(nm, [128, 512], F32)
            self.tiles.append((t, fw.buf(nm)))
        self.i = 0

    def get(self):
        r = self.tiles[self.i]
        self.i = (self.i + 1) % len(self.tiles)
        return r


class WSlots:
    def __init__(self, fw, name, kc, n, width=128):
        self.fw = fw
        self.kc = kc
        self.slots = [Tile(fw, f"{name}{i}", [128, kc, width], BF16) for i in range(n)]
        self.i = 0

    def load(self, src_ap, kc=None, eng="pool"):
        s = self.slots[self.i]
        self.i = (self.i + 1) % len(self.slots)
        kc = kc or self.kc
        self.fw.op(eng, lambda e: e.dma_start(out=s.t[:, 0:kc, :], in_=src_ap), writes=[s.b[0]], dma=True)
        return s


def slices_of(nt, mx=512):
    n = -(-nt // mx)
    base = -(-nt // n)
    if base % 2:
        base += 1
    out = []
    o = 0
    while o < nt:
        w = min(base, nt - o)
        out.append((o, w))
        o += w
    return out


class StageB:
    def __init__(self, fw, NT, bl_dt=BF16):
        self.fw = fw
        self.NT = NT
        self.sl = slices_of(NT)
        self.h = Tile(fw, "h", [128, DC, NT], F32, DC)
        self.of = Tile(fw, "of", [128, DC, NT], F32, DC)
        self.mn = Tile(fw, "mn", [128, DC, NT], BF16, DC)
        self.act = Tile(fw, "act", [128, FC, NT], BF16, FC)
        self.w16 = WSlots(fw, "w16_", 16, 6)
        self.w44 = WSlots(fw, "w44_", 44, 3)
        self.psum = PsPool(fw, [f"psB{i}" for i in range(6)])
        self.ss = [(fw.ps(f"ssB{i}", [128, 512], F32), fw.buf(f"ssB{i}")) for i in range(2)]
        self.ones = Tile(fw, "onesB", [128, 128], BF16)
        fw.op("pool", lambda e: e.memset(self.ones.t[:], 1.0), writes=[self.ones.b[0]])
        self.sq = [Tile(fw, f"sqB{i}", [128, 512], BF16) for i in range(3)]
        self.sqi = 0
        self.tmp = [Tile(fw, f"tmpB{i}", [128, 512], F32) for i in range(3)]
        self.tmpi = 0
        self.rstd = Tile(fw, "rstdB", [128, NT], F32, len(self.sl))
        self.gains = Tile(fw, "gainsB", [128, 4, DC], F32)
        self.blA = [Tile(fw, f"blA{i}", [128, NT], bl_dt) for i in range(2)]
        self.blB = [Tile(fw, f"blB{i}", [128, NT], bl_dt) for i in range(2)]

    def _sq(self):
        s = self.sq[self.sqi]
        self.sqi = (self.sqi + 1) % len(self.sq)
        return s

    def _tmp(self):
        s = self.tmp[self.tmpi]
        self.tmpi = (self.tmpi + 1) % len(self.tmp)
        return s

    def ss_accum(self, src_ap, src_bufs, si, c, nchunks, eng="act"):
        fw = self.fw
        o, w = self.sl[si]
        s = self._sq()
        fw.op(eng, lambda e: e.activation(out=s.t[:, 0:w], in_=src_ap, func=AF.Square),
              reads=src_bufs, writes=[s.b[0]])
        sst, ssb = self.ss[si]
        fw.op("pe", lambda e: e.matmul(sst[:, 0:w], lhsT=self.ones.t[:], rhs=s.t[:, 0:w],
                                       start=(c == 0), stop=(c == nchunks - 1)),
              reads=[s.b[0], self.ones.b[0]], writes=[ssb])

    def make_rstd(self, si, dim):
        fw = self.fw
        o, w = self.sl[si]
        sst, ssb = self.ss[si]
        r = self.rstd
        fw.op("dve", lambda e: e.tensor_scalar(out=r.t[:, o:o + w], in0=sst[:, 0:w], scalar1=1.0 / dim,
                                               scalar2=RMS_EPS, op0=ALU.mult, op1=ALU.add),
              reads=[ssb], writes=[r.b[si]])
        fw.op("act", lambda e: e.activation(out=r.t[:, o:o + w], in_=r.t[:, o:o + w], func=AF.Sqrt),
              reads=[r.b[si]], writes=[r.b[si]])
        fw.op("dve", lambda e: e.reciprocal(out=r.t[:, o:o + w], in_=r.t[:, o:o + w]),
              reads=[r.b[si]], writes=[r.b[si]])

    def resid_add(self, gi):
        fw = self.fw
        for si, (o, w) in enumerate(self.sl):
            for c in range(DC):
                t = self._tmp()
                fw.op("dve", lambda e, t=t, c=c, o=o, w=w: e.tensor_tensor(
                    out=t.t[:, 0:w], in0=self.of.t[:, c, o:o + w], in1=self.rstd.t[:, o:o + w], op=ALU.mult),
                    reads=[self.of.b[c], self.rstd.b[si]], writes=[t.b[0]])
                fw.op("dve", lambda e, t=t, c=c, o=o, w=w: e.scalar_tensor_tensor(
                    out=self.h.t[:, c, o:o + w], in0=t.t[:, 0:w], scalar=self.gains.t[:, gi, c:c + 1],
                    in1=self.h.t[:, c, o:o + w], op0=ALU.mult, op1=ALU.add),
                    reads=[t.b[0], self.h.b[c]], writes=[self.h.b[c]])

    def run(self, mT, hT_in, hT_out, wo, kmap_n, gains, wg, wu, wd, out_tok=None):
        fw = self.fw
        NT = self.NT
        fw.op("sp", lambda e: e.dma_start(out=self.gains.t[:].rearrange("p a b -> p (a b)"), in_=gains),
              writes=[self.gains.b[0]], dma=True)
        if isinstance(mT, tuple):
            mA, mB, sel = mT
            for c in range(DC):
                ta = self.blA[c % 2]
                tb = self.blB[c % 2]
                fw.op("sp", lambda e, c=c, ta=ta: e.dma_start(out=ta.t[:], in_=mA(c)),
                      writes=ta.b, dma=True)
                fw.op("sp", lambda e, c=c, tb=tb: e.dma_start(out=tb.t[:], in_=mB(c)),
                      writes=tb.b, dma=True)
                fw.op("dve", lambda e, c=c, ta=ta: e.tensor_scalar(out=self.mn.t[:, c, :], in0=ta.t[:], scalar1=sel.t[:, 0:1],
                                                             scalar2=None, op0=ALU.mult),
                      reads=ta.b + sel.b, writes=[self.mn.b[c]])
                fw.op("dve", lambda e, c=c, tb=tb: e.scalar_tensor_tensor(out=self.mn.t[:, c, :], in0=tb.t[:], scalar=sel.t[:, 1:2],
                                                                    in1=self.mn.t[:, c, :], op0=ALU.mult, op1=ALU.add),
                      reads=tb.b + sel.b + [self.mn.b[c]], writes=[self.mn.b[c]])
        else:
            for c in range(DC):
                fw.op("sp", lambda e, c=c: e.dma_start(out=self.mn.t[:, c, :], in_=mT[c * 128:(c + 1) * 128, :]),
                      writes=[self.mn.b[c]], dma=True)
        for c in range(DC):
            fw.op("sp", lambda e, c=c: e.dma_start(out=self.h.t[:, c, :], in_=hT_in[c * 128:(c + 1) * 128, :]),
                  writes=[self.h.b[c]], dma=True)
        for ec in range(DC):
            ws = self.w16.load(wo[ec], kc=kmap_n)
            if kmap_n == DC:
                kcs = list(range(DC))
            else:
                g = ec // kmap_n
                kcs = [g * kmap_n + i for i in range(kmap_n)]
            for si, (o, w) in enumerate(self.sl):
                pt, pb = self.psum.get()
                for i, kc in enumerate(kcs):
                    fw.op("pe", lambda e, pt=pt, ws=ws, i=i, kc=kc, o=o, w=w: e.matmul(
                        pt[:, 0:w], lhsT=ws.t[:, i, :], rhs=self.mn.t[:, kc, o:o + w],
                        start=(i == 0), stop=(i == len(kcs) - 1)),
                        reads=[ws.b[0], self.mn.b[kc]], writes=[pb])
                fw.op("dve", lambda e, pt=pt, ec=ec, o=o, w=w: e.tensor_scalar(
                    out=self.of.t[:, ec, o:o + w], in0=pt[:, 0:w], scalar1=self.gains.t[:, 3, ec:ec + 1],
                    scalar2=None, op0=ALU.mult),
                    reads=[pb, self.gains.b[0]], writes=[self.of.b[ec]])
                self.ss_accum(self.of.t[:, ec, o:o + w], [self.of.b[ec]], si, ec, DC)
        for si in range(len(self.sl)):
            self.make_rstd(si, D)
        self.resid_add(0)
        for c in range(DC):
            for si, (o, w) in enumerate(self.sl):
                self.ss_accum(self.h.t[:, c, o:o + w], [self.h.b[c]], si, c, DC)
        for si in range(len(self.sl)):
            self.make_rstd(si, D)
        for si, (o, w) in enumerate(self.sl):
            for c in range(DC):
                t = self._tmp()
                fw.op("dve", lambda e, t=t, c=c, o=o, w=w: e.tensor_tensor(
                    out=t.t[:, 0:w], in0=self.h.t[:, c, o:o + w], in1=self.rstd.t[:, o:o + w], op=ALU.mult),
                    reads=[self.h.b[c], self.rstd.b[si]], writes=[t.b[0]])
                fw.op("act", lambda e, t=t, c=c, o=o, w=w: e.activation(
                    out=self.mn.t[:, c, o:o + w], in_=t.t[:, 0:w], func=AF.Copy,
                    scale=self.gains.t[:, 1, c:c + 1]),
                    reads=[t.b[0], self.gains.b[0]], writes=[self.mn.b[c]])
        for fc in range(FC):
            wsg = self.w16.load(wg[fc])
            wsu = self.w16.load(wu[fc])
            for si, (o, w) in enumerate(self.sl):
                pg, pgb = self.psum.get()
                pu, pub = self.psum.get()
                for kc in range(DC):
                    fw.op("pe", lambda e, pg=pg, wsg=wsg, kc=kc, o=o, w=w: e.matmul(
                        pg[:, 0:w], lhsT=wsg.t[:, kc, :], rhs=self.mn.t[:, kc, o:o + w],
                        start=(kc == 0), stop=(kc == DC - 1)),
                        reads=[wsg.b[0], self.mn.b[kc]], writes=[pgb])
                for kc in range(DC):
                    fw.op("pe", lambda e, pu=pu, wsu=wsu, kc=kc, o=o, w=w: e.matmul(
                        pu[:, 0:w], lhsT=wsu.t[:, kc, :], rhs=self.mn.t[:, kc, o:o + w],
                        start=(kc == 0), stop=(kc == DC - 1)),
                        reads=[wsu.b[0], self.mn.b[kc]], writes=[pub])
                t = self._tmp()
                fw.op("act", lambda e, t=t, pg=pg, w=w: e.activation(out=t.t[:, 0:w], in_=pg[:, 0:w], func=AF.Silu),
                      reads=[pgb], writes=[t.b[0]])
                fw.op("dve", lambda e, t=t, pu=pu, fc=fc, o=o, w=w: e.tensor_tensor(
                    out=self.act.t[:, fc, o:o + w], in0=pu[:, 0:w], in1=t.t[:, 0:w], op=ALU.mult),
                    reads=[pub, t.b[0]], writes=[self.act.b[fc]])
        for ec in range(DC):
            ws = self.w44.load(wd[ec])
            for si, (o, w) in enumerate(self.sl):
                pt, pb = self.psum.get()
                for kc in range(FC):
                    fw.op("pe", lambda e, pt=pt, ws=ws, kc=kc, o=o, w=w: e.matmul(
                        pt[:, 0:w], lhsT=ws.t[:, kc, :], rhs=self.act.t[:, kc, o:o + w],
                        start=(kc == 0), stop=(kc == FC - 1)),
                        reads=[ws.b[0], self.act.b[kc]], writes=[pb])
                fw.op("dve", lambda e, pt=pt, ec=ec, o=o, w=w: e.tensor_copy(
                    out=self.of.t[:, ec, o:o + w], in_=pt[:, 0:w]),
                    reads=[pb], writes=[self.of.b[ec]])
                self.ss_accum(self.of.t[:, ec, o:o + w], [self.of.b[ec]], si, ec, DC)
        for si in range(len(self.sl)):
            self.make_rstd(si, D)
        self.resid_add(2)
        toks = []
        for c in range(DC):
            toks.append(fw.op("sp", lambda e, c=c: e.dma_start(out=hT_out[c * 128:(c + 1) * 128, :],
                                                               in_=self.h.t[:, c, :]),
                              reads=[self.h.b[c]], dma=True))
        return toks


def tile_w(W, kc_list_per_out=None):
    K, M = W.shape
    KC, MC = K // 128, M // 128
    return np.ascontiguousarray(W.reshape(KC, 128, MC, 128).transpose(2, 1, 0, 3)).reshape(MC, 128, KC * 128)


def gains_pack(g1, g2, g3, osc):
    arr = np.stack([g.reshape(16, 128).T for g in (g1, g2, g3, osc)], axis=1)
    return np.ascontiguousarray(arr.reshape(128, 64)).astype(np.float32)


import math

L = 2064
NTILE = 17
CO = 1024
CC = 8
C0 = -math.exp(-0.5)
LNX_EPS = 64e-5


def tiles_of(Ltot=L):
    out = []
    o = 0
    while o < Ltot:
        out.append((o, min(128, Ltot - o)))
        o += 128
    return out


class H:
    def __init__(self, fw):
        self.fw = fw

    def tt(self, eng, out, in0, in1, op, r, w):
        return self.fw.op(eng, lambda e: e.tensor_tensor(out=out, in0=in0, in1=in1, op=op), reads=r, writes=w)

    def ts(self, eng, out, in0, s1, s2, op0, op1, r, w):
        if op1 is None:
            return self.fw.op(eng, lambda e: e.tensor_scalar(out=out, in0=in0, scalar1=s1, scalar2=None, op0=op0),
                              reads=r, writes=w)
        return self.fw.op(eng, lambda e: e.tensor_scalar(out=out, in0=in0, scalar1=s1, scalar2=s2, op0=op0, op1=op1),
                          reads=r, writes=w)

    def stt(self, out, in0, sc, in1, op0, op1, r, w):
        return self.fw.op("dve", lambda e: e.scalar_tensor_tensor(out=out, in0=in0, scalar=sc, in1=in1, op0=op0,
                                                                  op1=op1), reads=r, writes=w)

    def act(self, out, in_, func, r, w, scale=None, bias=None, eng="act"):
        kw = {}
        if scale is not None:
            kw["scale"] = scale
        if bias is not None:
            kw["bias"] = bias
        return self.fw.op(eng, lambda e: e.activation(out=out, in_=in_, func=func, **kw), reads=r, writes=w)

    def cp(self, eng, out, in_, r, w):
        if eng == "act":
            return self.fw.op("act", lambda e: e.activation(out=out, in_=in_, func=AF.Copy), reads=r, writes=w)
        return self.fw.op(eng, lambda e: e.tensor_copy(out=out, in_=in_), reads=r, writes=w)

    def mm(self, out, lhsT, rhs, start, stop, r, w):
        return self.fw.op("pe", lambda e: e.matmul(out, lhsT=lhsT, rhs=rhs, start=start, stop=stop), reads=r, writes=w)

    def dma(self, eng, out, in_, r, w):
        return self.fw.op(eng, lambda e: e.dma_start(out=out, in_=in_), reads=r, writes=w, dma=True)


PFX = [""]


def declare_rwkv(nc, vres, fused=None):
    d = {}
    P = PFX[0]
    I = lambda n, s, dt=F32: nc.dram_tensor(P + n, s, dt, kind="ExternalInput").ap()
    if fused is None:
        hT = I("hT", [D, L])
        d["hT_fn"] = lambda o, w: hT[:, o:o + w]
    else:
        d["hT_fn"] = fused["hT_fn"]
        d["chunked"] = True
    d["gpre"] = I("gpre", [128, 16])
    d["mixp"] = I("mixp", [128, 6 * 16])
    for n in ("w_r", "w_k", "w_v"):
        d[n] = I(n, [CC, 128, 2048])
    d["w1t"] = I("w1t", [128, 16 * 96])
    d["a1t"] = I("a1t", [128, 16 * 96])
    d["g1t"] = I("g1t", [2, 128, 2048])
    d["w2o"] = I("w2o", [96, CO])
    d["a2o"] = I("a2o", [96, CO])
    d["g2o"] = I("g2o", [128, 2 * CO])
    d["chanp"] = I("chanp", [128, 8 * CC])
    if vres:
        d["v1t"] = I("v1t", [128, 16 * 64])
        d["v2o"] = I("v2o", [64, CO])
        d["vfirst"] = I("vfirst", [CO, L]) if fused is None else fused["vfirst"]
        d["vscr"] = nc.dram_tensor(P + "vscr", [CO, L], F32, kind="Internal").ap()
    else:
        if fused is None:
            d["vscr"] = nc.dram_tensor(P + "vfirst_out", [CO, L], F32, kind="ExternalOutput").ap()
        else:
            d["vscr"] = nc.dram_tensor(P + "vfirst_scr", [CO, L], F32, kind="Internal").ap()
    d["scr"] = [nc.dram_tensor(P + f"scr{i}", [CO, L], F32, kind="Internal").ap() for i in range(5)]
    if fused is None:
        d["mT_out"] = nc.dram_tensor(P + "mT_out", [CO, L], BF16, kind="ExternalOutput").ap()
    else:
        d["mT_out_t"] = nc.dram_tensor(P + "mT_src", [CO, L], F32)
        d["m_dt"] = F32
        d["mT_out"] = d["mT_out_t"].ap()
    return d


def rwkv_phase1(fw, d, vres):
    h = H(fw)
    TH = 688
    SL = 344
    ones = Tile(fw, "ones1", [128, 128], BF16)
    fw.op("pool", lambda e: e.memset(ones.t[:], 1.0), writes=[ones.b[0]])
    hnb = Tile(fw, "hnb", [128, 16, L + 2], BF16, 16 * 3)
    xs = [Tile(fw, f"xs{i}", [128, 16, TH], BF16, 16) for i in range(2)]
    hsl = [Tile(fw, f"hsl{i}", [128, 16, SL], F32) for i in range(1)]
    gpre = Tile(fw, "gpre", [128, 16], F32)
    mixp = Tile(fw, "mixp", [128, 6, 16], F32)
    omm = Tile(fw, "omm", [128, 6, 16], F32)
    chanp = Tile(fw, "chanp1", [128, 8, CC], F32)
    w2 = Tile(fw, "w2", [96, CO], BF16)
    a2 = Tile(fw, "a2", [96, CO], BF16)
    g2 = Tile(fw, "g2", [128, 2, CO], BF16)
    slots = WSlots(fw, "ws1_", 16, 4)
    psum = PsPool(fw, [f"ps1_{i}" for i in range(7)])
    sst, ssb = fw.ps("ss1", [128, 512], F32), fw.buf("ss1")
    sq = [Tile(fw, f"sq1_{i}", [128, SL], BF16) for i in range(2)]
    rstd = Tile(fw, "rstd1", [128, SL], F32)
    zt = Tile(fw, "zt", [128, 2, TH], BF16)
    ev = [Tile(fw, f"ev{i}", [128, SL], F32) for i in range(4)]
    evi = [0]

    def getev():
        t = ev[evi[0]]
        evi[0] = (evi[0] + 1) % len(ev)
        return t

    h.dma("sp", gpre.t[:], d["gpre"], [], gpre.b)
    h.dma("sp", mixp.t[:].rearrange("p a b -> p (a b)"), d["mixp"], [], mixp.b)
    h.dma("sp", chanp.t[:].rearrange("p a b -> p (a b)"), d["chanp"], [], chanp.b)
    h.dma("pool", w2.t[:], d["w2o"], [], w2.b)
    h.dma("pool", a2.t[:], d["a2o"], [], a2.b)
    h.dma("pool", g2.t[:].rearrange("p a b -> p (a b)"), d["g2o"], [], g2.b)
    if vres:
        v2 = Tile(fw, "v2", [64, CO], BF16)
        h.dma("pool", v2.t[:], d["v2o"], [], v2.b)
    h.ts("dve", omm.t[:], mixp.t[:], -1.0, 1.0, ALU.mult, ALU.add, mixp.b, omm.b)
    allhb = hnb.b
    fw.op("pool", lambda e: e.memset(hnb.t[:, :, 0:1], 0.0), writes=[hnb.b[c * 3] for c in range(16)])

    for s in range(6):
        o = s * SL
        third = o // TH
        hs = hsl[0]
        if d.get("chunked"):
            for c in range(16):
                h.dma("sp", hs.t[:, c, :], d["hT_fn"](c, o, SL), [], hs.b)
        else:
            h.dma("sp", hs.t[:], d["hT_fn"](o, SL).rearrange("(c p) t -> p c t", p=128), [], hs.b)
        for c in range(16):
            q = sq[c % 2]
            h.act(q.t[:], hs.t[:, c, :], AF.Square, hs.b, q.b)
            h.mm(sst[:, 0:SL], ones.t[:], q.t[:], c == 0, c == 15, q.b + ones.b, [ssb])
        h.ts("dve", rstd.t[:], sst[:, 0:SL], 1.0 / D, RMS_EPS, ALU.mult, ALU.add, [ssb], rstd.b)
        h.act(rstd.t[:], rstd.t[:], AF.Sqrt, rstd.b, rstd.b)
        fw.op("dve", lambda e: e.reciprocal(out=rstd.t[:], in_=rstd.t[:]), reads=rstd.b, writes=rstd.b)
        for c in range(16):
            h.stt(hnb.t[:, c, 1 + o:1 + o + SL], hs.t[:, c, :], gpre.t[:, c:c + 1], rstd.t[:], ALU.mult, ALU.mult,
                  hs.b + gpre.b + rstd.b, [hnb.b[c * 3 + third]])

    scr = d["scr"]
    plan = [("r", 0), ("w", 1), ("k", 2), ("v", 3), ("a", 4), ("g", 5)]
    xi = 0
    xx = Tile(fw, "xx1", [128, 16, TH], BF16, 16)
    for third in range(3):
        o = third * TH
        for c in range(16):
            rb = [hnb.b[c * 3 + third]] + ([hnb.b[c * 3 + third - 1]] if third > 0 else [])
            h.tt("dve", xx.t[:, c, :], hnb.t[:, c, o:o + TH], hnb.t[:, c, 1 + o:1 + o + TH], ALU.subtract,
                 rb, [xx.b[c]])
        for pname, mi in plan:
            x = xs[xi % 2]
            xi += 1
            for c in range(16):
                h.stt(x.t[:, c, :], xx.t[:, c, :], mixp.t[:, mi, c:c + 1], hnb.t[:, c, 1 + o:1 + o + TH], ALU.mult, ALU.add,
                      [xx.b[c], hnb.b[c * 3 + third]] + mixp.b, [x.b[c]])

            def proj(ws, width, sl_o, pt, pb):
                for kc in range(16):
                    h.mm(pt[0:width, 0:SL], ws.t[:, kc, 0:width], x.t[:, kc, sl_o:sl_o + SL], kc == 0, kc == 15,
                         ws.b + [x.b[kc]], [pb])

            if pname in ("r", "k"):
                dst = scr[0] if pname == "r" else scr[1]
                for cc in range(CC):
                    ws = slots.load(d["w_" + pname][cc])
                    for s2 in range(2):
                        pt, pb = psum.get()
                        proj(ws, 128, s2 * SL, pt, pb)
                        e_ = getev()
                        h.cp("act", e_.t[:], pt[:, 0:SL], [pb], e_.b)
                        t0 = o + s2 * SL
                        h.dma("sp", dst[cc * 128:(cc + 1) * 128, t0:t0 + SL], e_.t[:], e_.b, [])
            elif pname == "v":
                if vres:
                    s_ = slots.slots[slots.i]
                    slots.i = (slots.i + 1) % len(slots.slots)
                    fw.op("pool", lambda e, s_=s_: e.dma_start(out=s_.t[:, :, 0:64],
                                                               in_=d["v1t"].rearrange("p (k m) -> p k m", m=64)),
                          writes=s_.b, dma=True)
                    for s2 in range(2):
                        pt, pb = psum.get()
                        proj(s_, 64, s2 * SL, pt, pb)
                        h.cp("act", zt.t[0:64, 0, s2 * SL:(s2 + 1) * SL], pt[0:64, 0:SL], [pb], zt.b)
                for cc in range(CC):
                    ws = slots.load(d["w_v"][cc])
                    for s2 in range(2):
                        t0 = o + s2 * SL
                        pt, pb = psum.get()
                        proj(ws, 128, s2 * SL, pt, pb)
                        e_ = getev()
                        if not vres:
                            h.cp("act", e_.t[:], pt[:, 0:SL], [pb], e_.b)
                        else:
                            p2, p2b = psum.get()
                            h.mm(p2[:, 0:SL], v2.t[:, cc * 128:(cc + 1) * 128], zt.t[0:64, 0, s2 * SL:(s2 + 1) * SL],
                                 True, True, v2.b + zt.b, [p2b])
                            vm = getev()
                            h.act(vm.t[:], p2[:, 0:SL], AF.Sigmoid, [p2b] + chanp.b, vm.b, bias=chanp.t[:, 2, cc:cc + 1])
                            vf = getev()
                            h.dma("sp", vf.t[:], d["vfirst"][cc * 128:(cc + 1) * 128, t0:t0 + SL], [], vf.b)
                            h.tt("dve", vf.t[:], vf.t[:], pt[:, 0:SL], ALU.subtract, vf.b + [pb], vf.b)
                            h.tt("dve", vf.t[:], vf.t[:], vm.t[:], ALU.mult, vf.b + vm.b, vf.b)
                            h.tt("dve", e_.t[:], vf.t[:], pt[:, 0:SL], ALU.add, vf.b + [pb], e_.b)
                        h.dma("sp", d["vscr"][cc * 128:(cc + 1) * 128, t0:t0 + SL], e_.t[:], e_.b, [])
            elif pname in ("w", "a"):
                s_ = slots.slots[slots.i]
                slots.i = (slots.i + 1) % len(slots.slots)
                src = d["w1t"] if pname == "w" else d["a1t"]
                fw.op("pool", lambda e, s_=s_, src=src: e.dma_start(out=s_.t[:, :, 0:96],
                                                                    in_=src.rearrange("p (k m) -> p k m", m=96)),
                      writes=s_.b, dma=True)
                for s2 in range(2):
                    pt, pb = psum.get()
                    proj(s_, 96, s2 * SL, pt, pb)
                    if pname == "w":
                        h.act(zt.t[0:96, 0, s2 * SL:(s2 + 1) * SL], pt[0:96, 0:SL], AF.Tanh, [pb], zt.b)
                    else:
                        h.cp("act", zt.t[0:96, 0, s2 * SL:(s2 + 1) * SL], pt[0:96, 0:SL], [pb], zt.b)
                w2t = w2 if pname == "w" else a2
                pi = 0 if pname == "w" else 1
                dst = scr[2] if pname == "w" else scr[3]
                for cc in range(CC):
                    for s2 in range(2):
                        t0 = o + s2 * SL
                        pt, pb = psum.get()
                        h.mm(pt[:, 0:SL], w2t.t[:, cc * 128:(cc + 1) * 128], zt.t[0:96, 0, s2 * SL:(s2 + 1) * SL],
                             True, True, w2t.b + zt.b, [pb])
                        e_ = getev()
                        h.act(e_.t[:], pt[:, 0:SL], AF.Sigmoid, [pb] + chanp.b, e_.b, bias=chanp.t[:, pi, cc:cc + 1])
                        h.dma("sp", dst[cc * 128:(cc + 1) * 128, t0:t0 + SL], e_.t[:], e_.b, [])
            elif pname == "g":
                for gc in range(2):
                    ws = slots.load(d["g1t"][gc])
                    for s2 in range(2):
                        pt, pb = psum.get()
                        proj(ws, 128, s2 * SL, pt, pb)
                        h.act(zt.t[:, gc, s2 * SL:(s2 + 1) * SL], pt[:, 0:SL], AF.Sigmoid, [pb], zt.b)
                for cc in range(CC):
                    for s2 in range(2):
                        t0 = o + s2 * SL
                        pt, pb = psum.get()
                        for gc in range(2):
                            h.mm(pt[:, 0:SL], g2.t[:, gc, cc * 128:(cc + 1) * 128], zt.t[:, gc, s2 * SL:(s2 + 1) * SL],
                                 gc == 0, gc == 1, g2.b + zt.b, [pb])
                        e_ = getev()
                        h.cp("act", e_.t[:], pt[:, 0:SL], [pb], e_.b)
                        h.dma("sp", scr[4][cc * 128:(cc + 1) * 128, t0:t0 + SL], e_.t[:], e_.b, [])
    e = fw.E["sp"]
    fw.final_wait("sp", [t for t in e.dlast if t is not None])


DBG_ON = False
DBG_SPECS = {}


def rwkv_phase2(fw, d, consts):
    h = H(fw)
    nc = fw.nc

    def dbg(name, ap, shape, dt, bufs):
        if not DBG_ON:
            return
        t = nc.dram_tensor("dbg_" + name, list(shape), dt, kind="ExternalOutput").ap()
        DBG_SPECS[name] = shape
        h.dma("sp", t, ap, bufs, [])

    TL = tiles_of()
    NG = 2
    GC_ = 4
    ident = Tile(fw, "ident", [128, 128], BF16)
    mask4 = Tile(fw, "mask4", [128, 4, 128], BF16)
    lsm = Tile(fw, "lsm", [128, 128], BF16)
    bdones = Tile(fw, "bdones", [128, 128], BF16)
    onesf = Tile(fw, "onesf", [128, 128], F32)
    h.dma("pool", ident.t[:], consts["ident"], [], ident.b)
    h.dma("pool", mask4.t[:].rearrange("p a b -> p (a b)"), consts["mask4"], [], mask4.b)
    h.dma("pool", lsm.t[:], consts["lsm"], [], lsm.b)
    h.dma("pool", bdones.t[:], consts["bdones"], [], bdones.b)
    fw.op("pool", lambda e: e.memset(onesf.t[:], 1.0), writes=onesf.b)
    chanp = Tile(fw, "chanp2", [128, 8, CC], F32)
    h.dma("sp", chanp.t[:].rearrange("p a b -> p (a b)"), d["chanp"], [], chanp.b)
    PW = {}
    for nm, qi in (("kk", 3), ("ka", 4), ("rk", 5), ("lg", 6), ("lb", 7)):
        t = Tile(fw, "pw_" + nm, [128, CC, 128], F32)
        for cc in range(CC):
            h.ts("pool", t.t[:, cc, :], onesf.t[:], chanp.t[:, qi, cc:cc + 1], None, ALU.mult, None,
                 onesf.b + chanp.b, t.b)
        PW[nm] = t

    def dbl(name, shape, dt):
        return [Tile(fw, f"{name}{i}", shape, dt) for i in range(2)]

    W3 = [128, GC_, 128]
    IN = {nm: dbl("in_" + nm, W3, F32) for nm in ("r", "k", "v", "sw", "as", "g")}
    AR = dbl("AR", [128, GC_, 2, 128], BF16)
    BK = dbl("BK", [128, GC_, 2, 128], BF16)
    FB = dbl("FB", [128, GC_, 3, 128], BF16)
    TM = dbl("TM", [128, GC_, 4, 128], BF16)
    BV = dbl("BV", W3, F32)
    GCt = dbl("GCt", [128, GC_], F32)
    tnames = ["kk0", "kk", "kp", "bq", "cs", "G1", "G2", "G3", "G4", "t1", "rn", "d1"]
    TP = {nm: Tile(fw, "tp_" + nm, W3, F32) for nm in tnames}
    csA = Tile(fw, "csA", [128, GC_, 256], F32)
    csB = Tile(fw, "csB", [128, GC_, 256], F32)
    fw.op("pool", lambda e: e.memset(csA.t[:], 0.0), writes=csA.b)
    fw.op("pool", lambda e: e.memset(csB.t[:], 0.0), writes=csB.b)
    sq16 = Tile(fw, "sq16", W3, BF16)
    rk16 = Tile(fw, "rk16", W3, BF16)
    cl = Tile(fw, "cl", [128, GC_], F32)
    y32G = [Tile(fw, f"y32_{q}", W3, F32) for q in range(2)]
    y16 = Tile(fw, "y16", W3, BF16)
    yc = Tile(fw, "yc", W3, F32)
    rs = Tile(fw, "rs", W3, F32)
    yn = Tile(fw, "yn", W3, F32)
    m16 = dbl("m16", W3, d.get("m_dt", BF16))
    NH = 8
    MallG = [[Tile(fw, f"Mall{q}_{i}", [128, 4, 128], BF16) for i in range(NH)] for q in range(NG)]
    PT0G = [[Tile(fw, f"PT0_{q}_{i}", [128, 128], BF16) for i in range(NH)] for q in range(NG)]
    TtG = [[[Tile(fw, f"T{q}_{i}_{j}", [128, 128], BF16) for j in range(2)] for i in range(NH)] for q in range(NG)]
    PPG = [[[Tile(fw, f"PP{q}_{i}_{j}", [128, 2, 128], BF16) for j in range(2)] for i in range(NH)] for q in range(NG)]
    XsG = [Tile(fw, f"Xs{q}", [128, NH, 64], BF16) for q in range(NG)]
    AhG = [Tile(fw, f"Ah{q}", [128, GC_, 128], BF16) for q in range(NG)]
    vnG = [Tile(fw, f"vn{q}", [128, NH, 64], BF16) for q in range(NG)]
    ST32 = [Tile(fw, f"ST32_{g}", [128, GC_, 64], F32) for g in range(NG)]
    STb = [[Tile(fw, f"STb{g}_{i}", [128, GC_, 64], BF16) for i in range(2)] for g in range(NG)]
    for g in range(NG):
        fw.op("pool", lambda e, g=g: e.memset(ST32[g].t[:], 0.0), writes=ST32[g].b)
        fw.op("pool", lambda e, g=g: e.memset(STb[g][0].t[:], 0.0), writes=STb[g][0].b)
    psum = PsPool(fw, [f"ps2_{i}" for i in range(8)])

    def v3(pt, b):
        return pt[:, :].rearrange("p (a b) -> p a b", b=b)

    scr = d["scr"]
    srcs = {"r": scr[0], "k": scr[1], "sw": scr[2], "as": scr[3], "g": scr[4], "v": d["vscr"]}
    out_toks = []
    def iter_body(ti, t0, tn, g, it):
        if True:
            par = it % 2
            Mall, PT0, Tt, PP, Xs, Ah, vn = MallG[g], PT0G[g], TtG[g], PPG[g], XsG[g], AhG[g], vnG[g]
            y32 = y32G[g]
            cs_ = slice(g * GC_, (g + 1) * GC_)
            X = {}
            for nm in IN:
                tl = IN[nm][par]
                h.dma("sp", tl.t[:, :, 0:tn],
                      srcs[nm][g * 512:(g + 1) * 512, t0:t0 + tn].rearrange("(c p) t -> p c t", p=128), [], tl.b)
                X[nm] = tl
            R, K, V, SW, AS, G = (X[n] for n in ("r", "k", "v", "sw", "as", "g"))
            W = lambda t: t.t[:, :, 0:tn]
            ar, bk, fb, tm, bv, gct = AR[par], BK[par], FB[par], TM[par], BV[par], GCt[par]
            T_ = TP
            h.tt("dve", W(T_["kk0"]), W(K), PW["kk"].t[:, cs_, 0:tn], ALU.mult, K.b + PW["kk"].b, T_["kk0"].b)
            h.act(W(sq16), W(T_["kk0"]), AF.Square, T_["kk0"].b, sq16.b)
            pt, pb = psum.get()
            p3 = v3(pt, 128)
            h.mm(p3[:, :, 0:tn], bdones.t[:], W(sq16), True, True, bdones.b + sq16.b, [pb])
            h.act(W(T_["rn"]), p3[:, :, 0:tn], AF.Sqrt, [pb], T_["rn"].b)
            h.ts("dve", W(T_["rn"]), W(T_["rn"]), 1e-12, None, ALU.max, None, T_["rn"].b, T_["rn"].b)
            fw.op("dve", lambda e, a=W(T_["rn"]): e.reciprocal(out=a, in_=a), reads=T_["rn"].b, writes=T_["rn"].b)
            h.tt("dve", W(T_["kk"]), W(T_["kk0"]), W(T_["rn"]), ALU.mult, T_["kk0"].b + T_["rn"].b, T_["kk"].b)
            h.stt(W(T_["t1"]), W(AS), -1.0, PW["ka"].t[:, cs_, 0:tn], ALU.add, ALU.mult, AS.b + PW["ka"].b, T_["t1"].b)
            h.stt(W(T_["kp"]), W(T_["t1"]), 1.0, W(K), ALU.add, ALU.mult, T_["t1"].b + K.b, T_["kp"].b)
            h.tt("dve", W(T_["bq"]), W(T_["kk"]), W(AS), ALU.mult, T_["kk"].b + AS.b, T_["bq"].b)
            h.cp("act", csA.t[:, :, 128:128 + tn], W(SW), SW.b, csA.b)
            src_, dst_ = csA, csB
            s_ = 1
            lvl = 0
            while s_ < tn:
                eng_ = "dve"
                h.tt(eng_, dst_.t[:, :, 128:128 + tn], src_.t[:, :, 128:128 + tn], src_.t[:, :, 128 - s_:128 - s_ + tn],
                     ALU.add, src_.b, dst_.b)
                src_, dst_ = dst_, src_
                s_ *= 2
                lvl += 1
            h.cp("act", W(T_["cs"]), src_.t[:, :, 128:128 + tn], src_.b, T_["cs"].b)
            h.ts("dve", cl.t[:], T_["cs"].t[:, :, tn - 1], C0, None, ALU.mult, None, T_["cs"].b, cl.b)
            h.act(gct.t[:], cl.t[:], AF.Exp, cl.b, gct.b)
            h.act(W(T_["G1"]), W(T_["cs"]), AF.Exp, T_["cs"].b, T_["G1"].b, scale=C0)
            h.act(W(T_["G2"]), W(T_["cs"]), AF.Exp, T_["cs"].b, T_["G2"].b, scale=-C0)
            h.tt("dve", W(T_["d1"]), W(T_["cs"]), W(SW), ALU.subtract, T_["cs"].b + SW.b, T_["d1"].b)
            h.act(W(T_["G3"]), W(T_["d1"]), AF.Exp, T_["d1"].b, T_["G3"].b, scale=C0)
            for c in range(GC_):
                h.act(T_["G4"].t[:, c, 0:tn], T_["cs"].t[:, c, 0:tn], AF.Exp, T_["cs"].b + cl.b, T_["G4"].b,
                      scale=-C0, bias=cl.t[:, c:c + 1])
            h.stt(ar.t[:, :, 0, 0:tn], W(T_["kk"]), -1.0, W(T_["G3"]), ALU.mult, ALU.mult, T_["kk"].b + T_["G3"].b, ar.b)
            h.tt("dve", ar.t[:, :, 1, 0:tn], W(R), W(T_["G1"]), ALU.mult, R.b + T_["G1"].b, ar.b)
            h.tt("dve", bk.t[:, :, 0, 0:tn], W(T_["bq"]), W(T_["G2"]), ALU.mult, T_["bq"].b + T_["G2"].b, bk.b)
            h.tt("dve", bk.t[:, :, 1, 0:tn], W(T_["kp"]), W(T_["G2"]), ALU.mult, T_["kp"].b + T_["G2"].b, bk.b)
            h.tt("dve", fb.t[:, :, 0, 0:tn], W(T_["bq"]), W(T_["G4"]), ALU.mult, T_["bq"].b + T_["G4"].b, fb.b)
            h.tt("dve", fb.t[:, :, 1, 0:tn], W(T_["kp"]), W(T_["G4"]), ALU.mult, T_["kp"].b + T_["G4"].b, fb.b)
            h.cp("act", fb.t[:, :, 2, 0:tn], W(V), V.b, fb.b)
            h.tt("dve", W(T_["t1"]), W(R), PW["rk"].t[:, cs_, 0:tn], ALU.mult, R.b + PW["rk"].b, T_["t1"].b)
            h.tt("dve", W(rk16), W(T_["t1"]), W(T_["kp"]), ALU.mult, T_["t1"].b + T_["kp"].b, rk16.b)
            pt, pb = psum.get()
            p3 = v3(pt, 128)
            h.mm(p3[:, :, 0:tn], bdones.t[:], W(rk16), True, True, bdones.b + rk16.b, [pb])
            h.tt("dve", W(bv), p3[:, :, 0:tn], W(V), ALU.mult, [pb] + V.b, bv.b)
            for c in range(GC_):
                pt, pb = psum.get()
                p3 = v3(pt, 128)
                srcs_t = [ar.t[:, c, 0, 0:tn], fb.t[:, c, 0, 0:tn], fb.t[:, c, 1, 0:tn], fb.t[:, c, 2, 0:tn]]
                srcb = [ar.b, fb.b, fb.b, fb.b]
                for q in range(4):
                    h.mm(p3[0:tn, q, :], srcs_t[q], ident.t[:], True, True, srcb[q] + ident.b, [pb])
                h.cp("act", tm.t[0:tn, c, :, :], p3[0:tn, :, :], [pb], tm.b)
            yield "prep"
            heads = [(c, h2) for c in range(GC_) for h2 in range(2)]
            K_lv = max(1, math.ceil(math.log2(tn)))
            Tcur = {}
            for hi, (c, h2) in enumerate(heads):
                rows = slice(64 * h2, 64 * h2 + 64)
                pt, pb = psum.get()
                p3 = v3(pt, 128)
                h.mm(p3[0:tn, 0:2, 0:tn], bk.t[rows, c, 0, 0:tn], ar.t[rows, c, :, 0:tn], True, True, bk.b + ar.b, [pb])
                h.mm(p3[0:tn, 2:4, 0:tn], bk.t[rows, c, 1, 0:tn], ar.t[rows, c, :, 0:tn], True, True, bk.b + ar.b, [pb])
                ma = Mall[hi]
                h.tt("dve", ma.t[0:tn, :, 0:tn], p3[0:tn, :, 0:tn], mask4.t[0:tn, :, 0:tn], ALU.mult, [pb] + mask4.b, ma.b)
                pt, pb = psum.get()
                h.mm(pt[0:tn, 0:tn], ar.t[rows, c, 0, 0:tn], bk.t[rows, c, 0, 0:tn], True, True, ar.b + bk.b, [pb])
                h.tt("dve", PT0[hi].t[0:tn, 0:tn], pt[0:tn, 0:tn], lsm.t[0:tn, 0:tn], ALU.mult, [pb] + lsm.b, PT0[hi].b)
                h.tt("dve", Tt[hi][0].t[0:tn, 0:tn], ma.t[0:tn, 0, 0:tn], ident.t[0:tn, 0:tn], ALU.add,
                     ma.b + ident.b, Tt[hi][0].b)
                Tcur[hi] = 0
            yield "c"
            for k in range(1, K_lv):
                last = (k == K_lv - 1)
                for hi in range(len(heads)):
                    if k == 1:
                        P_ap, P_b = Mall[hi].t[0:tn, 0, 0:tn], Mall[hi].b
                        PT_ap, PT_b = PT0[hi].t[0:tn, 0:tn], PT0[hi].b
                    else:
                        pp = PP[hi][(k - 1) % 2]
                        P_ap, P_b = pp.t[0:tn, 0, 0:tn], pp.b
                        PT_ap, PT_b = pp.t[0:tn, 1, 0:tn], pp.b
                    ppn = PP[hi][k % 2]
                    pt, pb = psum.get()
                    p3 = v3(pt, 128)
                    h.mm(p3[0:tn, 1, 0:tn], P_ap, PT_ap, True, True, P_b + PT_b, [pb])
                    if not last:
                        h.mm(p3[0:tn, 0, 0:tn], PT_ap, P_ap, True, True, P_b + PT_b, [pb])
                        h.cp("act", ppn.t[0:tn, :, 0:tn], p3[0:tn, 0:2, 0:tn], [pb], ppn.b)
                    else:
                        h.cp("act", ppn.t[0:tn, 1, 0:tn], p3[0:tn, 1, 0:tn], [pb], ppn.b)
                yield "c"
                for hi in range(len(heads)):
                    ppn = PP[hi][k % 2]
                    tc_ = Tt[hi][Tcur[hi]]
                    tn_ = Tt[hi][1 - Tcur[hi]]
                    pt, pb = psum.get()
                    h.mm(pt[0:tn, 0:tn], ppn.t[0:tn, 1, 0:tn], tc_.t[0:tn, 0:tn], True, True, ppn.b + tc_.b, [pb])
                    h.tt("dve", tn_.t[0:tn, 0:tn], pt[0:tn, 0:tn], tc_.t[0:tn, 0:tn], ALU.add, [pb] + tc_.b, tn_.b)
                    Tcur[hi] = 1 - Tcur[hi]
                yield "c"
            TF = [Tt[hi][Tcur[hi]] for hi in range(len(heads))]
            pt, pb = psum.get()
            p3 = v3(pt, 64)
            for hi, (c, h2) in enumerate(heads):
                cols = slice(64 * h2, 64 * h2 + 64)
                h.mm(p3[0:tn, hi, :], Mall[hi].t[0:tn, 2, 0:tn], tm.t[0:tn, c, 3, cols], True, True, Mall[hi].b + tm.b, [pb])
            h.cp("act", Xs.t[0:tn, :, :], p3[0:tn, 0:NH, :], [pb], Xs.b)
            yield "c"
            pt, pb = psum.get()
            p3 = v3(pt, 128)
            for hi, (c, h2) in enumerate(heads):
                rows = slice(64 * h2, 64 * h2 + 64)
                h.mm(p3[rows, c, 0:tn], tm.t[0:tn, c, 0, rows], TF[hi].t[0:tn, 0:tn], True, True, tm.b + TF[hi].b, [pb])
            h.cp("act", Ah.t[:, :, 0:tn], p3[:, 0:GC_, 0:tn], [pb], Ah.b)
            yield "c"
            stb_old = STb[g][ti % 2]
            stb_new = STb[g][(ti + 1) % 2]
            pt, pb = psum.get()
            p3 = v3(pt, 64)
            for hi, (c, h2) in enumerate(heads):
                rows = slice(64 * h2, 64 * h2 + 64)
                h.mm(p3[0:tn, hi, :], TF[hi].t[0:tn, 0:tn], Xs.t[0:tn, hi, :], True, False, TF[hi].b + Xs.b, [pb])
                h.mm(p3[0:tn, hi, :], Ah.t[rows, c, 0:tn], stb_old.t[rows, c, :], False, True, Ah.b + stb_old.b, [pb])
            h.cp("dve", vn.t[0:tn, :, :], p3[0:tn, 0:NH, :], [pb], vn.b)
            yield "c"
            pty, pyb = psum.get()
            py3 = v3(pty, 128)
            for hi, (c, h2) in enumerate(heads):
                rows = slice(64 * h2, 64 * h2 + 64)
                h.mm(py3[rows, c, 0:tn], stb_old.t[rows, c, :], ar.t[rows, c, 1, 0:tn], True, False, stb_old.b + ar.b, [pyb])
                h.mm(py3[rows, c, 0:tn], tm.t[0:tn, c, 3, rows], Mall[hi].t[0:tn, 3, 0:tn], False, False, tm.b + Mall[hi].b, [pyb])
                h.mm(py3[rows, c, 0:tn], vn.t[0:tn, hi, :], Mall[hi].t[0:tn, 1, 0:tn], False, True, vn.b + Mall[hi].b, [pyb])
            h.cp("act", W(y32), py3[:, 0:GC_, 0:tn], [pyb], y32.b)
            yield "c"
            pt, pb = psum.get()
            p3 = v3(pt, 64)
            for hi, (c, h2) in enumerate(heads):
                rows = slice(64 * h2, 64 * h2 + 64)
                h.mm(p3[rows, c, :], tm.t[0:tn, c, 2, rows], tm.t[0:tn, c, 3, rows], True, False, tm.b, [pb])
                h.mm(p3[rows, c, :], tm.t[0:tn, c, 1, rows], vn.t[0:tn, hi, :], False, True, tm.b + vn.b, [pb])
            for c in range(GC_):
                h.stt(ST32[g].t[:, c, :], ST32[g].t[:, c, :], gct.t[:, c:c + 1], p3[:, c, :], ALU.mult, ALU.add,
                      ST32[g].b + gct.b + [pb], ST32[g].b)
            h.cp("act", stb_new.t[:], ST32[g].t[:], ST32[g].b, stb_new.b)
            yield "chunk_done"
            h.cp("act", W(y16), W(y32), y32.b, y16.b)
            pt, pb = psum.get()
            p3 = v3(pt, 128)
            h.mm(p3[:, 0:GC_, 0:tn], bdones.t[:], W(y16), True, True, bdones.b + y16.b, [pb])
            h.stt(W(yc), p3[:, 0:GC_, 0:tn], -1.0 / 64, W(y32), ALU.mult, ALU.add, [pb] + y32.b, yc.b)
            h.act(W(sq16), W(yc), AF.Square, yc.b, sq16.b)
            pt, pb = psum.get()
            p3 = v3(pt, 128)
            h.mm(p3[:, 0:GC_, 0:tn], bdones.t[:], W(sq16), True, True, bdones.b + sq16.b, [pb])
            h.ts("dve", W(rs), p3[:, 0:GC_, 0:tn], 1.0 / 64, LNX_EPS, ALU.mult, ALU.add, [pb], rs.b)
            h.act(W(rs), W(rs), AF.Sqrt, rs.b, rs.b)
            fw.op("dve", lambda e, a=W(rs): e.reciprocal(out=a, in_=a), reads=rs.b, writes=rs.b)
            h.tt("dve", W(yn), W(yc), W(rs), ALU.mult, yc.b + rs.b, yn.b)
            h.tt("dve", W(yn), W(yn), PW["lg"].t[:, cs_, 0:tn], ALU.mult, yn.b + PW["lg"].b, yn.b)
            h.tt("dve", W(yn), W(yn), PW["lb"].t[:, cs_, 0:tn], ALU.add, yn.b + PW["lb"].b, yn.b)
            h.tt("dve", W(yn), W(yn), W(bv), ALU.add, yn.b + bv.b, yn.b)
            mo = m16[par]
            h.tt("dve", W(mo), W(yn), W(G), ALU.mult, yn.b + G.b, mo.b)
            out_toks.append(h.dma("sp", d["mT_out"][g * 512:(g + 1) * 512, t0:t0 + tn].rearrange("(c p) t -> p c t", p=128),
                                  W(mo), mo.b, []))
            if ti == 0 and g == 0:
                for nm in ("r", "k", "v", "sw", "as", "g"):
                    dbg("in_" + nm, X[nm].t[:], [128, GC_, 128], F32, X[nm].b)
                for nm in ("kk", "kp", "bq", "cs", "G1", "G2", "G3", "G4"):
                    dbg(nm, T_[nm].t[:], [128, GC_, 128], F32, T_[nm].b)
                dbg("ar", ar.t[:], [128, GC_, 2, 128], BF16, ar.b)
                dbg("bk", bk.t[:], [128, GC_, 2, 128], BF16, bk.b)
                dbg("tm", tm.t[:], [128, GC_, 4, 128], BF16, tm.b)
                dbg("mall0", Mall[0].t[:], [128, 4, 128], BF16, Mall[0].b)
                dbg("mall1", Mall[1].t[:], [128, 4, 128], BF16, Mall[1].b)
                dbg("pt0", PT0[0].t[:], [128, 128], BF16, PT0[0].b)
                dbg("tf0", TF[0].t[:], [128, 128], BF16, TF[0].b)
                dbg("tf1", TF[1].t[:], [128, 128], BF16, TF[1].b)
                dbg("xs", Xs.t[:], [128, NH, 64], BF16, Xs.b)
                dbg("ah", Ah.t[:], [128, GC_, 128], BF16, Ah.b)
                dbg("vn", vn.t[:], [128, NH, 64], BF16, vn.b)
                dbg("y32", y32.t[:], [128, GC_, 128], F32, y32.b)
                dbg("yn", yn.t[:], [128, GC_, 128], F32, yn.b)
                dbg("bv", bv.t[:], [128, GC_, 128], F32, bv.b)
                dbg("st32", ST32[0].t[:], [128, GC_, 64], F32, ST32[0].b)
    its = []
    it = 0
    for ti, (t0, tn) in enumerate(TL):
        for g in range(NG):
            its.append((ti, t0, tn, g, it))
            it += 1
    gens = [iter_body(*a) for a in its]
    pairs = [tuple(gens[NG * i:NG * (i + 1)]) for i in range(len(TL))]
    for i, grp_ in enumerate(pairs):
        for gq in grp_:
            assert next(gq) == "prep"
        done = [False] * len(grp_)
        while not all(done):
            for qi_, gq in enumerate(grp_):
                if not done[qi_]:
                    done[qi_] = (next(gq) == "chunk_done")
        for gq in grp_:
            for _ in gq:
                pass
    fw.final_wait("sp", [t for t in fw.E["sp"].dlast if t is not None])


def rwkv_consts_np():
    import ml_dtypes
    bf = ml_dtypes.bfloat16
    ident = np.eye(128, dtype=np.float32)
    s = np.arange(128)[:, None]
    t = np.arange(128)[None, :]
    us = (t > s).astype(np.float32)
    ui = (t >= s).astype(np.float32)
    mask4 = np.stack([us, ui, us, ui], axis=1).reshape(128, 512)
    lsm = (t < s).astype(np.float32)
    bd = ((s // 64) == (t // 64)).astype(np.float32)
    return {"c_ident": ident, "c_mask4": mask4, "c_lsm": lsm, "c_bdones": bd}


def declare_consts(nc):
    I = lambda n, s: nc.dram_tensor(n, s, F32, kind="ExternalInput").ap()
    return {"ident": I("c_ident", [128, 128]), "mask4": I("c_mask4", [128, 512]), "lsm": I("c_lsm", [128, 128]),
            "bdones": I("c_bdones", [128, 128])}


def build_rwkv(vres):
    nc = bass.Bass("TRN2", target_bir_lowering=False)
    d = declare_rwkv(nc, vres)
    consts = declare_consts(nc)
    with ExitStack() as st0:
        fw = FW(nc, st0)
        with ExitStack() as st:
            fw.stack = st
            rwkv_phase1(fw, d, vres)
            print("sbuf remaining p1", nc.sbuf_bytes_remaining)
            fw.emit()
        for e in fw.E.values():
            e.ops = []
        fw.barrier()
        with ExitStack() as st:
            fw.stack = st
            rwkv_phase2(fw, d, consts)
            print("sbuf remaining p2", nc.sbuf_bytes_remaining)
            fw.emit()
        print("rwkv ops", {k: v.n for k, v in fw.E.items()})
    return nc


def rwkv_inputs_np(hT_b, hg, j, inp, vres, gpre, vfirst=None):
    own = slice(hg * CO, (hg + 1) * CO)
    f32 = np.float32
    m = {}
    m["hT"] = hT_b
    m["gpre"] = np.ascontiguousarray(gpre.reshape(16, 128).T)
    m["mixp"] = np.ascontiguousarray(inp["rwkv_mix"][j].reshape(6, 16, 128).transpose(2, 0, 1)).reshape(128, 96)
    wr = inp["rwkv_w_rkv"][j]
    m["w_r"] = tile_w(wr[0][:, own])
    m["w_k"] = tile_w(wr[1][:, own])
    m["w_v"] = tile_w(wr[2][:, own])
    lt = lambda w: np.ascontiguousarray(w.reshape(16, 128, -1).transpose(1, 0, 2)).reshape(128, -1)
    m["w1t"] = lt(inp["rwkv_w1"][j])
    m["a1t"] = lt(inp["rwkv_a1"][j])
    m["g1t"] = tile_w(inp["rwkv_g1"][j])
    m["w2o"] = np.ascontiguousarray(inp["rwkv_w2"][j][:, own])
    m["a2o"] = np.ascontiguousarray(inp["rwkv_a2"][j][:, own])
    m["g2o"] = np.ascontiguousarray(inp["rwkv_g2"][j][:, own].reshape(2, 128, CO).transpose(1, 0, 2)).reshape(128, 2 * CO)
    v0 = inp["rwkv_v0"][j - 1] if vres else np.zeros(D, f32)
    plist = [inp["rwkv_w0"][j], inp["rwkv_a0"][j], v0, inp["rwkv_k_k"][j], inp["rwkv_k_a"][j],
             inp["rwkv_r_k"][j].reshape(-1), inp["rwkv_lnx_g"][j], inp["rwkv_lnx_b"][j]]
    cp = np.stack([p[own].reshape(CC, 128).T for p in plist], axis=1)
    m["chanp"] = np.ascontiguousarray(cp.reshape(128, 8 * CC)).astype(f32)
    if vres:
        m["v1t"] = lt(inp["rwkv_v1"][j - 1])
        m["v2o"] = np.ascontiguousarray(inp["rwkv_v2"][j - 1][:, own])
        m["vfirst"] = vfirst
    m.update(rwkv_consts_np())
    return m


import math

NTOK = 1032
HALO = 16
NEXT = NTOK + HALO


def declare_pool(nc, fused=None):
    P = PFX[0]
    I = lambda n, s, dt=F32: nc.dram_tensor(P + n, s, dt, kind="ExternalInput").ap()
    d = {"gpre": I("gpre", [128, 16]), "invcnt": I("invcnt", [128, 4 * NTOK])}
    if fused is None:
        d["hT"] = I("hT", [D, NEXT])
        d["mT_out"] = nc.dram_tensor(P + "mT_out", [D, NTOK], BF16, kind="ExternalOutput").ap()
    else:
        d["h_own"] = fused["h_own"]
        d["h_halo"] = fused["h_halo"]
        d["sel"] = fused["sel"]
        d["mT_out"] = nc.dram_tensor(P + "mT_pool", [D, NTOK], BF16, kind="Internal").ap()
    return d


def pool_stage(fw, d):
    h = H(fw)
    SLS = slices_of(NEXT)
    ones = Tile(fw, "onesP", [128, 128], BF16)
    fw.op("pool", lambda e: e.memset(ones.t[:], 1.0), writes=ones.b)
    hn = Tile(fw, "hnP", [128, 16, NEXT], F32, 16)
    gpre = Tile(fw, "gpreP", [128, 16], F32)
    invc = Tile(fw, "invc", [128, 4, NTOK], F32)
    sq = [Tile(fw, f"sqP{i}", [128, 512], BF16) for i in range(2)]
    rstd = Tile(fw, "rstdP", [128, NEXT], F32, len(SLS))
    A = [Tile(fw, f"plA{i}", [128, NEXT], F32) for i in range(4)]
    mo = [Tile(fw, f"plM{i}", [128, NTOK], BF16) for i in range(2)]
    ss = [(fw.ps(f"ssP{i}", [128, 512], F32), fw.buf()) for i in range(len(SLS))]
    h.dma("sp", gpre.t[:], d["gpre"], [], gpre.b)
    h.dma("sp", invc.t[:].rearrange("p a b -> p (a b)"), d["invcnt"], [], invc.b)
    if "h_own" in d:
        sel = d["sel"]
        for c in range(16):
            h.dma("sp", hn.t[:, c, HALO:NEXT], d["h_own"][c * 128:(c + 1) * 128, :], [], [hn.b[c]])
            h.dma("sp", hn.t[:, c, 0:HALO], d["h_halo"](c), [], [hn.b[c]])
        h.ts("dve", hn.t[:, :, 0:HALO], hn.t[:, :, 0:HALO], sel.t[:, 2:3], None, ALU.mult, None, hn.b + sel.b, hn.b)
    else:
        for c in range(16):
            h.dma("sp", hn.t[:, c, :], d["hT"][c * 128:(c + 1) * 128, :], [], [hn.b[c]])
    for c in range(16):
        for si, (o, w) in enumerate(SLS):
            q = sq[(c * len(SLS) + si) % 2]
            h.act(q.t[:, 0:w], hn.t[:, c, o:o + w], AF.Square, [hn.b[c]], q.b)
            h.mm(ss[si][0][:, 0:w], ones.t[:], q.t[:, 0:w], c == 0, c == 15, q.b + ones.b, [ss[si][1]])
    for si, (o, w) in enumerate(SLS):
        h.ts("dve", rstd.t[:, o:o + w], ss[si][0][:, 0:w], 1.0 / D, RMS_EPS, ALU.mult, ALU.add, [ss[si][1]], [rstd.b[si]])
        h.act(rstd.t[:, o:o + w], rstd.t[:, o:o + w], AF.Sqrt, [rstd.b[si]], [rstd.b[si]])
        fw.op("dve", lambda e, o=o, w=w: e.reciprocal(out=rstd.t[:, o:o + w], in_=rstd.t[:, o:o + w]),
              reads=[rstd.b[si]], writes=[rstd.b[si]])
    toks = []
    ai = 0
    for c in range(16):
        h.stt(hn.t[:, c, :], hn.t[:, c, :], gpre.t[:, c:c + 1], rstd.t[:], ALU.mult, ALU.mult,
              [hn.b[c]] + gpre.b + rstd.b, [hn.b[c]])
        gi = c // 4
        win = 2 ** (gi + 1)
        src_ap, src_b = hn.t[:, c, :], [hn.b[c]]
        lo = 0
        sh = 1
        lvl = 0
        while sh < win:
            dst = A[ai % 4]
            ai += 1
            lo2 = lo + sh
            eng = "dve" if lvl % 2 == 0 else "pool"
            h.tt(eng, dst.t[:, lo2:NEXT], src_ap[:, lo2:NEXT], src_ap[:, lo2 - sh:NEXT - sh], ALU.add, src_b, dst.b)
            src_ap, src_b = dst.t[:, :], dst.b
            lo = lo2
            sh *= 2
            lvl += 1
        t1 = A[ai % 4]
        ai += 1
        h.tt("pool", t1.t[:, 0:NTOK], src_ap[:, HALO:NEXT], invc.t[:, gi, :], ALU.mult, src_b + invc.b, t1.b)
        m = mo[c % 2]
        h.tt("dve", m.t[:], t1.t[:, 0:NTOK], hn.t[:, c, HALO:NEXT], ALU.subtract, t1.b + [hn.b[c]], m.b)
        toks.append(h.dma("sp", d["mT_out"][c * 128:(c + 1) * 128, :], m.t[:], m.b, []))
    fw.final_wait("sp", toks)


def build_pool():
    nc = bass.Bass("TRN2", target_bir_lowering=False)
    d = declare_pool(nc)
    with ExitStack() as st:
        fw = FW(nc, st)
        pool_stage(fw, d)
        fw.emit()
    return nc


def pool_inputs_np(hT_ext, half, gpre):
    t = np.arange(NTOK) + half * NTOK
    rows = []
    for win in (2, 4, 8, 16):
        cnt = np.minimum(t + 1, win).astype(np.float32)
        rows.append(1.0 / cnt)
    ic = np.stack(rows, 0).astype(np.float32)
    return {"hT": hT_ext, "gpre": np.ascontiguousarray(gpre.reshape(16, 128).T),
            "invcnt": np.ascontiguousarray(np.broadcast_to(ic.reshape(1, 4 * NTOK), (128, 4 * NTOK)))}


HO = 8
SLQ = 344
MOFF = 344
MW = 816


def declare_mla(nc, fused=None):
    P = PFX[0]
    I = lambda n, s, dt=F32: nc.dram_tensor(P + n, s, dt, kind="ExternalInput").ap()
    d = {"gpre": I("gpre", [128, 16]),
         "w_in": I("w_in", [8, 128, 2048]), "w_inpe": I("w_inpe", [2, 128, 16 * 64]),
         "qkg": I("qkg", [128, 8]),
         "w_qn": I("w_qn", [HO, 128, 4 * 128]), "w_qp": I("w_qp", [HO, 2, 128, 4 * 64]),
         "w_kn": I("w_kn", [HO, 128, 4 * 128]), "w_v": I("w_v", [HO, 128, 4 * 128]),
         "cosT": I("cosT", [64, L]), "sinT": I("sinT", [64, L]), "mbig": I("mbig", [128, MW])}
    if fused is None:
        hT = I("hT", [D, L])
        d["hT_fn"] = lambda o, w: hT[:, o:o + w]
        d["oT_out"] = nc.dram_tensor(P + "oT_out", [HO * 128, L], BF16, kind="ExternalOutput").ap()
    else:
        d["hT_fn"] = fused["hT_fn"]
        d["chunked"] = True
        d["mT_out_t"] = nc.dram_tensor(P + "oT_src", [HO * 128, L], F32)
        d["m_dt"] = F32
        d["oT_out"] = d["mT_out_t"].ap()
    return d


def mla_persist(fw, d):
    h = H(fw)
    TL = tiles_of()
    QS = [(i * SLQ, SLQ) for i in range(6)]
    scale = (128 + 64) ** -0.5
    ones = Tile(fw, "onesM", [128, 128], BF16)
    fw.op("pool", lambda e: e.memset(ones.t[:], 1.0), writes=ones.b)
    qkg = Tile(fw, "qkg", [128, 8], F32)
    cq = Tile(fw, "cq", [128, 4, L], BF16, 4)
    ckv = Tile(fw, "ckv", [128, 4, L], BF16, 4)
    kp = Tile(fw, "kpM", [64, L], BF16)
    cosT = Tile(fw, "cosT", [64, L], F32)
    sinT = Tile(fw, "sinT", [64, L], F32)
    mbig = Tile(fw, "mbig", [128, MW], BF16)
    s4 = WSlots(fw, "ws4_", 4, 6)
    psum = PsPool(fw, [f"psM{i}" for i in range(4)])
    pO = [(fw.ps(f"pO{i}", [128, 512], F32), fw.buf()) for i in range(2)]
    pL = [(fw.ps(f"pL{i}", [128, 512], F32), fw.buf()) for i in range(2)]


    return locals()


def mla_phase1(fw, d, P):
    h = P['h']; TL = P['TL']; QS = P['QS']; ones = P['ones']; qkg = P['qkg']; cq = P['cq']; ckv = P['ckv']; kp = P['kp']
    cosT = P['cosT']; sinT = P['sinT']; mbig = P['mbig']; psum = P['psum']; pO = P['pO']; pL = P['pL']; s4 = P['s4']; scale = P['scale']
    hnb = Tile(fw, "hnbM", [128, 16, L], BF16, 16)
    hsl = Tile(fw, "hslM", [128, 16, SLQ], F32)
    gpre = Tile(fw, "gpreM", [128, 16], F32)
    raw = Tile(fw, "rawM", [128, 4, L], F32, 4)
    slots = WSlots(fw, "wsM_", 16, 3)
    sq = [Tile(fw, f"sqM{i}", [128, SLQ], BF16) for i in range(2)]
    rstd = Tile(fw, "rstdM", [128, SLQ], F32)
    tmpf = [Tile(fw, f"tmpM{i}", [128, SLQ], F32) for i in range(3)]
    tmi = [0]
    def gtmp():
        t = tmpf[tmi[0]]
        tmi[0] = (tmi[0] + 1) % 3
        return t
    h.dma("sp", gpre.t[:], d["gpre"], [], gpre.b)
    h.dma("sp", qkg.t[:], d["qkg"], [], qkg.b)
    h.dma("sp", cosT.t[:], d["cosT"], [], cosT.b)
    h.dma("sp", sinT.t[:], d["sinT"], [], sinT.b)
    h.dma("pool", mbig.t[:], d["mbig"], [], mbig.b)
    for s in range(6):
        o = s * SLQ
        if d.get("chunked"):
            for c in range(16):
                h.dma("sp", hsl.t[:, c, :], d["hT_fn"](c, o, SLQ), [], hsl.b)
        else:
            h.dma("sp", hsl.t[:], d["hT_fn"](o, SLQ).rearrange("(c p) t -> p c t", p=128), [], hsl.b)
        for c in range(16):
            q = sq[c % 2]
            h.act(q.t[:], hsl.t[:, c, :], AF.Square, hsl.b, q.b)
            h.mm(pL[0][0][:, 0:SLQ], ones.t[:], q.t[:], c == 0, c == 15, q.b + ones.b, [pL[0][1]])
        h.ts("dve", rstd.t[:], pL[0][0][:, 0:SLQ], 1.0 / D, RMS_EPS, ALU.mult, ALU.add, [pL[0][1]], rstd.b)
        h.act(rstd.t[:], rstd.t[:], AF.Sqrt, rstd.b, rstd.b)
        fw.op("dve", lambda e: e.reciprocal(out=rstd.t[:], in_=rstd.t[:]), reads=rstd.b, writes=rstd.b)
        for c in range(16):
            h.stt(hnb.t[:, c, o:o + SLQ], hsl.t[:, c, :], gpre.t[:, c:c + 1], rstd.t[:], ALU.mult, ALU.mult,
                  hsl.b + gpre.b + rstd.b, [hnb.b[c]])

    def proj16(ws, width, o, pt, pb):
        for kc in range(16):
            h.mm(pt[0:width, 0:SLQ], ws.t[:, kc, 0:width], hnb.t[:, kc, o:o + SLQ], kc == 0, kc == 15,
                 ws.b + [hnb.b[kc]], [pb])

    for which, dst, gofs in ((0, cq, 0), (1, ckv, 4)):
        for c in range(4):
            ws = slots.load(d["w_in"][which * 4 + c])
            for s in range(6):
                o = s * SLQ
                pt, pb = psum.get()
                proj16(ws, 128, o, pt, pb)
                h.cp("act", raw.t[:, c, o:o + SLQ], pt[:, 0:SLQ], [pb], [raw.b[c]])
        for s in range(6):
            o = s * SLQ
            for c in range(4):
                q = sq[c % 2]
                h.act(q.t[:], raw.t[:, c, o:o + SLQ], AF.Square, [raw.b[c]], q.b)
                h.mm(pL[0][0][:, 0:SLQ], ones.t[:], q.t[:], c == 0, c == 3, q.b + ones.b, [pL[0][1]])
            h.ts("dve", rstd.t[:], pL[0][0][:, 0:SLQ], 1.0 / 512, RMS_EPS, ALU.mult, ALU.add, [pL[0][1]], rstd.b)
            h.act(rstd.t[:], rstd.t[:], AF.Sqrt, rstd.b, rstd.b)
            fw.op("dve", lambda e: e.reciprocal(out=rstd.t[:], in_=rstd.t[:]), reads=rstd.b, writes=rstd.b)
            for c in range(4):
                h.stt(dst.t[:, c, o:o + SLQ], raw.t[:, c, o:o + SLQ], qkg.t[:, gofs + c:gofs + c + 1], rstd.t[:],
                      ALU.mult, ALU.mult, [raw.b[c]] + qkg.b + rstd.b, [dst.b[c]])
    wsA = slots.slots[slots.i]; slots.i = (slots.i + 1) % len(slots.slots)
    fw.op("pool", lambda e: e.dma_start(out=wsA.t[:, :, 0:64], in_=d["w_inpe"][0].rearrange("p (k m) -> p k m", m=64)),
          writes=wsA.b, dma=True)
    wsB = slots.slots[slots.i]; slots.i = (slots.i + 1) % len(slots.slots)
    fw.op("pool", lambda e: e.dma_start(out=wsB.t[:, :, 0:64], in_=d["w_inpe"][1].rearrange("p (k m) -> p k m", m=64)),
          writes=wsB.b, dma=True)
    for s in range(6):
        o = s * SLQ
        pa, pab = psum.get()
        proj16(wsA, 64, o, pa, pab)
        pb_, pbb = psum.get()
        proj16(wsB, 64, o, pb_, pbb)
        t1 = gtmp()
        t2 = gtmp()
        h.tt("dve", t1.t[0:64, :], pa[0:64, 0:SLQ], cosT.t[:, o:o + SLQ], ALU.mult, [pab] + cosT.b, t1.b)
        h.tt("dve", t2.t[0:64, :], pb_[0:64, 0:SLQ], sinT.t[:, o:o + SLQ], ALU.mult, [pbb] + sinT.b, t2.b)
        h.tt("pool", kp.t[:, o:o + SLQ], t1.t[0:64, :], t2.t[0:64, :], ALU.add, t1.b + t2.b, kp.b)

    return locals()


def mla_phase2(fw, d, P):
    h = P['h']; TL = P['TL']; QS = P['QS']; ones = P['ones']; qkg = P['qkg']; cq = P['cq']; ckv = P['ckv']; kp = P['kp']
    cosT = P['cosT']; sinT = P['sinT']; mbig = P['mbig']; psum = P['psum']; pO = P['pO']; pL = P['pL']; s4 = P['s4']; scale = P['scale']
    tmpf = [Tile(fw, f"tmpM2_{i}", [128, SLQ], F32) for i in range(3)]
    tmi = [0]

    def gtmp():
        t = tmpf[tmi[0]]
        tmi[0] = (tmi[0] + 1) % 3
        return t
    qn = [Tile(fw, f"qn{i}", [128, L], BF16) for i in range(2)]
    qp = [Tile(fw, f"qp{i}", [64, L], BF16) for i in range(2)]
    kn = [Tile(fw, f"kn{i}", [128, L], BF16) for i in range(2)]
    vt = [Tile(fw, f"vt{i}", [128, 20, 128], BF16) for i in range(2)]
    pts = [Tile(fw, f"ptS{i}", [128, SLQ], BF16) for i in range(3)]
    pti = 0
    rl = Tile(fw, "rlM", [128, SLQ], F32)
    ob = [Tile(fw, f"obM{i}", [128, SLQ], d.get("m_dt", BF16)) for i in range(2)]
    out_toks = []

    def proj4(ws, width, src, o, pt, pb):
        for kc in range(4):
            h.mm(pt[0:width, 0:SLQ], ws.t[:, kc, 0:width], src.t[:, kc, o:o + SLQ], kc == 0, kc == 3,
                 ws.b + [src.b[kc]], [pb])

    for hd in range(HO):
        par = hd % 2
        wqn = s4.load(d["w_qn"][hd])
        wkn = s4.load(d["w_kn"][hd])
        wv = s4.load(d["w_v"][hd])
        wqa = s4.slots[s4.i]; s4.i = (s4.i + 1) % len(s4.slots)
        fw.op("pool", lambda e, wqa=wqa, hd=hd: e.dma_start(out=wqa.t[:, :, 0:64],
                                                         in_=d["w_qp"][hd, 0].rearrange("p (k m) -> p k m", m=64)),
              writes=wqa.b, dma=True)
        wqb = s4.slots[s4.i]; s4.i = (s4.i + 1) % len(s4.slots)
        fw.op("pool", lambda e, wqb=wqb, hd=hd: e.dma_start(out=wqb.t[:, :, 0:64],
                                                         in_=d["w_qp"][hd, 1].rearrange("p (k m) -> p k m", m=64)),
              writes=wqb.b, dma=True)
        for s in range(6):
            o = s * SLQ
            pt, pb = psum.get()
            proj4(wqn, 128, cq, o, pt, pb)
            h.cp("act", qn[par].t[:, o:o + SLQ], pt[:, 0:SLQ], [pb], qn[par].b)
            pt, pb = psum.get()
            proj4(wkn, 128, ckv, o, pt, pb)
            h.cp("act", kn[par].t[:, o:o + SLQ], pt[:, 0:SLQ], [pb], kn[par].b)
            pa, pab = psum.get()
            proj4(wqa, 64, cq, o, pa, pab)
            pb_, pbb = psum.get()
            proj4(wqb, 64, cq, o, pb_, pbb)
            t1 = gtmp()
            t2 = gtmp()
            h.tt("dve", t1.t[0:64, :], pa[0:64, 0:SLQ], cosT.t[:, o:o + SLQ], ALU.mult, [pab] + cosT.b, t1.b)
            h.tt("dve", t2.t[0:64, :], pb_[0:64, 0:SLQ], sinT.t[:, o:o + SLQ], ALU.mult, [pbb] + sinT.b, t2.b)
            h.tt("pool", qp[par].t[:, o:o + SLQ], t1.t[0:64, :], t2.t[0:64, :], ALU.add, t1.b + t2.b, qp[par].b)
        for tb in range(0, len(TL), 4):
            pt, pb = psum.get()
            p3 = pt[:, :].rearrange("p (a b) -> p a b", b=128)
            grp = TL[tb:tb + 4]
            for gi_, (t0, tn) in enumerate(grp):
                for kc in range(4):
                    h.mm(p3[0:tn, gi_, :], ckv.t[:, kc, t0:t0 + tn], wv.t[:, kc, :], kc == 0, kc == 3,
                         [ckv.b[kc]] + wv.b, [pb])
            tn_last = grp[-1][1]
            if tn_last == 128:
                h.cp("act", vt[par].t[:, tb:tb + len(grp), :], p3[:, 0:len(grp), :], [pb], vt[par].b)
            else:
                if len(grp) > 1:
                    h.cp("act", vt[par].t[:, tb:tb + len(grp) - 1, :], p3[:, 0:len(grp) - 1, :], [pb], vt[par].b)
                h.cp("act", vt[par].t[0:tn_last, tb + len(grp) - 1, :], p3[0:tn_last, len(grp) - 1, :], [pb], vt[par].b)
        for qi, (q0, qw) in enumerate(QS):
            po, pob = pO[qi % 2]
            pl, plb = pL[qi % 2]
            kts = [(ki, k0, tn) for ki, (k0, tn) in enumerate(TL) if k0 <= q0 + qw - 1]
            for j_, (ki, k0, tn) in enumerate(kts):
                pt, pb = psum.get()
                h.mm(pt[0:tn, 0:qw], kn[par].t[:, k0:k0 + tn], qn[par].t[:, q0:q0 + qw], True, False,
                     kn[par].b + qn[par].b, [pb])
                h.mm(pt[0:tn, 0:qw], kp.t[:, k0:k0 + tn], qp[par].t[:, q0:q0 + qw], False, True,
                     kp.b + qp[par].b, [pb])
                P = pts[pti % 3]
                pti += 1
                h.act(P.t[0:tn, 0:qw], pt[0:tn, 0:qw], AF.Exp, [pb], P.b, scale=scale)
                if k0 + tn - 1 > q0:
                    mo_ = (q0 - k0) + MOFF
                    h.tt("dve", P.t[0:tn, 0:qw], P.t[0:tn, 0:qw], mbig.t[0:tn, mo_:mo_ + qw], ALU.mult,
                         P.b + mbig.b, P.b)
                first = (j_ == 0)
                last = (j_ == len(kts) - 1)
                h.mm(po[:, 0:qw], vt[par].t[0:tn, ki, :], P.t[0:tn, 0:qw], first, last, vt[par].b + P.b, [pob])
                h.mm(pl[:, 0:qw], ones.t[0:tn, :], P.t[0:tn, 0:qw], first, last, ones.b + P.b, [plb])
            fw.op("dve", lambda e, pl=pl, qw=qw: e.reciprocal(out=rl.t[:, 0:qw], in_=pl[:, 0:qw]), reads=[plb], writes=rl.b)
            o_ = ob[qi % 2]
            h.tt("dve", o_.t[:, 0:qw], po[:, 0:qw], rl.t[:, 0:qw], ALU.mult, [pob] + rl.b, o_.b)
            out_toks.append(h.dma("sp", d["oT_out"][hd * 128:(hd + 1) * 128, q0:q0 + qw], o_.t[:, 0:qw], o_.b, []))
    fw.final_wait("sp", [t for t in fw.E["sp"].dlast if t is not None])


def build_mla():
    nc = bass.Bass("TRN2", target_bir_lowering=False)
    d = declare_mla(nc)
    with ExitStack() as st0:
        fw = FW(nc, st0)
        P = mla_persist(fw, d)
        with ExitStack() as st:
            fw.stack = st
            mla_phase1(fw, d, P)
            print("mla sbuf remaining p1", nc.sbuf_bytes_remaining)
            fw.emit()
        for e in fw.E.values():
            e.ops = []
        fw.barrier()
        with ExitStack() as st:
            fw.stack = st
            mla_phase2(fw, d, P)
            print("mla sbuf remaining p2", nc.sbuf_bytes_remaining)
            fw.emit()
        print("mla ops", {k: v.n for k, v in fw.E.items()})
    return nc


def mla_consts_np():
    inv_freq = 1.0 / (10000.0 ** (np.arange(0, 64, 2, dtype=np.float32) / 64))
    pos = np.arange(L, dtype=np.float32)
    ang = pos[:, None] * inv_freq[None, :]
    cos, sin = np.cos(ang).astype(np.float32), np.sin(ang).astype(np.float32)
    cosT = np.concatenate([cos, cos], 1).T
    sinT = np.concatenate([-sin, sin], 1).T
    k = np.arange(128)[:, None]
    xo = np.arange(MW)[None, :]
    mbig = ((xo - MOFF) >= k).astype(np.float32)
    return {"cosT": np.ascontiguousarray(cosT), "sinT": np.ascontiguousarray(sinT), "mbig": mbig}


def mla_inputs_np(hT_b, hg, inp, gpre, consts):
    m = {"hT": hT_b, "gpre": np.ascontiguousarray(gpre.reshape(16, 128).T)}
    w_in = inp["mla_w_in"][0]
    m["w_in"] = tile_w(w_in[:, 0:1024])
    kpe = w_in[:, 1024:1088]
    sw = np.concatenate([kpe[:, 32:64], kpe[:, 0:32]], 1)
    lt = lambda w: np.ascontiguousarray(w.reshape(-1, 128, w.shape[1]).transpose(1, 0, 2)).reshape(128, -1)
    m["w_inpe"] = np.stack([lt(kpe), lt(sw)], 0)
    m["qkg"] = np.ascontiguousarray(np.concatenate([inp["mla_q_norm"][0].reshape(4, 128).T,
                                                    inp["mla_kv_norm"][0].reshape(4, 128).T], 1))
    wuq = inp["mla_w_uq"][0]
    wukv = inp["mla_w_ukv"][0]
    qn, qp, kn, vv = [], [], [], []
    for hl in range(HO):
        hd = hg * HO + hl
        qn.append(lt(wuq[:, hd * 192:hd * 192 + 128]))
        pe = wuq[:, hd * 192 + 128:hd * 192 + 192]
        pes = np.concatenate([pe[:, 32:64], pe[:, 0:32]], 1)
        qp.append(np.stack([lt(pe), lt(pes)], 0))
        kn.append(lt(wukv[:, hd * 256:hd * 256 + 128]))
        vv.append(lt(wukv[:, hd * 256 + 128:hd * 256 + 256]))
    m["w_qn"] = np.stack(qn, 0)
    m["w_qp"] = np.stack(qp, 0)
    m["w_kn"] = np.stack(kn, 0)
    m["w_v"] = np.stack(vv, 0)
    m.update(consts)
    return m


N_META = 16
PAIRS = [[0, 1], [2, 3], [4, 5], [6, 7]]
_FUSED = {}


def _next_block(fw):
    for e in fw.E.values():
        e.ops = []
    fw.barrier()


def _allgather(fw, src_t, dst_list, relay):
    bcc = fw.buf()
    for k, dst_t in enumerate(dst_list):
        fw.op("pool", lambda e, k=k, dst_t=dst_t: e.collective_compute(
            "AllGather", mybir.AluOpType.bypass, replica_groups=PAIRS,
            ins=[src_t.ap()[k * 128:(k + 1) * 128, :]], outs=[dst_t.ap().opt()]), writes=[bcc], cc=True)
    fw.op("pool", lambda e: e.memset(relay.t[:], 0.0), reads=[bcc], writes=relay.b)
    fw.barrier()


def build_fused():
    nc = bass.Bass("TRN2", target_bir_lowering=False)
    NT = NTOK // 2
    I = lambda n, s, dt=F32: nc.dram_tensor(n, s, dt, kind="ExternalInput").ap()
    h0T = I("h0T", [D, L])
    h0own = I("h0own", [D, NTOK])
    sel_in = I("sel", [128, 4])
    hO = nc.dram_tensor("hO", [D, NTOK], F32, kind="ExternalOutput").ap()
    h_own_t = nc.dram_tensor("h_own", [D, NTOK], F32)
    hg_c = [nc.dram_tensor(f"hg_c{c}", [256, NTOK], F32) for c in range(16)]
    h_own = h_own_t.ap()
    consts = declare_consts(nc)

    def hT_gathered(c, o, w):
        blk = o // NTOK
        c0 = o % NTOK
        assert c0 + w <= NTOK
        return hg_c[c].ap()[blk * 128:(blk + 1) * 128, c0:c0 + w]

    with ExitStack() as st0:
        fw = FW(nc, st0)
        relay = Tile(fw, "relay", [128, 8], F32)
        sel = Tile(fw, "sel", [128, 4], F32)
        fw.op("sp", lambda e: e.dma_start(out=sel.t[:], in_=sel_in), writes=sel.b, dma=True)
        vfirst_ap = None
        for i in range(4):
            kind = i % 3
            PFX[0] = f"L{i}_"
            hfn = (lambda c, o, w: h0T[c * 128:(c + 1) * 128, o:o + w]) if i == 0 else hT_gathered
            m_t = None
            if kind == 0:
                vres = i > 0
                d = declare_rwkv(nc, vres, fused={"hT_fn": hfn, "vfirst": vfirst_ap})
                if not vres:
                    vfirst_ap = d["vscr"]
                with ExitStack() as st:
                    fw.stack = st
                    rwkv_phase1(fw, d, vres)
                    fw.emit()
                _next_block(fw)
                with ExitStack() as st:
                    fw.stack = st
                    rwkv_phase2(fw, d, consts)
                    fw.emit()
                _next_block(fw)
                m_t = d["mT_out_t"]
            elif kind == 1:
                d = declare_mla(nc, fused={"hT_fn": hfn})
                with ExitStack() as stp:
                    fw.stack = stp
                    P = mla_persist(fw, d)
                    with ExitStack() as st:
                        fw.stack = st
                        mla_phase1(fw, d, P)
                        fw.emit()
                    _next_block(fw)
                    with ExitStack() as st:
                        fw.stack = st
                        mla_phase2(fw, d, P)
                        fw.emit()
                _next_block(fw)
                m_t = d["mT_out_t"]
            else:
                d = declare_pool(nc, fused={"h_own": h_own, "h_halo": (lambda c: hg_c[c].ap()[0:128, NTOK - HALO:NTOK]), "sel": sel})
                with ExitStack() as st:
                    fw.stack = st
                    pool_stage(fw, d)
                    fw.emit()
                _next_block(fw)
            if m_t is not None:
                mg_c = [nc.dram_tensor(f"mg{i}_c{c}", [256, L], F32) for c in range(CC)]
                _allgather(fw, m_t, mg_c, relay)
            PFX[0] = f"B{i}_"
            kmap_n = 4 if kind == 2 else 16
            wo = I(f"B{i}_wo", [16, 128, kmap_n * 128])
            gains = I(f"B{i}_gains", [128, 64])
            wg = I(f"B{i}_wg", [FC, 128, 2048])
            wu = I(f"B{i}_wu", [FC, 128, 2048])
            wd = I(f"B{i}_wd", [16, 128, FC * 128])
            h_in = h0own if i == 0 else h_own
            h_out = hO if i == 3 else h_own
            with ExitStack() as st:
                fw.stack = st
                sb = StageB(fw, NT, bl_dt=F32)
                toks = []
                for p in range(2):
                    cs = slice(p * NT, (p + 1) * NT)
                    if m_t is not None:
                        mT = ((lambda c, p=p, mg_c=mg_c: mg_c[c % CC].ap()[(c // CC) * 128:(c // CC + 1) * 128, p * NT:(p + 1) * NT]),
                              (lambda c, p=p, mg_c=mg_c: mg_c[c % CC].ap()[(c // CC) * 128:(c // CC + 1) * 128,
                                                                           NTOK + p * NT:NTOK + (p + 1) * NT]), sel)
                    else:
                        mT = d["mT_out"][:, cs]
                    toks += sb.run(mT, h_in[:, cs], h_out[:, cs], wo, kmap_n, gains, wg, wu, wd)
                fw.final_wait("sp", toks)
                fw.emit()
            _next_block(fw)
            if i < 3:
                _allgather(fw, h_own_t, hg_c, relay)
        with ExitStack() as st:
            fw.stack = st
            fw.emit()
        print("fused ops", {k: v.n for k, v in fw.E.items()})
    return nc


def kernel(**inp):
    inp = {k: np.asarray(v) for k, v in inp.items()}
    x = inp["x"].astype(np.float32)
    B = x.shape[0]
    meta = np.broadcast_to(inp["meta_tokens"][None].astype(np.float32), (B, N_META, D))
    h0 = np.concatenate([meta, x], axis=1)
    ones_d = np.ones(D, np.float32)
    if "nc" not in _FUSED:
        _FUSED["nc"] = build_fused()
    nc = _FUSED["nc"]
    mla_c = mla_consts_np()
    shared = {}
    for i in range(4):
        kind, j = i % 3, i // 3
        if kind == 0:
            wo = tile_w(inp["rwkv_w_o"][j]); osc = ones_d
        elif kind == 1:
            wo = tile_w(inp["mla_w_o"][j]); osc = ones_d
        else:
            wo = np.concatenate([tile_w(inp["pool_w"][j][g]) for g in range(4)], axis=0); osc = inp["pool_scale"][j]
        shared[f"B{i}_wo"] = wo
        shared[f"B{i}_gains"] = gains_pack(inp["norm_mix_post"][i], inp["norm_ffn_pre"][i], inp["norm_ffn_post"][i], osc)
        shared[f"B{i}_wg"] = tile_w(inp["ffn_w_gate"][i])
        shared[f"B{i}_wu"] = tile_w(inp["ffn_w_up"][i])
        shared[f"B{i}_wd"] = tile_w(inp["ffn_w_down"][i])
    shared.update(rwkv_consts_np())
    maps = []
    for core in range(8):
        b, hf = core // 2, core % 2
        m = dict(shared)
        m["h0T"] = np.ascontiguousarray(h0[b].T)
        m["h0own"] = np.ascontiguousarray(h0[b, hf * NTOK:(hf + 1) * NTOK].T)
        selv = np.zeros((128, 4), np.float32)
        selv[:, 0] = 1.0 if hf == 0 else 0.0
        selv[:, 1] = 0.0 if hf == 0 else 1.0
        selv[:, 2] = 0.0 if hf == 0 else 1.0
        m["sel"] = selv
        for i in range(4):
            kind, j = i % 3, i // 3
            gpre = inp["norm_mix_pre"][i]
            if kind == 0:
                mm = rwkv_inputs_np(None, hf, j, inp, j > 0, gpre, None)
                for k_ in ("hT", "vfirst", "c_ident", "c_mask4", "c_lsm", "c_bdones"):
                    mm.pop(k_, None)
            elif kind == 1:
                mm = mla_inputs_np(None, hf, inp, gpre, mla_c)
                mm.pop("hT", None)
            else:
                mm = pool_inputs_np(None, hf, gpre)
                mm.pop("hT", None)
            for k_, v_ in mm.items():
                m[f"L{i}_{k_}"] = v_
        maps.append(m)
    res = run_bass_kernel_spmd(nc, maps, core_ids=list(range(8)))
    out = np.empty((B, L, D), np.float32)
    for core in range(8):
        b, hf = core // 2, core % 2
        out[b, hf * NTOK:(hf + 1) * NTOK] = res.results[core]["hO"].T
    return np.ascontiguousarray(out[:, N_META:])
```
